# Optimizing a Trainium2 kernel written in Bass

```python
import math
import jax
import jax.numpy as jnp
from jax import lax
import numpy as np

D_MODEL = 1024
BATCH = 8
SEQ = 2048
DEPTH = 4

CHUNK = 64
NORM_EPS = 1e-6

HG_HEADS = 4
HG_DK = 128
HG_DV = 128
HG_KEY = HG_HEADS * HG_DK
HG_VAL = HG_HEADS * HG_DV

GLA_HEADS = 4
GLA_DK = 64
GLA_DV = 128
GLA_KEY = GLA_HEADS * GLA_DK
GLA_VAL = GLA_HEADS * GLA_DV
GLA_RANK = 16
GLA_TAU = 16.0

ML_HEADS = 4
ML_DK = 64
ML_DV = 128
ML_KEY = ML_HEADS * ML_DK
ML_VAL = ML_HEADS * ML_DV
CONV_K = 4

D_MIX = HG_VAL + GLA_VAL + ML_VAL

SEGMENTS = (
    ("hg_q", HG_KEY), ("hg_f", HG_KEY), ("hg_i", HG_VAL), ("hg_z", HG_VAL),
    ("gla_q", GLA_KEY), ("gla_k", GLA_KEY), ("gla_v", GLA_VAL), ("gla_a", GLA_RANK), ("gla_z", GLA_VAL),
    ("ml_q", ML_KEY), ("ml_k", ML_KEY), ("ml_v", ML_VAL), ("ml_i", ML_HEADS), ("ml_f", ML_HEADS),
    ("ml_o", ML_VAL), ("ml_z", ML_VAL),
)
N_IN = (3 * HG_KEY + HG_VAL) - HG_KEY + HG_VAL + (2 * GLA_KEY + 2 * GLA_VAL + GLA_RANK) + (2 * ML_KEY + 3 * ML_VAL + 2 * ML_HEADS)

kernel_name = "hymba_style_hgrn2_gla_mlstm_trunk"


def _segment_offsets():
    offs = {}
    start = 0
    for name, width in SEGMENTS:
        offs[name] = (start, start + width)
        start += width
    return offs


def _split_columns(proj):
    return {name: proj[..., s:e] for name, (s, e) in _segment_offsets().items()}


def _rmsnorm(x, w):
    xf = x.astype(jnp.float32)
    y = xf * lax.rsqrt(jnp.mean(xf * xf, axis=-1, keepdims=True) + NORM_EPS)
    return (y * w.astype(jnp.float32)).astype(x.dtype)


def _heads(a, n_heads):
    b, t, _ = a.shape
    return a.reshape(b, t, n_heads, -1).transpose(0, 2, 1, 3).astype(jnp.float32)


def _to_chunks(a, n_chunks):
    a = a.reshape(a.shape[:2] + (n_chunks, CHUNK) + a.shape[3:])
    return jnp.moveaxis(a, 2, 0)


def _from_chunks(a):
    a = jnp.moveaxis(a, 0, 2)
    return a.reshape(a.shape[:2] + (a.shape[2] * a.shape[3],) + a.shape[4:])


def _causal_conv(a, w, bias):
    k = w.shape[0]
    y = lax.conv_general_dilated(a, w[:, None, :].astype(a.dtype), window_strides=(1,), padding=[(k - 1, 0)],
                                 dimension_numbers=("NWC", "WIO", "NWC"), feature_group_count=a.shape[-1])
    return y + bias.astype(a.dtype)


def _chunk_gated_linear_attention(q, k, v, log_g):
    bsz, nh, seq, dk = q.shape
    dv = v.shape[-1]
    nc = seq // CHUNK
    causal = jnp.tril(jnp.ones((CHUNK, CHUNK), dtype=bool))[None, None, :, :, None]

    def step(state, inp):
        qc, kc, vc, gc = inp
        b = jnp.cumsum(gc, axis=2)
        dec = jnp.exp(jnp.where(causal, b[:, :, :, None, :] - b[:, :, None, :, :], -jnp.inf))
        att = jnp.einsum("bhtd,bhsd,bhtsd->bhts", qc, kc, dec)
        out = jnp.einsum("bhts,bhsv->bhtv", att, vc) + jnp.einsum("bhtd,bhdv->bhtv", qc * jnp.exp(b), state)
        b_last = b[:, :, -1:, :]
        state = jnp.exp(b_last[:, :, 0, :, None]) * state + jnp.einsum("bhsd,bhsv->bhdv", kc * jnp.exp(b_last - b), vc)
        return state, out

    s0 = jnp.zeros((bsz, nh, dk, dv), jnp.float32)
    _, out = lax.scan(step, s0, (_to_chunks(q, nc), _to_chunks(k, nc), _to_chunks(v, nc), _to_chunks(log_g, nc)))
    return _from_chunks(out)


def _chunk_mlstm(q, k, v, i_pre, log_f):
    bsz, nh, seq, dk = q.shape
    dv = v.shape[-1]
    nc = seq // CHUNK
    causal = jnp.tril(jnp.ones((CHUNK, CHUNK), dtype=bool))[None, None]

    def step(carry, inp):
        c_st, n_st, m_st = carry
        qc, kc, vc, ic, fc = inp
        b = jnp.cumsum(fc, axis=-1)
        d = jnp.where(causal, b[..., :, None] - b[..., None, :] + ic[..., None, :], -jnp.inf)
        m_t = jnp.maximum(b + m_st[..., None], jnp.max(d, axis=-1))
        w = jnp.exp(d - m_t[..., None])
        inter = jnp.exp(b + m_st[..., None] - m_t)
        s = jnp.einsum("bhtd,bhsd->bhts", qc, kc) * w
        num = inter[..., None] * jnp.einsum("bhtd,bhdv->bhtv", qc, c_st) + jnp.einsum("bhts,bhsv->bhtv", s, vc)
        den = inter * jnp.einsum("bhtd,bhd->bht", qc, n_st) + jnp.sum(s, axis=-1)
        h = num / jnp.maximum(jnp.abs(den), jnp.exp(-m_t))[..., None]
        m_new = m_t[..., -1]
        wk = jnp.exp(b[..., -1:] - b + ic - m_new[..., None])
        scale = jnp.exp(b[..., -1] + m_st - m_new)
        c_st = scale[..., None, None] * c_st + jnp.einsum("bhs,bhsd,bhsv->bhdv", wk, kc, vc)
        n_st = scale[..., None] * n_st + jnp.einsum("bhs,bhsd->bhd", wk, kc)
        return (c_st, n_st, m_new), h

    carry0 = (jnp.zeros((bsz, nh, dk, dv), jnp.float32), jnp.zeros((bsz, nh, dk), jnp.float32),
              jnp.zeros((bsz, nh), jnp.float32))
    _, h = lax.scan(step, carry0, (_to_chunks(q, nc), _to_chunks(k, nc), _to_chunks(v, nc),
                                   _to_chunks(i_pre, nc), _to_chunks(log_f, nc)))
    return _from_chunks(h)


def _gated_head_norm(o, z, w):
    bsz, nh, seq, dv = o.shape
    of = o.transpose(0, 2, 1, 3)
    of = of * lax.rsqrt(jnp.mean(of * of, axis=-1, keepdims=True) + NORM_EPS)
    of = of.reshape(bsz, seq, nh * dv) * w.astype(jnp.float32)
    return (of * jax.nn.silu(z.astype(jnp.float32))).astype(z.dtype)


def _hgrn2_branch(parts, lb):
    q = jax.nn.silu(_heads(parts["hg_q"], HG_HEADS))
    lb = lb.reshape(HG_HEADS, 1, HG_DK)
    f = lb + (1.0 - lb) * jax.nn.sigmoid(_heads(parts["hg_f"], HG_HEADS))
    i = _heads(parts["hg_i"], HG_HEADS)
    return _chunk_gated_linear_attention(q, 1.0 - f, i, jnp.log(f))


def _gla_branch(parts, w_a2, b_a2):
    q = _heads(parts["gla_q"], GLA_HEADS) * (GLA_DK ** -0.5)
    k = _heads(parts["gla_k"], GLA_HEADS)
    v = _heads(parts["gla_v"], GLA_HEADS)
    a = parts["gla_a"] @ w_a2 + b_a2
    log_alpha = jax.nn.log_sigmoid(_heads(a, GLA_HEADS)) / GLA_TAU
    return _chunk_gated_linear_attention(q, k, v, log_alpha)


def _mlstm_branch(parts, conv_w, conv_b):
    qk = jax.nn.silu(_causal_conv(jnp.concatenate([parts["ml_q"], parts["ml_k"]], axis=-1), conv_w, conv_b))
    q = _heads(qk[..., :ML_KEY], ML_HEADS) * (ML_DK ** -0.5)
    k = _heads(qk[..., ML_KEY:], ML_HEADS)
    v = _heads(parts["ml_v"], ML_HEADS)
    i_pre = parts["ml_i"].astype(jnp.float32).transpose(0, 2, 1)
    log_f = jax.nn.log_sigmoid(parts["ml_f"].astype(jnp.float32).transpose(0, 2, 1))
    h = _chunk_mlstm(q, k, v, i_pre, log_f)
    return h * jax.nn.sigmoid(_heads(parts["ml_o"], ML_HEADS))


def setup_inputs(seed: int = 0) -> dict:
    key = jax.random.key(seed)
    ks = jax.random.split(key, 16)
    nrm = jax.random.normal
    x = nrm(ks[0], (BATCH, SEQ, D_MODEL), jnp.float32)
    norm_w = 1.0 + 0.02 * nrm(ks[1], (DEPTH, D_MODEL), jnp.float32)
    w_in = nrm(ks[2], (DEPTH, D_MODEL, N_IN), jnp.float32) * (D_MODEL ** -0.5)
    b_in = 0.02 * nrm(ks[3], (DEPTH, N_IN), jnp.float32)
    f_start, f_end = _segment_offsets()["ml_f"]
    b_in = b_in.at[:, f_start:f_end].add(jnp.linspace(3.0, 6.0, ML_HEADS, dtype=jnp.float32))
    hg_lb_logits = 0.1 * nrm(ks[4], (DEPTH, HG_KEY), jnp.float32)
    hg_norm_w = 1.0 + 0.02 * nrm(ks[5], (DEPTH, HG_VAL), jnp.float32)
    gla_w_a2 = nrm(ks[6], (DEPTH, GLA_RANK, GLA_KEY), jnp.float32) * (GLA_RANK ** -0.5)
    gla_b_a2 = 0.02 * nrm(ks[7], (DEPTH, GLA_KEY), jnp.float32)
    gla_norm_w = 1.0 + 0.02 * nrm(ks[8], (DEPTH, GLA_VAL), jnp.float32)
    ml_conv_w = nrm(ks[9], (DEPTH, CONV_K, 2 * ML_KEY), jnp.float32) * (CONV_K ** -0.5)
    ml_conv_b = 0.02 * nrm(ks[10], (DEPTH, 2 * ML_KEY), jnp.float32)
    ml_norm_w = 1.0 + 0.02 * nrm(ks[11], (DEPTH, ML_VAL), jnp.float32)
    w_out = nrm(ks[12], (DEPTH, D_MIX, D_MODEL), jnp.float32) * (D_MIX ** -0.5) * 0.5
    final_norm_w = 1.0 + 0.02 * nrm(ks[13], (D_MODEL,), jnp.float32)
    return {"x": x, "norm_w": norm_w, "w_in": w_in, "b_in": b_in, "hg_lb_logits": hg_lb_logits,
            "hg_norm_w": hg_norm_w, "gla_w_a2": gla_w_a2, "gla_b_a2": gla_b_a2, "gla_norm_w": gla_norm_w,
            "ml_conv_w": ml_conv_w, "ml_conv_b": ml_conv_b, "ml_norm_w": ml_norm_w, "w_out": w_out,
            "final_norm_w": final_norm_w}


def reference(x, norm_w, w_in, b_in, hg_lb_logits, hg_norm_w, gla_w_a2, gla_b_a2, gla_norm_w,
              ml_conv_w, ml_conv_b, ml_norm_w, w_out, final_norm_w):
    p = jax.nn.softmax(hg_lb_logits.astype(jnp.float32), axis=0)
    lower_bounds = jnp.cumsum(p, axis=0) - p[0:1]
    for l in range(DEPTH):
        h = _rmsnorm(x, norm_w[l])
        parts = _split_columns(h @ w_in[l] + b_in[l])
        y_hg = _gated_head_norm(_hgrn2_branch(parts, lower_bounds[l]), parts["hg_z"], hg_norm_w[l])
        y_gla = _gated_head_norm(_gla_branch(parts, gla_w_a2[l], gla_b_a2[l]), parts["gla_z"], gla_norm_w[l])
        y_ml = _gated_head_norm(_mlstm_branch(parts, ml_conv_w[l], ml_conv_b[l]), parts["ml_z"], ml_norm_w[l])
        y = jnp.concatenate([y_hg, y_gla, y_ml], axis=-1).astype(x.dtype)
        x = x + y @ w_out[l]
    return _rmsnorm(x, final_norm_w)
```

```python
import math
from contextlib import ExitStack

import numpy as np
import concourse.bass as bass
import concourse.mybir as mybir
from concourse.bass_utils import run_bass_kernel_spmd

F32 = mybir.dt.float32
BF16 = mybir.dt.bfloat16
AF = mybir.ActivationFunctionType
ALU = mybir.AluOpType

D_MODEL = 1024
KC = 8
SEQ = 2048
DEPTH = 4
BATCH = 8
EPS = 1e-6
ST = 512
TT = 128

OFF = dict(hg_q=0, hg_f=512, hg_i=1024, hg_z=1536, gla_q=2048, gla_k=2304, gla_v=2560, gla_a=3072,
           gla_z=3088, ml_q=3600, ml_k=3856, ml_v=4112, ml_i=4624, ml_f=4628, ml_o=4632, ml_z=5144)

EPOCH = 30000
SPARSE = True
NDMASEM = 16


class Buf:
    def __init__(self, t, name):
        self.t = t
        self.name = name
        self.writer = None
        self.readers = {}


class Sched:
    def __init__(self, nc, stack, needed=None):
        self.nc = nc
        self.stack = stack
        self.dry = needed is None
        self.needed = needed if needed is not None else set()
        self.used = set()
        self.E = {"pe": nc.tensor, "act": nc.scalar, "dve": nc.vector, "pool": nc.gpsimd, "sp": nc.sync}
        self.cnt = {e: 0 for e in self.E}
        self.sig = {e: 0 for e in self.E}
        self.rank = {e: {} for e in self.E}
        self.sems = {e: [] for e in self.E}
        self.known = {e: {} for e in self.E}
        self.dpool = {}
        self.final = []

    def sbuf(self, name, shape, dt):
        return Buf(self.stack.enter_context(self.nc.sbuf_tensor("sb_" + name, shape, dt)), name)

    def psum(self, name, shape, dt):
        return Buf(self.stack.enter_context(self.nc.psum_tensor("ps_" + name, shape, dt)), name)

    def view(self, parent, name):
        return Buf(parent.t, name)

    def _newsem(self, name):
        return self.stack.enter_context(self.nc.semaphore(name))

    def _esem(self, e, r):
        ep = (r - 1) // EPOCH
        while len(self.sems[e]) <= ep:
            self.sems[e].append(self._newsem(f"s_{e}_{len(self.sems[e])}"))
        return self.sems[e][ep], r - ep * EPOCH

    def _deps(self, e, reads, writes):
        deps = {}

        def add(tok):
            key = tok[1]
            val = tok[2] if tok[0] == "e" else tok[3]
            if key not in deps or deps[key][0] < val:
                deps[key] = (val, tok)

        for r in reads:
            if r.writer is not None:
                add(r.writer)
        for w in writes:
            if w.writer is not None and (w.writer[1] != e or e != "pe"):
                add(w.writer)
            for tok in w.readers.values():
                if tok[1] != e or e != "pe":
                    add(tok)
        for key, (val, tok) in deps.items():
            if self.known[e].get(key, 0) >= val:
                continue
            if tok[0] == "e":
                f, k = tok[1], tok[2]
                self.used.add((f, k))
                sem, sval = self._esem(f, self.rank[f][k])
                self.E[e].wait_ge(sem, sval)
            else:
                self.E[e].wait_ge(tok[2], tok[3])
            self.known[e][key] = val

    def _commit(self, tok, reads, writes):
        for r in reads:
            r.readers[tok[1]] = tok
        for w in writes:
            w.writer = tok
            w.readers = {}

    def op(self, e, fn, reads=(), writes=()):
        self._deps(e, reads, writes)
        ins = fn(self.E[e])
        self.cnt[e] += 1
        k = self.cnt[e]
        if self.dry or (e, k) in self.needed:
            self.sig[e] += 1
            sem, _ = self._esem(e, self.sig[e])
            ins.then_inc(sem, 1)
        self.rank[e][k] = self.sig[e]
        self._commit(("e", e, k), reads, writes)
        return ins

    def dma(self, e, out, in_, reads=(), writes=(), final=False):
        P = self.dpool.setdefault(e, {"sem": [], "use": [], "next": 0})
        i = P["next"]
        P["next"] = (P["next"] + 1) % NDMASEM
        if i >= len(P["sem"]):
            P["sem"].append(self._newsem(f"s_dma_{e}_{i}"))
            P["use"].append(0)
        key = ("dma" + e, i)
        if P["use"][i] > 0 and self.known[e].get(key, 0) < 16 * P["use"][i]:
            self.E[e].wait_ge(P["sem"][i], 16 * P["use"][i])
            self.known[e][key] = 16 * P["use"][i]
        self._deps(e, reads, writes)
        ins = self.E[e].dma_start(out=out, in_=in_)
        P["use"][i] += 1
        tok = ("d", key, P["sem"][i], 16 * P["use"][i])
        ins.then_inc(P["sem"][i], 16)
        self._commit(tok, reads, writes)
        if final:
            self.final.append(tok)
        return ins

    def finish(self):
        for tok in self.final:
            self.E["sp"].wait_ge(tok[2], tok[3])


def col_layout(L):
    idx = {}

    def add(*name):
        idx[name] = len(idx)

    for l in range(L):
        for c in range(KC):
            add("nw", l, c)
        for pr in range(2):
            for nm in ("q0", "q1", "f0", "f1", "z0", "z1", "nw0", "nw1", "v0", "v1"):
                add("hg", l, pr, nm)
            for nm in ("q", "k", "z0", "z1", "a16", "ba2", "nw0", "nw1", "v0", "v1"):
                add("gla", l, pr, nm)
            for nm in ("q", "k", "i", "f", "o0", "o1", "z0", "z1", "nw0", "nw1", "v0", "v1", "cbq", "cbk",
                       "cwq0", "cwq1", "cwq2", "cwq3", "cwk0", "cwk1", "cwk2", "cwk3"):
                add("ml", l, pr, nm)
    return idx


GROUPS = [("hg", 0), ("hg", 1), ("gla", 0), ("gla", 1), ("ml", 0), ("ml", 1)]
NCOLS = {"hg": 1024, "gla": 784, "ml": 1280}
VOFF = {"hg": 768, "gla": 528, "ml": 1024}


def host_prep(inp, L):
    w_in = np.asarray(inp["w_in"], np.float32)
    b_in = np.asarray(inp["b_in"], np.float32)
    w_out = np.asarray(inp["w_out"], np.float32)
    ar = np.arange
    out = {}

    def colsel(kind, pr):
        h0, h1 = 2 * pr, 2 * pr + 1
        if kind == "hg":
            parts = [OFF["hg_q"] + h0 * 128 + ar(128), OFF["hg_q"] + h1 * 128 + ar(128),
                     OFF["hg_f"] + h0 * 128 + ar(128), OFF["hg_f"] + h1 * 128 + ar(128),
                     OFF["hg_z"] + h0 * 128 + ar(128), OFF["hg_z"] + h1 * 128 + ar(128),
                     OFF["hg_i"] + h0 * 128 + ar(256)]
        elif kind == "gla":
            parts = [OFF["gla_q"] + h0 * 64 + ar(128), OFF["gla_k"] + h0 * 64 + ar(128),
                     OFF["gla_z"] + h0 * 128 + ar(128), OFF["gla_z"] + h1 * 128 + ar(128),
                     OFF["gla_a"] + ar(16), OFF["gla_v"] + h0 * 128 + ar(256)]
        else:
            rep = lambda c0: np.concatenate([np.full(64, c0 + h0), np.full(64, c0 + h1)])
            parts = [OFF["ml_q"] + h0 * 64 + ar(128), OFF["ml_k"] + h0 * 64 + ar(128),
                     rep(OFF["ml_i"]), rep(OFF["ml_f"]),
                     OFF["ml_o"] + h0 * 128 + ar(128), OFF["ml_o"] + h1 * 128 + ar(128),
                     OFF["ml_z"] + h0 * 128 + ar(128), OFF["ml_z"] + h1 * 128 + ar(128),
                     OFF["ml_v"] + h0 * 128 + ar(256)]
        return np.concatenate(parts)

    for kind in ("hg", "gla", "ml"):
        arr = np.empty((L, 2, 128, KC, NCOLS[kind]), np.float32)
        for l in range(L):
            for pr in range(2):
                sel = colsel(kind, pr)
                arr[l, pr] = w_in[l][:, sel].reshape(KC, 128, -1).transpose(1, 0, 2)
        out["w_" + kind] = arr
    wo = np.empty((L, 6, 128, 2, D_MODEL), np.float32)
    for l in range(L):
        for g in range(6):
            wo[l, g] = w_out[l][g * 256:(g + 1) * 256].reshape(2, 128, D_MODEL).transpose(1, 0, 2)
    out["w_o"] = wo

    idx = col_layout(L)
    cols = np.zeros((128, len(idx)), np.float32)
    for l in range(L):
        for c in range(KC):
            cols[:, idx["nw", l, c]] = inp["norm_w"][l][c * 128:(c + 1) * 128]
        for pr in range(2):
            h0, h1 = 2 * pr, 2 * pr + 1
            b = b_in[l]
            sl = lambda seg, h, w=128: b[OFF[seg] + h * w: OFF[seg] + (h + 1) * w]
            cols[:, idx["hg", l, pr, "q0"]] = sl("hg_q", h0)
            cols[:, idx["hg", l, pr, "q1"]] = sl("hg_q", h1)
            cols[:, idx["hg", l, pr, "f0"]] = sl("hg_f", h0)
            cols[:, idx["hg", l, pr, "f1"]] = sl("hg_f", h1)
            cols[:, idx["hg", l, pr, "z0"]] = sl("hg_z", h0)
            cols[:, idx["hg", l, pr, "z1"]] = sl("hg_z", h1)
            cols[:, idx["hg", l, pr, "nw0"]] = inp["hg_norm_w"][l][h0 * 128:(h0 + 1) * 128]
            cols[:, idx["hg", l, pr, "nw1"]] = inp["hg_norm_w"][l][h1 * 128:(h1 + 1) * 128]
            cols[:, idx["gla", l, pr, "q"]] = b[OFF["gla_q"] + h0 * 64: OFF["gla_q"] + h0 * 64 + 128]
            cols[:, idx["gla", l, pr, "k"]] = b[OFF["gla_k"] + h0 * 64: OFF["gla_k"] + h0 * 64 + 128]
            cols[:, idx["gla", l, pr, "z0"]] = sl("gla_z", h0)
            cols[:, idx["gla", l, pr, "z1"]] = sl("gla_z", h1)
            cols[0:16, idx["gla", l, pr, "a16"]] = b[OFF["gla_a"]:OFF["gla_a"] + 16]
            cols[:, idx["gla", l, pr, "ba2"]] = inp["gla_b_a2"][l][h0 * 64: h0 * 64 + 128]
            cols[:, idx["gla", l, pr, "nw0"]] = inp["gla_norm_w"][l][h0 * 128:(h0 + 1) * 128]
            cols[:, idx["gla", l, pr, "nw1"]] = inp["gla_norm_w"][l][h1 * 128:(h1 + 1) * 128]
            cols[:, idx["ml", l, pr, "q"]] = b[OFF["ml_q"] + h0 * 64: OFF["ml_q"] + h0 * 64 + 128]
            cols[:, idx["ml", l, pr, "k"]] = b[OFF["ml_k"] + h0 * 64: OFF["ml_k"] + h0 * 64 + 128]
            cols[0:64, idx["ml", l, pr, "i"]] = b[OFF["ml_i"] + h0]
            cols[64:128, idx["ml", l, pr, "i"]] = b[OFF["ml_i"] + h1]
            cols[0:64, idx["ml", l, pr, "f"]] = b[OFF["ml_f"] + h0]
            cols[64:128, idx["ml", l, pr, "f"]] = b[OFF["ml_f"] + h1]
            cols[:, idx["ml", l, pr, "o0"]] = sl("ml_o", h0)
            cols[:, idx["ml", l, pr, "o1"]] = sl("ml_o", h1)
            cols[:, idx["ml", l, pr, "z0"]] = sl("ml_z", h0)
            cols[:, idx["ml", l, pr, "z1"]] = sl("ml_z", h1)
            cols[:, idx["ml", l, pr, "nw0"]] = inp["ml_norm_w"][l][h0 * 128:(h0 + 1) * 128]
            cols[:, idx["ml", l, pr, "nw1"]] = inp["ml_norm_w"][l][h1 * 128:(h1 + 1) * 128]
            cw = inp["ml_conv_w"][l]
            cb = inp["ml_conv_b"][l]
            cols[:, idx["ml", l, pr, "cbq"]] = cb[h0 * 64: h0 * 64 + 128]
            cols[:, idx["ml", l, pr, "cbk"]] = cb[256 + h0 * 64: 256 + h0 * 64 + 128]
            for j in range(4):
                cols[:, idx["ml", l, pr, f"cwq{j}"]] = cw[j, h0 * 64: h0 * 64 + 128]
                cols[:, idx["ml", l, pr, f"cwk{j}"]] = cw[j, 256 + h0 * 64: 256 + h0 * 64 + 128]
            for kind, seg in (("hg", "hg_i"), ("gla", "gla_v"), ("ml", "ml_v")):
                cols[:, idx[kind, l, pr, "v0"]] = sl(seg, h0)
                cols[:, idx[kind, l, pr, "v1"]] = sl(seg, h1)
    out["cols"] = cols
    out["wa2"] = np.ascontiguousarray(np.asarray(inp["gla_w_a2"], np.float32)[:L].transpose(1, 0, 2))
    out["lbl"] = np.ascontiguousarray(
        np.asarray(inp["hg_lb_logits"], np.float32)[:L].reshape(L, 4, 128).transpose(2, 1, 0))
    out["fnw"] = np.ascontiguousarray(np.broadcast_to(np.asarray(inp["final_norm_w"], np.float32)[None, :], (128, D_MODEL)))
    return out


def build(T=SEQ, L=DEPTH, enable=("hg", "gla", "ml")):
    _, used = _build(T, L, enable, None)
    nc, _ = _build(T, L, enable, used if SPARSE else None)
    return nc


def _build(T, L, enable, needed):
    nc = bass.Bass("TRN2", target_bir_lowering=False)
    NT = T // TT
    NS = T // ST
    idx = col_layout(L)
    NCL = len(idx)

    def dram(name, shape, kind="ExternalInput"):
        return nc.dram_tensor(name, list(shape), F32, kind=kind).ap()

    x_d = dram("x", [T, D_MODEL])
    y_d = dram("y", [T, D_MODEL], kind="ExternalOutput")
    w_d = {k: dram("w_" + k, [L, 2, 128, KC, NCOLS[k]]) for k in ("hg", "gla", "ml")}
    wo_d = dram("w_o", [L, 6, 128, 2, D_MODEL])
    cols_d = dram("cols", [128, NCL])
    wa2_d = dram("wa2", [16, L, 256])
    lbl_d = dram("lbl", [128, 4, L])
    fnw_d = dram("fnw", [128, D_MODEL])

    with ExitStack() as stack:
        S = Sched(nc, stack, needed)
        op = S.op

        xT = S.sbuf("xT", [128, KC, T], F32)
        xTg = [[S.view(xT, f"xT{c}_{s}") for s in range(NS)] for c in range(KC)]
        hT = S.sbuf("hT", [128, KC, T], BF16)
        hTg = [S.view(hT, f"hT{s}") for s in range(NS)]
        full = set(enable) == {"hg", "gla", "ml"}
        wbuf = [S.sbuf(f"wbuf{i}", [128, KC, 1280 if (i == 0 or not full) else 1024], BF16) for i in range(2)]
        wobuf = [S.sbuf(f"wobuf{i}", [128, 2, D_MODEL], BF16) for i in range(2)]
        cols = S.sbuf("cols", [128, NCL], F32)
        ncols = S.sbuf("ncols", [128, NCL], F32)
        hcols = S.sbuf("hcols", [128, NCL], F32)
        wa2 = S.sbuf("wa2", [16, L, 256], BF16)
        lbl = S.sbuf("lbl", [128, 4, L], F32)
        lbA = S.sbuf("lbA", [128, 4, L], F32)
        lbB = S.sbuf("lbB", [128, 4, L], F32)
        lbNB = S.sbuf("lbNB", [128, 4, L], F32)
        identf = S.sbuf("identf", [128, 128], F32)
        identb = S.sbuf("identb", [128, 128], BF16)
        onesb = S.sbuf("onesb", [128, 128], BF16)
        mask = S.sbuf("mask", [128, 128], BF16)
        rst = S.sbuf("rst", [128, ST], BF16)
        cst = S.sbuf("cst", [128, 8], F32)
        C_LN8, C_E1024, C_E128, C_ONE, C_ZERO, C_LN32, C_LNS128 = 0, 1, 2, 3, 4, 5, 6

        pb = [S.psum(f"pb{i}", [128, 512], F32) for i in range(8)]
        att_slots = [pb[2]] * 4
        u_slots = [pb[3]] * 2
        ktp = pb[3]

        def att_ap(i):
            return pb[2].t[:, i * 128:(i + 1) * 128]

        def ktp_ap():
            return pb[3].t[:, 256:512].bitcast(BF16)

        S.dma("sp", cols.t[:], cols_d, writes=[cols])
        S.dma("sp", lbl.t[:], lbl_d, writes=[lbl])
        S.dma("pool", wa2.t[:], wa2_d, writes=[wa2])
        op("dve", lambda E: E.tensor_scalar(ncols.t[:], cols.t[:], -1.0, None, op0=ALU.mult), [cols], [ncols])
        op("dve", lambda E: E.tensor_scalar(hcols.t[:], cols.t[:], 0.5, None, op0=ALU.mult), [cols], [hcols])
        op("pool", lambda E: E.memset(identf.t[:], 0.0), [], [identf])
        op("pool", lambda E: E.affine_select(out=identf.t[:], in_=identf.t[:], pattern=[[-1, 128]],
                                             compare_op=ALU.not_equal, fill=1.0, base=0, channel_multiplier=1),
           [identf], [identf])
        op("pool", lambda E: E.tensor_copy(identb.t[:], identf.t[:]), [identf], [identb])
        op("pool", lambda E: E.memset(onesb.t[:], 1.0), [], [onesb])
        op("pool", lambda E: E.memset(mask.t[:], 1.0), [], [mask])
        op("pool", lambda E: E.affine_select(out=mask.t[:], in_=mask.t[:], pattern=[[1, 128]],
                                             compare_op=ALU.is_ge, fill=0.0, base=0, channel_multiplier=-1),
           [mask], [mask])
        op("pool", lambda E: E.memset(rst.t[:], 1.0), [], [rst])
        for j in range(ST // TT):
            op("pool", lambda E, j=j: E.memset(rst.t[:, j * TT:j * TT + 1], 0.0), [], [rst])
        for ci, val in ((C_LN8, math.log(0.125)), (C_E1024, 1024.0 * EPS), (C_E128, 128.0 * EPS),
                        (C_ONE, 1.0), (C_ZERO, 0.0), (C_LN32, math.log(32.0)), (C_LNS128, 0.5 * math.log(128.0))):
            op("pool", lambda E, ci=ci, val=val: E.memset(cst.t[:, ci:ci + 1], val), [], [cst])

        def cc(i):
            return cst.t[:, i:i + 1]

        for b in (2, 3):
            op("dve", lambda E, b=b: E.memset(pb[b].t[:], 0.0), [], [pb[b]])

        tmpl = S.sbuf("tmpl", [128, 4, L], F32)
        suml = S.sbuf("suml", [128, 4], F32)
        lb = S.sbuf("lb", [128, 4, L], F32)
        op("act", lambda E: E.activation(out=tmpl.t[:], in_=lbl.t[:], func=AF.Exp), [lbl], [tmpl])
        op("dve", lambda E: E.tensor_reduce(out=suml.t[:], in_=tmpl.t[:], axis=mybir.AxisListType.X, op=ALU.add), [tmpl], [suml])
        op("dve", lambda E: E.reciprocal(suml.t[:], suml.t[:]), [suml], [suml])
        op("dve", lambda E: E.tensor_tensor(out=tmpl.t[:], in0=tmpl.t[:], in1=suml.t[:].unsqueeze(2).to_broadcast([128, 4, L]),
                                            op=ALU.mult), [tmpl, suml], [tmpl])
        op("dve", lambda E: E.memset(lb.t[:], 0.0), [], [lb])
        for l in range(1, L):
            op("dve", lambda E, l=l: E.tensor_tensor(out=lb.t[:, :, l], in0=lb.t[:, :, l - 1], in1=tmpl.t[:, :, l], op=ALU.add),
               [lb, tmpl], [lb])
        op("dve", lambda E: E.tensor_scalar(lbA.t[:], lb.t[:], 0.5, 0.5, op0=ALU.mult, op1=ALU.add), [lb], [lbA])
        op("dve", lambda E: E.tensor_scalar(lbB.t[:], lb.t[:], -0.5, 0.5, op0=ALU.mult, op1=ALU.add), [lb], [lbB])
        op("dve", lambda E: E.tensor_scalar(lbNB.t[:], lb.t[:], 0.5, -0.5, op0=ALU.mult, op1=ALU.add), [lb], [lbNB])

        R32N = ["qs0", "qs1", "th0", "th1", "gz0_0", "gz1_0", "gz0_1", "gz1_1", "kk0", "kk1", "bcs0", "br0", "e1_0", "lnv0"]
        W32 = S.sbuf("W32", [128, len(R32N), ST], F32)
        r32 = {}
        for i, nm in enumerate(R32N):
            b_ = Buf(W32.t[:, i, :], nm)
            r32[nm] = b_
        R16N = ["sq0", "sq1", "y0", "y1"] + [f"{a}{u}_{p}" for p in range(2) for u in range(2) for a in ("qT", "kT", "ktok")]
        W16 = S.sbuf("W16", [128, len(R16N), ST], BF16)
        r16 = {}
        for i, nm in enumerate(R16N):
            r16[nm] = Buf(W16.t[:, i, :], nm)

        def wide32(i0):
            return W32.t[:, i0:i0 + 2, :].rearrange("p a b -> p (a b)")

        xin_views = [(wide32(0), [r32["qs0"], r32["qs1"]]), (wide32(2), [r32["th0"], r32["th1"]])]
        fnw_ap, fnw_b = wide32(4), [r32["gz0_0"], r32["gz1_0"]]

        class Rot:
            def __init__(self, name, n, shape, dt):
                self.b = [S.sbuf(f"{name}{i}", shape, dt) for i in range(n)]
                self.i = 0

            def get(self):
                r = self.b[self.i]
                self.i = (self.i + 1) % len(self.b)
                return r

        attp = Rot("attsb_", 4, [128, 128], BF16)
        scb = {f"{a}_{u}_{p}": S.sbuf(f"sc_{a}_{u}_{p}", [128, 4], F32) for a in ("c1", "c2", "c3") for u in range(2) for p in range(2)}
        vml = [S.sbuf(f"vsb{i}", [128, 4, 2, 128], BF16) for i in range(2)]
        Sst = [S.sbuf(f"Sst{i}", [128, 256 >> i], F32) for i in range(2)]
        Sbf = [S.sbuf(f"Sbf{i}", [128, 256 >> i], BF16) for i in range(2)]
        utmp = [S.sbuf(f"utmp{i}", [128, 256 >> i], F32) for i in range(2)]
        convin = [S.sbuf(f"convin{i}", [128, ST + 3], F32) for i in range(2)]
        vcnt = [0]
        gen_i = [0]

        step_kind = ["ml"]

        def gen_bank(kind):
            banks = [0, 1] if step_kind[0] == "ml" else [0, 1, 6, 7]
            b = banks[gen_i[0] % len(banks)]
            gen_i[0] += 1
            return pb[b]

        def act(out, in_, func, bias=None, scale=1.0, reads=(), writes=()):
            kw = {}
            if bias is not None:
                kw["bias"] = bias
            return op("act", lambda E: E.activation(out=out, in_=in_, func=func, scale=scale, **kw), reads, writes)

        for tt in range(NT):
            xin_ap, xin_b = xin_views[tt % 2]
            S.dma("sp", xin_ap, x_d[tt * TT:(tt + 1) * TT, :], writes=xin_b)
            st_ = tt // 4
            for half in range(2):
                ps = pb[half]
                for cq in range(4):
                    c = half * 4 + cq
                    op("pe", lambda E, c=c, cq=cq, ps=ps, xin_ap=xin_ap: E.transpose(ps.t[:, cq * 128:(cq + 1) * 128],
                                                                                   xin_ap[:, c * 128:(c + 1) * 128], identf.t[:]),
                       xin_b + [identf], [ps])
                tg = [xTg[half * 4 + cq][st_] for cq in range(4)]
                if half == 0:
                    op("dve", lambda E, ps=ps, half=half, tt=tt: E.tensor_copy(
                        xT.t[:, half * 4:(half + 1) * 4, tt * TT:(tt + 1) * TT],
                        ps.t[:].rearrange("p (a b) -> p a b", a=4)), [ps], tg)
                else:
                    act(xT.t[:, half * 4:(half + 1) * 4, tt * TT:(tt + 1) * TT],
                        ps.t[:].rearrange("p (a b) -> p a b", a=4), AF.Copy, reads=[ps], writes=tg)

        ORDER = [4, 0, 5, 1, 2, 3]
        glist = [(l, gi) for l in range(L) for gi in ORDER if GROUPS[gi][0] in enable]

        def load_weights(n):
            if n >= len(glist):
                return
            l, gi = glist[n]
            kind, pr = GROUPS[gi]
            wb = wbuf[n % 2]
            wo = wobuf[n % 2]
            ncl = NCOLS[kind]
            for c in range(KC):
                S.dma("pool", wb.t[:, c, 0:ncl], w_d[kind][l, pr, :, c, :], writes=[wb])
            S.dma("pool", wo.t[:], wo_d[l, gi], writes=[wo])

        load_weights(0)
        load_weights(1)

        def norm_phase(l):
            for s in range(NS):
                tok = slice(s * ST, (s + 1) * ST)
                ps = gen_bank("ml")
                for c in range(KC):
                    sq = r16[f"sq{c % 2}"]
                    act(sq.t[:], xT.t[:, c, tok], AF.Square, reads=[xTg[c][s]], writes=[sq])
                    op("pe", lambda E, ps=ps, sq=sq, c=c: E.matmul(ps.t[:], lhsT=onesb.t[:], rhs=sq.t[:], start=(c == 0), stop=(c == KC - 1)),
                       [onesb, sq], [ps])
                rstd = r32["lnv0"]
                act(rstd.t[:], ps.t[:], AF.Ln, bias=cc(C_E1024), reads=[ps, cst], writes=[rstd])
                act(rstd.t[:], rstd.t[:], AF.Exp, scale=-0.5, bias=cc(C_LN32), reads=[rstd, cst], writes=[rstd])
                for c in range(KC):
                    ci = idx["nw", l, c]
                    op("dve",
                       lambda E, c=c, ci=ci, rstd=rstd, tok=tok: E.scalar_tensor_tensor(
                           out=hT.t[:, c, tok], in0=xT.t[:, c, tok], scalar=cols.t[:, ci:ci + 1], in1=rstd.t[:],
                           op0=ALU.mult, op1=ALU.mult),
                       [xTg[c][s], cols, rstd], [hTg[s]])

        class Item:
            pass

        items = []
        for n, (l, gi) in enumerate(glist):
            for s in range(NS):
                it = Item()
                it.l, it.gi, it.s, it.n = l, gi, s, n
                it.kind, it.pr = GROUPS[gi]
                it.p = len(items) % 2
                it.ML = it.kind == "ml"
                it.dvp = 256 if it.ML else 128
                it.wb, it.wo = wbuf[n % 2], wobuf[n % 2]
                it.tok = slice(s * ST, (s + 1) * ST)
                it.vt = vml[len(items) % 2]
                it.gz = [r32[f"gz0_{it.p}"], r32[f"gz1_{it.p}"]]
                it.tho = [r32["th0"], r32["th1"]]
                it.heads = [(0, 0, 128), (1, 0, 128)] if it.kind == "hg" else [(0, 0, 64), (0, 64, 64)]
                it.nunit = 2 if it.kind == "hg" else 1
                it.last_of_group = (s == NS - 1)
                it.last_of_layer = it.last_of_group and (n + 1 == len(glist) or glist[n + 1][0] != l)
                it.first_of_layer = (s == 0) and (n == 0 or glist[n - 1][0] != l)
                items.append(it)

        def Ccol(it, nm, tab=None):
            tab = cols if tab is None else tab
            i = idx[it.kind, it.l, it.pr, nm]
            return tab.t[:, i:i + 1]

        def proj_fm(it, j, M=128):
            ps = gen_bank(it.kind)
            for c in range(KC):
                op("pe", lambda E, c=c, ps=ps: E.matmul(ps.t[0:M, :], lhsT=it.wb.t[:, c, j:j + M], rhs=hT.t[:, c, it.tok],
                                                        start=(c == 0), stop=(c == KC - 1)),
                   [it.wb, hTg[it.s]], [ps])
            return ps

        def genA1(it):
            kind, l, pr, s = it.kind, it.l, it.pr, it.s
            C = lambda nm, tab=None: Ccol(it, nm, tab)
            if kind == "hg":
                qs = [r32["qs0"], r32["qs1"]]
                th = [r32["th0"], r32["th1"]]
                kkb = [r32["kk0"], r32["kk1"]]
                for hh in range(2):
                    ps = proj_fm(it, hh * 128)
                    act(qs[hh].t[:], ps.t[:], AF.Silu, bias=C(f"q{hh}"), reads=[ps, cols], writes=[qs[hh]])
                    yield
                for hh in range(2):
                    ps = proj_fm(it, 256 + hh * 128)
                    act(th[hh].t[:], ps.t[:], AF.Tanh, bias=C(f"f{hh}", hcols), scale=0.5, reads=[ps, hcols], writes=[th[hh]])
                    yield
                prep = []
                for hh in range(2):
                    h = 2 * pr + hh
                    op("pool", lambda E, hh=hh, h=h: E.tensor_scalar(kkb[hh].t[:], th[hh].t[:], lbNB.t[:, h, l:l + 1], lbB.t[:, h, l:l + 1],
                                                                    op0=ALU.mult, op1=ALU.add), [th[hh], lbNB, lbB], [kkb[hh]])
                    op("pool", lambda E, hh=hh, h=h: E.tensor_scalar(th[hh].t[:], th[hh].t[:], lbB.t[:, h, l:l + 1], lbA.t[:, h, l:l + 1],
                                                                    op0=ALU.mult, op1=ALU.add), [th[hh], lbB, lbA], [th[hh]])
                for hh in range(2):
                    act(th[hh].t[:], th[hh].t[:], AF.Ln, reads=[th[hh]], writes=[th[hh]])
                    prep.append((qs[hh], kkb[hh], th[hh], 1.0, None))
            elif kind == "gla":
                ql, kl, sp_ = r32["qs0"], r32["qs1"], r32["th0"]
                ps = proj_fm(it, 0)
                act(ql.t[:], ps.t[:], AF.Identity, bias=C("q"), reads=[ps, cols], writes=[ql])
                yield
                ps = proj_fm(it, 128)
                act(kl.t[:], ps.t[:], AF.Identity, bias=C("k"), reads=[ps, cols], writes=[kl])
                yield
                ps = proj_fm(it, 512, M=16)
                i16 = idx[kind, l, pr, "a16"]
                a16b = r16["sq1"]
                act(a16b.t[0:16, :], ps.t[0:16, :], AF.Identity, bias=cols.t[0:16, i16:i16 + 1], reads=[ps, cols], writes=[a16b])
                ps = gen_bank(kind)
                op("pe", lambda E, ps=ps: E.matmul(ps.t[:], lhsT=wa2.t[0:16, l, pr * 128:(pr + 1) * 128], rhs=a16b.t[0:16, :],
                                                   start=True, stop=True), [wa2, a16b], [ps])
                act(sp_.t[:], ps.t[:], AF.Exp, bias=C("ba2", ncols), scale=-1.0, reads=[ps, ncols], writes=[sp_])
                act(sp_.t[:], sp_.t[:], AF.Ln, bias=cc(C_ONE), reads=[sp_, cst], writes=[sp_])
                yield
                prep = [(ql, kl, sp_, -1.0 / 16.0, None)]
            else:
                cv = [r32["qs0"], r32["qs1"]]
                if s == 0:
                    for i in range(2):
                        op("pool", lambda E, i=i: E.memset(convin[i].t[:, 0:3], 0.0), [], [convin[i]])
                for qi, nm in enumerate(("q", "k")):
                    ps = proj_fm(it, qi * 128)
                    cin = convin[qi]
                    acc = cv[qi]
                    act(cin.t[:, 3:ST + 3], ps.t[:], AF.Identity, bias=C(nm), reads=[ps, cols], writes=[cin])
                    op("pool", lambda E, acc=acc, cin=cin, nm=nm: E.tensor_scalar(
                        acc.t[:], cin.t[:, 0:ST], C(f"cw{nm}0"), C(f"cb{nm}"), op0=ALU.mult, op1=ALU.add), [cin, cols], [acc])
                    for j in range(1, 4):
                        op("dve", lambda E, acc=acc, cin=cin, nm=nm, j=j: E.scalar_tensor_tensor(
                            out=acc.t[:], in0=cin.t[:, j:j + ST], scalar=C(f"cw{nm}{j}"), in1=acc.t[:],
                            op0=ALU.mult, op1=ALU.add), [cin, cols, acc], [acc])
                    op("pool", lambda E, cin=cin: E.tensor_copy(cin.t[:, 0:3], cin.t[:, ST:ST + 3]), [cin], [cin])
                    yield
                for qi in range(2):
                    act(cv[qi].t[:], cv[qi].t[:], AF.Silu, reads=[cv[qi]], writes=[cv[qi]])
                irow, sp_ = r32["kk0"], r32["kk1"]
                ps = proj_fm(it, 256)
                act(irow.t[:], ps.t[:], AF.Identity, bias=C("i"), reads=[ps, cols], writes=[irow])
                yield
                ps = proj_fm(it, 384)
                act(sp_.t[:], ps.t[:], AF.Exp, bias=C("f", ncols), scale=-1.0, reads=[ps, ncols], writes=[sp_])
                act(sp_.t[:], sp_.t[:], AF.Ln, bias=cc(C_ONE), reads=[sp_, cst], writes=[sp_])
                yield
                prep = [(cv[0], cv[1], sp_, -1.0, irow)]

            it.unit = []
            for u, (qv, kv, gsrc, gscale, irow) in enumerate(prep):
                bcs, br, e1 = r32["bcs0"], r32["br0"], r32["e1_0"]
                op("dve", lambda E, bcs=bcs, gsrc=gsrc: E.tensor_tensor_scan(bcs.t[:], rst.t[:], gsrc.t[:], 0.0, op0=ALU.mult, op1=ALU.add),
                   [rst, gsrc], [bcs])
                b3 = bcs.t[:].rearrange("p (a b) -> p a b", a=4)
                br3 = br.t[:].rearrange("p (a b) -> p a b", a=4)
                op("pool", lambda E, br3=br3, b3=b3: E.tensor_tensor(out=br3, in0=b3, in1=b3[:, :, 63:64].to_broadcast([128, 4, 128]),
                                                                   op=ALU.subtract), [bcs], [br])
                c1, c2, c3 = (scb[f"{a}_{u}_{it.p}"] for a in ("c1", "c2", "c3"))
                act(c1.t[:], b3[:, :, 127], AF.Exp, scale=gscale, reads=[bcs], writes=[c1])
                act(c2.t[:], br3[:, :, 127], AF.Exp, scale=gscale, reads=[br], writes=[c2])
                act(c3.t[:], b3[:, :, 63], AF.Exp, scale=gscale, reads=[bcs], writes=[c3])
                if kind == "hg":
                    act(e1.t[:], br.t[:], AF.Exp, scale=gscale, reads=[br], writes=[e1])
                else:
                    act(e1.t[:], br.t[:], AF.Exp, scale=gscale, bias=cc(C_LN8), reads=[br, cst], writes=[e1])
                if irow is None:
                    act(br.t[:], br.t[:], AF.Exp, scale=-gscale, reads=[br], writes=[br])
                else:
                    op("pool", lambda E, br=br, irow=irow: E.tensor_tensor(out=br.t[:], in0=br.t[:], in1=irow.t[:], op=ALU.add),
                       [br, irow], [br])
                    act(br.t[:], br.t[:], AF.Exp, reads=[br], writes=[br])
                e2 = br
                qT, kT, ktok = (r16[f"{a}{u}_{it.p}"] for a in ("qT", "kT", "ktok"))
                op("dve", lambda E, qT=qT, qv=qv, e1=e1: E.tensor_tensor(out=qT.t[:], in0=qv.t[:], in1=e1.t[:], op=ALU.mult), [qv, e1], [qT])
                op("dve", lambda E, kT=kT, kv=kv, e2=e2: E.tensor_tensor(out=kT.t[:], in0=kv.t[:], in1=e2.t[:], op=ALU.mult), [kv, e2], [kT])
                it.unit.append((qT, kT, ktok, c1, c2, c3))
                yield

            vt = it.vt
            for hh in range(2):
                ps = proj_fm(it, VOFF[kind] + hh * 128)
                vst = r16[f"sq{hh}"]
                act(vst.t[:], ps.t[:], AF.Identity, bias=C(f"v{hh}"), reads=[ps, cols], writes=[vst])
                yield
            ps = gen_bank(kind)
            vp = ps.t[:].bitcast(BF16)
            for hh in range(2):
                vst = r16[f"sq{hh}"]
                for j in range(4):
                    op("pe", lambda E, hh=hh, j=j, vst=vst: E.transpose(vp[:, (hh * 4 + j) * 128:(hh * 4 + j + 1) * 128],
                                                                        vst.t[:, j * 128:(j + 1) * 128], identb.t[:]),
                       [vst, identb], [ps])
            act(vt.t[:], vp.rearrange("p (h j d) -> p j h d", h=2, j=4), AF.Copy, reads=[ps], writes=[vt])
            yield
            for u in range(len(it.unit)):
                qT, kT, ktok = it.unit[u][0], it.unit[u][1], it.unit[u][2]
                ps = gen_bank(kind)
                kp = ps.t[:, 0:256].bitcast(BF16)
                for j in range(4):
                    op("pe", lambda E, j=j, kT=kT, kp=kp: E.transpose(kp[:, j * 128:(j + 1) * 128], kT.t[:, j * 128:(j + 1) * 128], identb.t[:]),
                       [kT, identb], [ps])
                act(ktok.t[:], kp, AF.Copy, reads=[ps], writes=[ktok])
                yield

        def genA2(it):
            kind = it.kind
            C = lambda nm, tab=None: Ccol(it, nm, tab)
            zoff = {"hg": 512, "gla": 256, "ml": 768}[kind]
            for hh in range(2):
                ps = proj_fm(it, zoff + hh * 128)
                act(it.gz[hh].t[:], ps.t[:], AF.Silu, bias=C(f"z{hh}"), reads=[ps, cols], writes=[it.gz[hh]])
                yield

        Ob = [pb[4], pb[5]]
        Db = [pb[6], pb[7]]

        def att_bank(it, k):
            return pb[2], pb[2].t[:, (k % 4) * 128:(k % 4 + 1) * 128]

        def genCore(it):
            kind, ML, dvp, vt = it.kind, it.ML, it.dvp, it.vt
            unit, heads = it.unit, it.heads
            if it.s == 0:
                for u in range(it.nunit):
                    op("pool", lambda E, u=u: E.memset(Sst[u].t[:], 0.0), [], [Sst[u]])
            for j in range(4):
                for u in range(len(unit)):
                    c3 = unit[u][5]
                    op("dve", lambda E, u=u, c3=c3, j=j: E.tensor_scalar(Sbf[u].t[:, 0:dvp], Sst[u].t[:, 0:dvp], c3.t[:, j:j + 1], None, op0=ALU.mult),
                       [Sst[u], c3], [Sbf[u]])
                atts = []
                abs_ = [att_bank(it, j * 2 + hi) for hi in range(2)]
                for hi, (u, p0, dk) in enumerate(heads):
                    qT, kT = unit[u][0], unit[u][1]
                    ab, a_ap = abs_[hi]
                    op("pe", lambda E, a_ap=a_ap, qT=qT, kT=kT, p0=p0, dk=dk, j=j: E.matmul(
                        a_ap[:, 64:128], lhsT=kT.t[p0:p0 + dk, j * TT:(j + 1) * TT], rhs=qT.t[p0:p0 + dk, j * TT + 64:(j + 1) * TT],
                        start=True, stop=True), [qT, kT], [ab])
                    op("pe", lambda E, a_ap=a_ap, qT=qT, kT=kT, p0=p0, dk=dk, j=j: E.matmul(
                        a_ap[0:64, 0:64], lhsT=kT.t[p0:p0 + dk, j * TT:j * TT + 64], rhs=qT.t[p0:p0 + dk, j * TT:j * TT + 64],
                        start=True, stop=True), [qT, kT], [ab])
                for hi, (u, p0, dk) in enumerate(heads):
                    ab, a_ap = abs_[hi]
                    asb = attp.get()
                    op("dve", lambda E, asb=asb, a_ap=a_ap: E.tensor_tensor(out=asb.t[:], in0=a_ap, in1=mask.t[:], op=ALU.mult),
                       [ab, mask], [asb])
                    atts.append(asb)
                yield
                for hi, (u, p0, dk) in enumerate(heads):
                    qT, kT, ktok = unit[u][0], unit[u][1], unit[u][2]
                    asb = atts[hi]
                    outs = [(Ob[hi], 0)] + ([(Db[hi], 128)] if ML else [])
                    for (ob, vo) in outs:
                        op("pe", lambda E, ob=ob, vo=vo, asb=asb, hi=hi, j=j: E.matmul(
                            ob.t[:, j * TT:(j + 1) * TT], lhsT=(vt.t[:, j, hi, :] if vo == 0 else onesb.t[:]), rhs=asb.t[:], start=True, stop=False),
                           [vt, asb, onesb], [ob])
                        op("pe", lambda E, ob=ob, vo=vo, u=u, p0=p0, dk=dk, qT=qT, j=j: E.matmul(
                            ob.t[:, j * TT:(j + 1) * TT], lhsT=Sbf[u].t[p0:p0 + dk, vo:vo + 128], rhs=qT.t[p0:p0 + dk, j * TT:(j + 1) * TT],
                            start=False, stop=True), [Sbf[u], qT], [ob])
                    ucol = 0 if ML else u * 128
                    op("pe", lambda E, ktok=ktok, p0=p0, dk=dk, hi=hi, j=j, ucol=ucol: E.matmul(
                        pb[3].t[p0:p0 + dk, ucol:ucol + 128], lhsT=ktok.t[:, j * 128 + p0:j * 128 + p0 + dk], rhs=vt.t[:, j, hi, :],
                        start=True, stop=True), [ktok, vt], [pb[3]])
                    if ML:
                        op("pe", lambda E, ktok=ktok, p0=p0, dk=dk, hi=hi, j=j: E.matmul(
                            pb[3].t[p0:p0 + dk, 128:256], lhsT=ktok.t[:, j * 128 + p0:j * 128 + p0 + dk], rhs=onesb.t[:],
                            start=True, stop=True), [ktok, onesb], [pb[3]])
                for u in range(len(unit)):
                    c1, c2 = unit[u][3], unit[u][4]
                    ucol = 0 if ML else u * 128
                    op("dve", lambda E, u=u, c2=c2, j=j, ucol=ucol: E.tensor_scalar(
                        utmp[u].t[:, 0:dvp], pb[3].t[:, ucol:ucol + dvp], c2.t[:, j:j + 1], None, op0=ALU.mult),
                       [pb[3], c2], [utmp[u]])
                    op("dve", lambda E, u=u, c1=c1, j=j: E.scalar_tensor_tensor(
                        out=Sst[u].t[:, 0:dvp], in0=Sst[u].t[:, 0:dvp], scalar=c1.t[:, j:j + 1], in1=utmp[u].t[:, 0:dvp],
                        op0=ALU.mult, op1=ALU.add), [Sst[u], c1, utmp[u]], [Sst[u]])
                yield

        def genPost(it):
            kind, ML = it.kind, it.ML
            it.ys = []
            dsqs = [r32["qs1"], r32["kk1"]]
            if ML:
                for hi in range(2):
                    dsq = dsqs[hi]
                    act(dsq.t[:], Db[hi].t[:], AF.Square, reads=[Db[hi]], writes=[dsq])
                for hi in range(2):
                    dsq = dsqs[hi]
                    op("dve", lambda E, dsq=dsq: E.tensor_scalar(dsq.t[:], dsq.t[:], 1.0, 512.0 * EPS, op0=ALU.max, op1=ALU.mult), [dsq], [dsq])
                for hi in range(2):
                    pso = proj_fm(it, 512 + hi * 128)
                    act(it.tho[hi].t[:], pso.t[:], AF.Tanh, bias=Ccol(it, f"o{hi}", hcols), scale=0.5, reads=[pso, hcols], writes=[it.tho[hi]])
                    yield
            srcs = []
            for hi in range(2):
                if ML:
                    u2 = r32["qs0"] if hi == 0 else r32["kk0"]
                    op("dve", lambda E, u2=u2, hi=hi: E.scalar_tensor_tensor(
                        out=u2.t[:], in0=it.tho[hi].t[:], scalar=1.0, in1=Ob[hi].t[:], op0=ALU.add, op1=ALU.mult),
                       [it.tho[hi], Ob[hi]], [u2])
                    srcs.append((u2.t[:], u2))
                else:
                    srcs.append((Ob[hi].t[:], Ob[hi]))
            for hi in range(2):
                src, srcb = srcs[hi]
                osq = r16[f"sq{hi}"]
                act(osq.t[:], src, AF.Square, reads=[srcb], writes=[osq])
            yield
            for hi in range(2):
                nwc = Ccol(it, f"nw{hi}")
                src, srcb = srcs[hi]
                osq = r16[f"sq{hi}"]
                ps = gen_bank(kind)
                op("pe", lambda E, ps=ps, osq=osq: E.matmul(ps.t[:], lhsT=onesb.t[:], rhs=osq.t[:], start=True, stop=True), [onesb, osq], [ps])
                lnv = r32["lnv0"]
                if ML:
                    dsq = dsqs[hi]
                    op("dve", lambda E, ps=ps, dsq=dsq: E.tensor_tensor(out=dsq.t[:], in0=ps.t[:], in1=dsq.t[:], op=ALU.add), [ps, dsq], [dsq])
                    act(lnv.t[:], dsq.t[:], AF.Ln, reads=[dsq], writes=[lnv])
                else:
                    act(lnv.t[:], ps.t[:], AF.Ln, bias=cc(C_E128), reads=[ps, cst], writes=[lnv])
                act(lnv.t[:], lnv.t[:], AF.Exp, scale=-0.5, bias=cc(C_LNS128), reads=[lnv, cst], writes=[lnv])
                t2 = it.gz[hi]
                op("dve", lambda E, t2=t2, nwc=nwc, lnv=lnv: E.scalar_tensor_tensor(
                    out=t2.t[:], in0=t2.t[:], scalar=nwc, in1=lnv.t[:], op0=ALU.mult, op1=ALU.mult), [t2, cols, lnv], [t2])
                y_ = r16[f"y{hi}"]
                op("dve", lambda E, y_=y_, src=src, t2=t2: E.tensor_tensor(out=y_.t[:], in0=src, in1=t2.t[:], op=ALU.mult), [srcb, t2], [y_])
                it.ys.append(y_)
                yield

        def genOut(it):
            for c in range(KC):
                ps = gen_bank(it.kind)
                for hi in range(2):
                    op("pe", lambda E, ps=ps, hi=hi, c=c: E.matmul(ps.t[:], lhsT=it.wo.t[:, hi, c * 128:(c + 1) * 128], rhs=it.ys[hi].t[:],
                                                                  start=(hi == 0), stop=(hi == 1)), [it.wo, it.ys[hi]], [ps])
                op("dve", lambda E, ps=ps, c=c: E.tensor_tensor(out=xT.t[:, c, it.tok], in0=xT.t[:, c, it.tok], in1=ps.t[:], op=ALU.add),
                   [ps, xTg[c][it.s]], [xTg[c][it.s]])
                yield

        def run(*gens):
            gens = [g for g in gens if g is not None]
            while gens:
                for g in list(gens):
                    try:
                        next(g)
                    except StopIteration:
                        gens.remove(g)

        DEFER_OUT = True
        pending = None
        for i, it in enumerate(items):
            nxt = items[i + 1] if i + 1 < len(items) else None
            step_kind[0] = "ml" if it.first_of_layer else it.kind
            if it.first_of_layer:
                norm_phase(it.l)
                run(genA1(it))
                run(genA2(it))
            pipe_next = nxt is not None and not nxt.first_of_layer
            step_kind[0] = it.kind
            if DEFER_OUT:
                run(genCore(it), genA1(nxt) if pipe_next else None, genOut(pending) if pending is not None else None)
                if pending is not None and pending.last_of_group:
                    load_weights(pending.n + 2)
                pending = None
                run(genPost(it), genA2(nxt) if pipe_next else None)
                if it.last_of_layer:
                    run(genOut(it))
                    if it.last_of_group:
                        load_weights(it.n + 2)
                else:
                    pending = it
            else:
                run(genCore(it), genA1(nxt) if pipe_next else None)
                run(genPost(it))
                run(genOut(it), genA2(nxt) if pipe_next else None)
                if it.last_of_group:
                    load_weights(it.n + 2)

        S.dma("sp", fnw_ap, fnw_d, writes=fnw_b)
        ssq = S.sbuf("ssq", [128, 1], F32)
        junk, junk2 = r32["kk0"], r32["kk1"]
        for tt in range(NT):
            s = tt // 4
            xo_ap, xo_b = xin_views[tt % 2]
            for half in range(2):
                ps = pb[half]
                for cq in range(4):
                    c = half * 4 + cq
                    op("pe", lambda E, c=c, cq=cq, ps=ps, tt=tt: E.transpose(ps.t[:, cq * 128:(cq + 1) * 128],
                                                                             xT.t[:, c, tt * TT:(tt + 1) * TT], identf.t[:]),
                       [xTg[c][s], identf], [ps])
                if half == 0:
                    op("dve", lambda E, ps=ps, xo_ap=xo_ap: E.tensor_copy(xo_ap[:, 0:512], ps.t[:]), [ps], xo_b)
                else:
                    act(xo_ap[:, 512:1024], ps.t[:], AF.Copy, reads=[ps], writes=xo_b)
            act(junk.t[:], xo_ap[:, 0:512], AF.Square, reads=xo_b, writes=[junk])
            act(junk2.t[:], xo_ap[:, 512:1024], AF.Square, reads=xo_b, writes=[junk2])
            op("dve", lambda E: E.tensor_tensor(out=junk.t[:], in0=junk.t[:], in1=junk2.t[:], op=ALU.add), [junk, junk2], [junk])
            op("dve", lambda E: E.tensor_reduce(out=ssq.t[:], in_=junk.t[:], axis=mybir.AxisListType.X, op=ALU.add), [junk], [ssq])
            act(ssq.t[:], ssq.t[:], AF.Ln, bias=cc(C_E1024), reads=[ssq, cst], writes=[ssq])
            act(ssq.t[:], ssq.t[:], AF.Exp, scale=-0.5, reads=[ssq], writes=[ssq])
            op("dve", lambda E, xo_ap=xo_ap: E.scalar_tensor_tensor(out=xo_ap, in0=xo_ap, scalar=ssq.t[:, 0:1], in1=fnw_ap, op0=ALU.mult, op1=ALU.mult),
               xo_b + [ssq] + fnw_b, xo_b)
            op("dve", lambda E, xo_ap=xo_ap: E.tensor_scalar(xo_ap, xo_ap, 32.0, None, op0=ALU.mult), xo_b, xo_b)
            S.dma("sp", y_d[tt * TT:(tt + 1) * TT, :], xo_ap, reads=xo_b, final=True)
        S.finish()
        used = S.used
    return nc, used


_CACHE = {}


def kernel(**inputs):
    x = np.asarray(inputs["x"], np.float32)
    B, T, _ = x.shape
    L = int(np.asarray(inputs["w_in"]).shape[0])
    shared = host_prep(inputs, L)
    key = (T, L)
    if key not in _CACHE:
        _CACHE[key] = build(T, L)
    nc = _CACHE[key]
    in_maps = []
    for b in range(B):
        m = dict(shared)
        m["x"] = np.ascontiguousarray(x[b])
        in_maps.append(m)
    res = run_bass_kernel_spmd(nc, in_maps, core_ids=list(range(B)))
    return np.stack([np.asarray(r["y"], np.float32) for r in res.results], axis=0)
```

```python
import math
from contextlib import ExitStack

import numpy as np
import concourse.bass as bass
import concourse.mybir as mybir
from concourse.bass_utils import run_bass_kernel_spmd

F32 = mybir.dt.float32
BF16 = mybir.dt.bfloat16
AF = mybir.ActivationFunctionType
ALU = mybir.AluOpType

D_MODEL = 1024
KC = 8
SEQ = 2048
DEPTH = 4
BATCH = 8
EPS = 1e-6
ST = 512
TT = 128

OFF = dict(hg_q=0, hg_f=512, hg_i=1024, hg_z=1536, gla_q=2048, gla_k=2304, gla_v=2560, gla_a=3072,
           gla_z=3088, ml_q=3600, ml_k=3856, ml_v=4112, ml_i=4624, ml_f=4628, ml_o=4632, ml_z=5144)

EPOCH = 30000
SPARSE = True
NDMASEM = 16


class Buf:
    def __init__(self, t, name):
        self.t = t
        self.name = name
        self.writer = None
        self.readers = {}


class Sched:
    def __init__(self, nc, stack, needed=None):
        self.nc = nc
        self.stack = stack
        self.dry = needed is None
        self.needed = needed if needed is not None else set()
        self.used = set()
        self.E = {"pe": nc.tensor, "act": nc.scalar, "dve": nc.vector, "pool": nc.gpsimd, "sp": nc.sync}
        self.cnt = {e: 0 for e in self.E}
        self.sig = {e: 0 for e in self.E}
        self.rank = {e: {} for e in self.E}
        self.sems = {e: [] for e in self.E}
        self.known = {e: {} for e in self.E}
        self.dpool = {}
        self.final = []

    def sbuf(self, name, shape, dt):
        return Buf(self.stack.enter_context(self.nc.sbuf_tensor("sb_" + name, shape, dt)), name)

    def psum(self, name, shape, dt):
        return Buf(self.stack.enter_context(self.nc.psum_tensor("ps_" + name, shape, dt)), name)

    def view(self, parent, name):
        return Buf(parent.t, name)

    def _newsem(self, name):
        return self.stack.enter_context(self.nc.semaphore(name))

    def _esem(self, e, r):
        ep = (r - 1) // EPOCH
        while len(self.sems[e]) <= ep:
            self.sems[e].append(self._newsem(f"s_{e}_{len(self.sems[e])}"))
        return self.sems[e][ep], r - ep * EPOCH

    def _deps(self, e, reads, writes):
        deps = {}

        def add(tok):
            key = tok[1]
            val = tok[2] if tok[0] == "e" else tok[3]
            if key not in deps or deps[key][0] < val:
                deps[key] = (val, tok)

        for r in reads:
            if r.writer is not None:
                add(r.writer)
        for w in writes:
            if w.writer is not None and (w.writer[1] != e or e != "pe"):
                add(w.writer)
            for tok in w.readers.values():
                if tok[1] != e or e != "pe":
                    add(tok)
        for key, (val, tok) in deps.items():
            if self.known[e].get(key, 0) >= val:
                continue
            if tok[0] == "e":
                f, k = tok[1], tok[2]
                self.used.add((f, k))
                sem, sval = self._esem(f, self.rank[f][k])
                self.E[e].wait_ge(sem, sval)
            else:
                self.E[e].wait_ge(tok[2], tok[3])
            self.known[e][key] = val

    def _commit(self, tok, reads, writes):
        for r in reads:
            r.readers[tok[1]] = tok
        for w in writes:
            w.writer = tok
            w.readers = {}

    def op(self, e, fn, reads=(), writes=()):
        self._deps(e, reads, writes)
        ins = fn(self.E[e])
        self.cnt[e] += 1
        k = self.cnt[e]
        if self.dry or (e, k) in self.needed:
            self.sig[e] += 1
            sem, _ = self._esem(e, self.sig[e])
            ins.then_inc(sem, 1)
        self.rank[e][k] = self.sig[e]
        self._commit(("e", e, k), reads, writes)
        return ins

    def dma(self, e, out, in_, reads=(), writes=(), final=False):
        P = self.dpool.setdefault(e, {"sem": [], "use": [], "next": 0})
        i = P["next"]
        P["next"] = (P["next"] + 1) % NDMASEM
        if i >= len(P["sem"]):
            P["sem"].append(self._newsem(f"s_dma_{e}_{i}"))
            P["use"].append(0)
        key = ("dma" + e, i)
        if P["use"][i] > 0 and self.known[e].get(key, 0) < 16 * P["use"][i]:
            self.E[e].wait_ge(P["sem"][i], 16 * P["use"][i])
            self.known[e][key] = 16 * P["use"][i]
        self._deps(e, reads, writes)
        ins = self.E[e].dma_start(out=out, in_=in_)
        P["use"][i] += 1
        tok = ("d", key, P["sem"][i], 16 * P["use"][i])
        ins.then_inc(P["sem"][i], 16)
        self._commit(tok, reads, writes)
        if final:
            self.final.append(tok)
        return ins

    def finish(self):
        for tok in self.final:
            self.E["sp"].wait_ge(tok[2], tok[3])


def col_layout(L):
    idx = {}

    def add(*name):
        idx[name] = len(idx)

    for l in range(L):
        for c in range(KC):
            add("nw", l, c)
        for pr in range(2):
            for nm in ("q0", "q1", "f0", "f1", "z0", "z1", "nw0", "nw1", "v0", "v1"):
                add("hg", l, pr, nm)
            for nm in ("q", "k", "z0", "z1", "a16", "ba2", "nw0", "nw1", "v0", "v1"):
                add("gla", l, pr, nm)
            for nm in ("q", "k", "i", "f", "o0", "o1", "z0", "z1", "nw0", "nw1", "v0", "v1", "cbq", "cbk",
                       "cwq0", "cwq1", "cwq2", "cwq3", "cwk0", "cwk1", "cwk2", "cwk3"):
                add("ml", l, pr, nm)
    return idx


GROUPS = [("hg", 0), ("hg", 1), ("gla", 0), ("gla", 1), ("ml", 0), ("ml", 1)]
NCOLS = {"hg": 1024, "gla": 784, "ml": 1280}
VOFF = {"hg": 768, "gla": 528, "ml": 1024}


def host_prep(inp, L):
    w_in = np.asarray(inp["w_in"], np.float32)
    b_in = np.asarray(inp["b_in"], np.float32)
    w_out = np.asarray(inp["w_out"], np.float32)
    ar = np.arange
    out = {}

    def colsel(kind, pr):
        h0, h1 = 2 * pr, 2 * pr + 1
        if kind == "hg":
            parts = [OFF["hg_q"] + h0 * 128 + ar(128), OFF["hg_q"] + h1 * 128 + ar(128),
                     OFF["hg_f"] + h0 * 128 + ar(128), OFF["hg_f"] + h1 * 128 + ar(128),
                     OFF["hg_z"] + h0 * 128 + ar(128), OFF["hg_z"] + h1 * 128 + ar(128),
                     OFF["hg_i"] + h0 * 128 + ar(256)]
        elif kind == "gla":
            parts = [OFF["gla_q"] + h0 * 64 + ar(128), OFF["gla_k"] + h0 * 64 + ar(128),
                     OFF["gla_z"] + h0 * 128 + ar(128), OFF["gla_z"] + h1 * 128 + ar(128),
                     OFF["gla_a"] + ar(16), OFF["gla_v"] + h0 * 128 + ar(256)]
        else:
            rep = lambda c0: np.concatenate([np.full(64, c0 + h0), np.full(64, c0 + h1)])
            parts = [OFF["ml_q"] + h0 * 64 + ar(128), OFF["ml_k"] + h0 * 64 + ar(128),
                     rep(OFF["ml_i"]), rep(OFF["ml_f"]),
                     OFF["ml_o"] + h0 * 128 + ar(128), OFF["ml_o"] + h1 * 128 + ar(128),
                     OFF["ml_z"] + h0 * 128 + ar(128), OFF["ml_z"] + h1 * 128 + ar(128),
                     OFF["ml_v"] + h0 * 128 + ar(256)]
        return np.concatenate(parts)

    for kind in ("hg", "gla", "ml"):
        arr = np.empty((L, 2, 128, KC, NCOLS[kind]), np.float32)
        for l in range(L):
            for pr in range(2):
                sel = colsel(kind, pr)
                arr[l, pr] = w_in[l][:, sel].reshape(KC, 128, -1).transpose(1, 0, 2)
        out["w_" + kind] = arr
    wo = np.empty((L, 6, 128, 2, D_MODEL), np.float32)
    for l in range(L):
        for g in range(6):
            wo[l, g] = w_out[l][g * 256:(g + 1) * 256].reshape(2, 128, D_MODEL).transpose(1, 0, 2)
    out["w_o"] = wo

    idx = col_layout(L)
    cols = np.zeros((128, len(idx)), np.float32)
    for l in range(L):
        for c in range(KC):
            cols[:, idx["nw", l, c]] = inp["norm_w"][l][c * 128:(c + 1) * 128]
        for pr in range(2):
            h0, h1 = 2 * pr, 2 * pr + 1
            b = b_in[l]
            sl = lambda seg, h, w=128: b[OFF[seg] + h * w: OFF[seg] + (h + 1) * w]
            cols[:, idx["hg", l, pr, "q0"]] = sl("hg_q", h0)
            cols[:, idx["hg", l, pr, "q1"]] = sl("hg_q", h1)
            cols[:, idx["hg", l, pr, "f0"]] = sl("hg_f", h0)
            cols[:, idx["hg", l, pr, "f1"]] = sl("hg_f", h1)
            cols[:, idx["hg", l, pr, "z0"]] = sl("hg_z", h0)
            cols[:, idx["hg", l, pr, "z1"]] = sl("hg_z", h1)
            cols[:, idx["hg", l, pr, "nw0"]] = inp["hg_norm_w"][l][h0 * 128:(h0 + 1) * 128]
            cols[:, idx["hg", l, pr, "nw1"]] = inp["hg_norm_w"][l][h1 * 128:(h1 + 1) * 128]
            cols[:, idx["gla", l, pr, "q"]] = b[OFF["gla_q"] + h0 * 64: OFF["gla_q"] + h0 * 64 + 128]
            cols[:, idx["gla", l, pr, "k"]] = b[OFF["gla_k"] + h0 * 64: OFF["gla_k"] + h0 * 64 + 128]
            cols[:, idx["gla", l, pr, "z0"]] = sl("gla_z", h0)
            cols[:, idx["gla", l, pr, "z1"]] = sl("gla_z", h1)
            cols[0:16, idx["gla", l, pr, "a16"]] = b[OFF["gla_a"]:OFF["gla_a"] + 16]
            cols[:, idx["gla", l, pr, "ba2"]] = inp["gla_b_a2"][l][h0 * 64: h0 * 64 + 128]
            cols[:, idx["gla", l, pr, "nw0"]] = inp["gla_norm_w"][l][h0 * 128:(h0 + 1) * 128]
            cols[:, idx["gla", l, pr, "nw1"]] = inp["gla_norm_w"][l][h1 * 128:(h1 + 1) * 128]
            cols[:, idx["ml", l, pr, "q"]] = b[OFF["ml_q"] + h0 * 64: OFF["ml_q"] + h0 * 64 + 128]
            cols[:, idx["ml", l, pr, "k"]] = b[OFF["ml_k"] + h0 * 64: OFF["ml_k"] + h0 * 64 + 128]
            cols[0:64, idx["ml", l, pr, "i"]] = b[OFF["ml_i"] + h0]
            cols[64:128, idx["ml", l, pr, "i"]] = b[OFF["ml_i"] + h1]
            cols[0:64, idx["ml", l, pr, "f"]] = b[OFF["ml_f"] + h0]
            cols[64:128, idx["ml", l, pr, "f"]] = b[OFF["ml_f"] + h1]
            cols[:, idx["ml", l, pr, "o0"]] = sl("ml_o", h0)
            cols[:, idx["ml", l, pr, "o1"]] = sl("ml_o", h1)
            cols[:, idx["ml", l, pr, "z0"]] = sl("ml_z", h0)
            cols[:, idx["ml", l, pr, "z1"]] = sl("ml_z", h1)
            cols[:, idx["ml", l, pr, "nw0"]] = inp["ml_norm_w"][l][h0 * 128:(h0 + 1) * 128]
            cols[:, idx["ml", l, pr, "nw1"]] = inp["ml_norm_w"][l][h1 * 128:(h1 + 1) * 128]
            cw = inp["ml_conv_w"][l]
            cb = inp["ml_conv_b"][l]
            cols[:, idx["ml", l, pr, "cbq"]] = cb[h0 * 64: h0 * 64 + 128]
            cols[:, idx["ml", l, pr, "cbk"]] = cb[256 + h0 * 64: 256 + h0 * 64 + 128]
            for j in range(4):
                cols[:, idx["ml", l, pr, f"cwq{j}"]] = cw[j, h0 * 64: h0 * 64 + 128]
                cols[:, idx["ml", l, pr, f"cwk{j}"]] = cw[j, 256 + h0 * 64: 256 + h0 * 64 + 128]
            for kind, seg in (("hg", "hg_i"), ("gla", "gla_v"), ("ml", "ml_v")):
                cols[:, idx[kind, l, pr, "v0"]] = sl(seg, h0)
                cols[:, idx[kind, l, pr, "v1"]] = sl(seg, h1)
    out["cols"] = cols
    out["wa2"] = np.ascontiguousarray(np.asarray(inp["gla_w_a2"], np.float32)[:L].transpose(1, 0, 2))
    out["lbl"] = np.ascontiguousarray(
        np.asarray(inp["hg_lb_logits"], np.float32)[:L].reshape(L, 4, 128).transpose(2, 1, 0))
    out["fnw"] = np.ascontiguousarray(np.broadcast_to(np.asarray(inp["final_norm_w"], np.float32)[None, :], (128, D_MODEL)))
    return out


def build(T=SEQ, L=DEPTH, enable=("hg", "gla", "ml")):
    _, used = _build(T, L, enable, None)
    nc, _ = _build(T, L, enable, used if SPARSE else None)
    return nc


def _build(T, L, enable, needed):
    nc = bass.Bass("TRN2", target_bir_lowering=False)
    NT = T // TT
    NS = T // ST
    idx = col_layout(L)
    NCL = len(idx)

    def dram(name, shape, kind="ExternalInput"):
        return nc.dram_tensor(name, list(shape), F32, kind=kind).ap()

    x_d = dram("x", [T, D_MODEL])
    y_d = dram("y", [T, D_MODEL], kind="ExternalOutput")
    w_d = {k: dram("w_" + k, [L, 2, 128, KC, NCOLS[k]]) for k in ("hg", "gla", "ml")}
    wo_d = dram("w_o", [L, 6, 128, 2, D_MODEL])
    cols_d = dram("cols", [128, NCL])
    wa2_d = dram("wa2", [16, L, 256])
    lbl_d = dram("lbl", [128, 4, L])
    fnw_d = dram("fnw", [128, D_MODEL])

    with ExitStack() as stack:
        S = Sched(nc, stack, needed)
        op = S.op

        xT = S.sbuf("xT", [128, KC, T], F32)
        xTg = [[S.view(xT, f"xT{c}_{s}") for s in range(NS)] for c in range(KC)]
        hT = S.sbuf("hT", [128, KC, T], BF16)
        hTg = [S.view(hT, f"hT{s}") for s in range(NS)]
        full = set(enable) == {"hg", "gla", "ml"}
        wbuf = [S.sbuf(f"wbuf{i}", [128, KC, 1280 if (i == 0 or not full) else 1024], BF16) for i in range(2)]
        wobuf = [S.sbuf(f"wobuf{i}", [128, 2, D_MODEL], BF16) for i in range(2)]
        cols = S.sbuf("cols", [128, NCL], F32)
        ncols = S.sbuf("ncols", [128, NCL], F32)
        hcols = S.sbuf("hcols", [128, NCL], F32)
        wa2 = S.sbuf("wa2", [16, L, 256], BF16)
        lbl = S.sbuf("lbl", [128, 4, L], F32)
        lbA = S.sbuf("lbA", [128, 4, L], F32)
        lbB = S.sbuf("lbB", [128, 4, L], F32)
        lbNB = S.sbuf("lbNB", [128, 4, L], F32)
        identf = S.sbuf("identf", [128, 128], F32)
        identb = S.sbuf("identb", [128, 128], BF16)
        onesb = S.sbuf("onesb", [128, 128], BF16)
        mask = S.sbuf("mask", [128, 128], BF16)
        rst = S.sbuf("rst", [128, ST], BF16)
        cst = S.sbuf("cst", [128, 8], F32)
        C_LN8, C_E1024, C_E128, C_ONE, C_ZERO, C_LN32, C_LNS128 = 0, 1, 2, 3, 4, 5, 6

        pb = [S.psum(f"pb{i}", [128, 512], F32) for i in range(8)]
        att_slots = [pb[2]] * 4
        u_slots = [pb[3]] * 2
        ktp = pb[3]

        def att_ap(i):
            return pb[2].t[:, i * 128:(i + 1) * 128]

        def ktp_ap():
            return pb[3].t[:, 256:512].bitcast(BF16)

        S.dma("sp", cols.t[:], cols_d, writes=[cols])
        S.dma("sp", lbl.t[:], lbl_d, writes=[lbl])
        S.dma("pool", wa2.t[:], wa2_d, writes=[wa2])
        op("dve", lambda E: E.tensor_scalar(ncols.t[:], cols.t[:], -1.0, None, op0=ALU.mult), [cols], [ncols])
        op("dve", lambda E: E.tensor_scalar(hcols.t[:], cols.t[:], 0.5, None, op0=ALU.mult), [cols], [hcols])
        op("pool", lambda E: E.memset(identf.t[:], 0.0), [], [identf])
        op("pool", lambda E: E.affine_select(out=identf.t[:], in_=identf.t[:], pattern=[[-1, 128]],
                                             compare_op=ALU.not_equal, fill=1.0, base=0, channel_multiplier=1),
           [identf], [identf])
        op("pool", lambda E: E.tensor_copy(identb.t[:], identf.t[:]), [identf], [identb])
        op("pool", lambda E: E.memset(onesb.t[:], 1.0), [], [onesb])
        op("pool", lambda E: E.memset(mask.t[:], 1.0), [], [mask])
        op("pool", lambda E: E.affine_select(out=mask.t[:], in_=mask.t[:], pattern=[[1, 128]],
                                             compare_op=ALU.is_ge, fill=0.0, base=0, channel_multiplier=-1),
           [mask], [mask])
        op("pool", lambda E: E.memset(rst.t[:], 1.0), [], [rst])
        for j in range(ST // TT):
            op("pool", lambda E, j=j: E.memset(rst.t[:, j * TT:j * TT + 1], 0.0), [], [rst])
        for ci, val in ((C_LN8, math.log(0.125)), (C_E1024, 1024.0 * EPS), (C_E128, 128.0 * EPS),
                        (C_ONE, 1.0), (C_ZERO, 0.0), (C_LN32, math.log(32.0)), (C_LNS128, 0.5 * math.log(128.0))):
            op("pool", lambda E, ci=ci, val=val: E.memset(cst.t[:, ci:ci + 1], val), [], [cst])

        def cc(i):
            return cst.t[:, i:i + 1]

        for b in (2, 3):
            op("dve", lambda E, b=b: E.memset(pb[b].t[:], 0.0), [], [pb[b]])

        tmpl = S.sbuf("tmpl", [128, 4, L], F32)
        suml = S.sbuf("suml", [128, 4], F32)
        lb = S.sbuf("lb", [128, 4, L], F32)
        op("act", lambda E: E.activation(out=tmpl.t[:], in_=lbl.t[:], func=AF.Exp), [lbl], [tmpl])
        op("dve", lambda E: E.tensor_reduce(out=suml.t[:], in_=tmpl.t[:], axis=mybir.AxisListType.X, op=ALU.add), [tmpl], [suml])
        op("dve", lambda E: E.reciprocal(suml.t[:], suml.t[:]), [suml], [suml])
        op("dve", lambda E: E.tensor_tensor(out=tmpl.t[:], in0=tmpl.t[:], in1=suml.t[:].unsqueeze(2).to_broadcast([128, 4, L]),
                                            op=ALU.mult), [tmpl, suml], [tmpl])
        op("dve", lambda E: E.memset(lb.t[:], 0.0), [], [lb])
        for l in range(1, L):
            op("dve", lambda E, l=l: E.tensor_tensor(out=lb.t[:, :, l], in0=lb.t[:, :, l - 1], in1=tmpl.t[:, :, l], op=ALU.add),
               [lb, tmpl], [lb])
        op("dve", lambda E: E.tensor_scalar(lbA.t[:], lb.t[:], 0.5, 0.5, op0=ALU.mult, op1=ALU.add), [lb], [lbA])
        op("dve", lambda E: E.tensor_scalar(lbB.t[:], lb.t[:], -0.5, 0.5, op0=ALU.mult, op1=ALU.add), [lb], [lbB])
        op("dve", lambda E: E.tensor_scalar(lbNB.t[:], lb.t[:], 0.5, -0.5, op0=ALU.mult, op1=ALU.add), [lb], [lbNB])

        R32N = ["qs0", "qs1", "th0", "th1", "gz0_0", "gz1_0", "gz0_1", "gz1_1", "kk0", "kk1", "bcs0", "br0", "e1_0", "lnv0"]
        W32 = S.sbuf("W32", [128, len(R32N), ST], F32)
        r32 = {}
        for i, nm in enumerate(R32N):
            b_ = Buf(W32.t[:, i, :], nm)
            r32[nm] = b_
        R16N = ["sq0", "sq1", "y0", "y1"] + [f"{a}{u}_{p}" for p in range(2) for u in range(2) for a in ("qT", "kT", "ktok")]
        W16 = S.sbuf("W16", [128, len(R16N), ST], BF16)
        r16 = {}
        for i, nm in enumerate(R16N):
            r16[nm] = Buf(W16.t[:, i, :], nm)

        def wide32(i0):
            return W32.t[:, i0:i0 + 2, :].rearrange("p a b -> p (a b)")

        xin_views = [(wide32(0), [r32["qs0"], r32["qs1"]]), (wide32(2), [r32["th0"], r32["th1"]])]
        fnw_ap, fnw_b = wide32(4), [r32["gz0_0"], r32["gz1_0"]]

        class Rot:
            def __init__(self, name, n, shape, dt):
                self.b = [S.sbuf(f"{name}{i}", shape, dt) for i in range(n)]
                self.i = 0

            def get(self):
                r = self.b[self.i]
                self.i = (self.i + 1) % len(self.b)
                return r

        attp = Rot("attsb_", 4, [128, 128], BF16)
        scb = {f"{a}_{u}_{p}": S.sbuf(f"sc_{a}_{u}_{p}", [128, 4], F32) for a in ("c1", "c2", "c3") for u in range(2) for p in range(2)}
        vml = [S.sbuf(f"vsb{i}", [128, 4, 2, 128], BF16) for i in range(2)]
        Sst = [S.sbuf(f"Sst{i}", [128, 256 >> i], F32) for i in range(2)]
        Sbf = [S.sbuf(f"Sbf{i}", [128, 256 >> i], BF16) for i in range(2)]
        utmp = [S.sbuf(f"utmp{i}", [128, 256 >> i], F32) for i in range(2)]
        convin = [S.sbuf(f"convin{i}", [128, ST + 3], F32) for i in range(2)]
        vcnt = [0]
        gen_i = [0]

        step_kind = ["ml"]

        def gen_bank(kind):
            banks = [0, 1, 3] if step_kind[0] == "ml" else [0, 1, 3, 6, 7]
            b = banks[gen_i[0] % len(banks)]
            gen_i[0] += 1
            return pb[b]

        def act(out, in_, func, bias=None, scale=1.0, reads=(), writes=()):
            kw = {}
            if bias is not None:
                kw["bias"] = bias
            return op("act", lambda E: E.activation(out=out, in_=in_, func=func, scale=scale, **kw), reads, writes)

        for tt in range(NT):
            xin_ap, xin_b = xin_views[tt % 2]
            S.dma("sp", xin_ap, x_d[tt * TT:(tt + 1) * TT, :], writes=xin_b)
            st_ = tt // 4
            for half in range(2):
                ps = pb[half]
                for cq in range(4):
                    c = half * 4 + cq
                    op("pe", lambda E, c=c, cq=cq, ps=ps, xin_ap=xin_ap: E.transpose(ps.t[:, cq * 128:(cq + 1) * 128],
                                                                                   xin_ap[:, c * 128:(c + 1) * 128], identf.t[:]),
                       xin_b + [identf], [ps])
                tg = [xTg[half * 4 + cq][st_] for cq in range(4)]
                if half == 0:
                    op("dve", lambda E, ps=ps, half=half, tt=tt: E.tensor_copy(
                        xT.t[:, half * 4:(half + 1) * 4, tt * TT:(tt + 1) * TT],
                        ps.t[:].rearrange("p (a b) -> p a b", a=4)), [ps], tg)
                else:
                    act(xT.t[:, half * 4:(half + 1) * 4, tt * TT:(tt + 1) * TT],
                        ps.t[:].rearrange("p (a b) -> p a b", a=4), AF.Copy, reads=[ps], writes=tg)

        ORDER = [4, 0, 5, 1, 2, 3]
        glist = [(l, gi) for l in range(L) for gi in ORDER if GROUPS[gi][0] in enable]

        def load_weights(n):
            if n >= len(glist):
                return
            l, gi = glist[n]
            kind, pr = GROUPS[gi]
            wb = wbuf[n % 2]
            wo = wobuf[n % 2]
            ncl = NCOLS[kind]
            for c in range(KC):
                S.dma("pool", wb.t[:, c, 0:ncl], w_d[kind][l, pr, :, c, :], writes=[wb])
            S.dma("pool", wo.t[:], wo_d[l, gi], writes=[wo])

        load_weights(0)
        load_weights(1)

        def norm_phase(l):
            for s in range(NS):
                tok = slice(s * ST, (s + 1) * ST)
                ps = gen_bank("ml")
                for c in range(KC):
                    sq = r16[f"sq{c % 2}"]
                    act(sq.t[:], xT.t[:, c, tok], AF.Square, reads=[xTg[c][s]], writes=[sq])
                    op("pe", lambda E, ps=ps, sq=sq, c=c: E.matmul(ps.t[:], lhsT=onesb.t[:], rhs=sq.t[:], start=(c == 0), stop=(c == KC - 1)),
                       [onesb, sq], [ps])
                rstd = r32["lnv0"]
                act(rstd.t[:], ps.t[:], AF.Ln, bias=cc(C_E1024), reads=[ps, cst], writes=[rstd])
                act(rstd.t[:], rstd.t[:], AF.Exp, scale=-0.5, bias=cc(C_LN32), reads=[rstd, cst], writes=[rstd])
                for c in range(KC):
                    ci = idx["nw", l, c]
                    op("dve",
                       lambda E, c=c, ci=ci, rstd=rstd, tok=tok: E.scalar_tensor_tensor(
                           out=hT.t[:, c, tok], in0=xT.t[:, c, tok], scalar=cols.t[:, ci:ci + 1], in1=rstd.t[:],
                           op0=ALU.mult, op1=ALU.mult),
                       [xTg[c][s], cols, rstd], [hTg[s]])

        class Item:
            pass

        items = []
        for n, (l, gi) in enumerate(glist):
            for s in range(NS):
                it = Item()
                it.l, it.gi, it.s, it.n = l, gi, s, n
                it.kind, it.pr = GROUPS[gi]
                it.p = len(items) % 2
                it.ML = it.kind == "ml"
                it.dvp = 256 if it.ML else 128
                it.wb, it.wo = wbuf[n % 2], wobuf[n % 2]
                it.tok = slice(s * ST, (s + 1) * ST)
                it.vt = vml[len(items) % 2]
                it.gz = [r32[f"gz0_{it.p}"], r32[f"gz1_{it.p}"]]
                it.tho = [r32["th0"], r32["th1"]]
                it.heads = [(0, 0, 128), (1, 0, 128)] if it.kind == "hg" else [(0, 0, 64), (0, 64, 64)]
                it.nunit = 2 if it.kind == "hg" else 1
                it.last_of_group = (s == NS - 1)
                it.last_of_layer = it.last_of_group and (n + 1 == len(glist) or glist[n + 1][0] != l)
                it.first_of_layer = (s == 0) and (n == 0 or glist[n - 1][0] != l)
                items.append(it)

        def Ccol(it, nm, tab=None):
            tab = cols if tab is None else tab
            i = idx[it.kind, it.l, it.pr, nm]
            return tab.t[:, i:i + 1]

        def proj_fm(it, j, M=128):
            ps = gen_bank(it.kind)
            for c in range(KC):
                op("pe", lambda E, c=c, ps=ps: E.matmul(ps.t[0:M, :], lhsT=it.wb.t[:, c, j:j + M], rhs=hT.t[:, c, it.tok],
                                                        start=(c == 0), stop=(c == KC - 1)),
                   [it.wb, hTg[it.s]], [ps])
            return ps

        def genA1(it):
            kind, l, pr, s = it.kind, it.l, it.pr, it.s
            C = lambda nm, tab=None: Ccol(it, nm, tab)
            if kind == "hg":
                qs = [r32["qs0"], r32["qs1"]]
                th = [r32["th0"], r32["th1"]]
                kkb = [r32["kk0"], r32["kk1"]]
                for hh in range(2):
                    ps = proj_fm(it, hh * 128)
                    act(qs[hh].t[:], ps.t[:], AF.Silu, bias=C(f"q{hh}"), reads=[ps, cols], writes=[qs[hh]])
                    yield
                for hh in range(2):
                    ps = proj_fm(it, 256 + hh * 128)
                    act(th[hh].t[:], ps.t[:], AF.Tanh, bias=C(f"f{hh}", hcols), scale=0.5, reads=[ps, hcols], writes=[th[hh]])
                    yield
                prep = []
                for hh in range(2):
                    h = 2 * pr + hh
                    op("pool", lambda E, hh=hh, h=h: E.tensor_scalar(kkb[hh].t[:], th[hh].t[:], lbNB.t[:, h, l:l + 1], lbB.t[:, h, l:l + 1],
                                                                    op0=ALU.mult, op1=ALU.add), [th[hh], lbNB, lbB], [kkb[hh]])
                    op("pool", lambda E, hh=hh, h=h: E.tensor_scalar(th[hh].t[:], th[hh].t[:], lbB.t[:, h, l:l + 1], lbA.t[:, h, l:l + 1],
                                                                    op0=ALU.mult, op1=ALU.add), [th[hh], lbB, lbA], [th[hh]])
                for hh in range(2):
                    act(th[hh].t[:], th[hh].t[:], AF.Ln, reads=[th[hh]], writes=[th[hh]])
                    prep.append((qs[hh], kkb[hh], th[hh], 1.0, None))
            elif kind == "gla":
                ql, kl, sp_ = r32["qs0"], r32["qs1"], r32["th0"]
                ps = proj_fm(it, 0)
                act(ql.t[:], ps.t[:], AF.Identity, bias=C("q"), reads=[ps, cols], writes=[ql])
                yield
                ps = proj_fm(it, 128)
                act(kl.t[:], ps.t[:], AF.Identity, bias=C("k"), reads=[ps, cols], writes=[kl])
                yield
                ps = proj_fm(it, 512, M=16)
                i16 = idx[kind, l, pr, "a16"]
                a16b = r16["sq1"]
                act(a16b.t[0:16, :], ps.t[0:16, :], AF.Identity, bias=cols.t[0:16, i16:i16 + 1], reads=[ps, cols], writes=[a16b])
                ps = gen_bank(kind)
                op("pe", lambda E, ps=ps: E.matmul(ps.t[:], lhsT=wa2.t[0:16, l, pr * 128:(pr + 1) * 128], rhs=a16b.t[0:16, :],
                                                   start=True, stop=True), [wa2, a16b], [ps])
                act(sp_.t[:], ps.t[:], AF.Exp, bias=C("ba2", ncols), scale=-1.0, reads=[ps, ncols], writes=[sp_])
                act(sp_.t[:], sp_.t[:], AF.Ln, bias=cc(C_ONE), reads=[sp_, cst], writes=[sp_])
                yield
                prep = [(ql, kl, sp_, -1.0 / 16.0, None)]
            else:
                cv = [r32["qs0"], r32["qs1"]]
                if s == 0:
                    for i in range(2):
                        op("pool", lambda E, i=i: E.memset(convin[i].t[:, 0:3], 0.0), [], [convin[i]])
                for qi, nm in enumerate(("q", "k")):
                    ps = proj_fm(it, qi * 128)
                    cin = convin[qi]
                    acc = cv[qi]
                    act(cin.t[:, 3:ST + 3], ps.t[:], AF.Identity, bias=C(nm), reads=[ps, cols], writes=[cin])
                    op("pool", lambda E, acc=acc, cin=cin, nm=nm: E.tensor_scalar(
                        acc.t[:], cin.t[:, 0:ST], C(f"cw{nm}0"), C(f"cb{nm}"), op0=ALU.mult, op1=ALU.add), [cin, cols], [acc])
                    for j in range(1, 4):
                        op("dve", lambda E, acc=acc, cin=cin, nm=nm, j=j: E.scalar_tensor_tensor(
                            out=acc.t[:], in0=cin.t[:, j:j + ST], scalar=C(f"cw{nm}{j}"), in1=acc.t[:],
                            op0=ALU.mult, op1=ALU.add), [cin, cols, acc], [acc])
                    op("pool", lambda E, cin=cin: E.tensor_copy(cin.t[:, 0:3], cin.t[:, ST:ST + 3]), [cin], [cin])
                    yield
                for qi in range(2):
                    act(cv[qi].t[:], cv[qi].t[:], AF.Silu, reads=[cv[qi]], writes=[cv[qi]])
                irow, sp_ = r32["kk0"], r32["kk1"]
                ps = proj_fm(it, 256)
                act(irow.t[:], ps.t[:], AF.Identity, bias=C("i"), reads=[ps, cols], writes=[irow])
                yield
                ps = proj_fm(it, 384)
                act(sp_.t[:], ps.t[:], AF.Exp, bias=C("f", ncols), scale=-1.0, reads=[ps, ncols], writes=[sp_])
                act(sp_.t[:], sp_.t[:], AF.Ln, bias=cc(C_ONE), reads=[sp_, cst], writes=[sp_])
                yield
                prep = [(cv[0], cv[1], sp_, -1.0, irow)]

            it.unit = []
            for u, (qv, kv, gsrc, gscale, irow) in enumerate(prep):
                bcs, br, e1 = r32["bcs0"], r32["br0"], r32["e1_0"]
                op("dve", lambda E, bcs=bcs, gsrc=gsrc: E.tensor_tensor_scan(bcs.t[:], rst.t[:], gsrc.t[:], 0.0, op0=ALU.mult, op1=ALU.add),
                   [rst, gsrc], [bcs])
                b3 = bcs.t[:].rearrange("p (a b) -> p a b", a=4)
                br3 = br.t[:].rearrange("p (a b) -> p a b", a=4)
                op("pool", lambda E, br3=br3, b3=b3: E.tensor_tensor(out=br3, in0=b3, in1=b3[:, :, 63:64].to_broadcast([128, 4, 128]),
                                                                   op=ALU.subtract), [bcs], [br])
                c1, c2, c3 = (scb[f"{a}_{u}_{it.p}"] for a in ("c1", "c2", "c3"))
                act(c1.t[:], b3[:, :, 127], AF.Exp, scale=gscale, reads=[bcs], writes=[c1])
                act(c2.t[:], br3[:, :, 127], AF.Exp, scale=gscale, reads=[br], writes=[c2])
                act(c3.t[:], b3[:, :, 63], AF.Exp, scale=gscale, reads=[bcs], writes=[c3])
                if kind == "hg":
                    act(e1.t[:], br.t[:], AF.Exp, scale=gscale, reads=[br], writes=[e1])
                else:
                    act(e1.t[:], br.t[:], AF.Exp, scale=gscale, bias=cc(C_LN8), reads=[br, cst], writes=[e1])
                if irow is None:
                    act(br.t[:], br.t[:], AF.Exp, scale=-gscale, reads=[br], writes=[br])
                else:
                    op("pool", lambda E, br=br, irow=irow: E.tensor_tensor(out=br.t[:], in0=br.t[:], in1=irow.t[:], op=ALU.add),
                       [br, irow], [br])
                    act(br.t[:], br.t[:], AF.Exp, reads=[br], writes=[br])
                e2 = br
                qT, kT, ktok = (r16[f"{a}{u}_{it.p}"] for a in ("qT", "kT", "ktok"))
                op("dve", lambda E, qT=qT, qv=qv, e1=e1: E.tensor_tensor(out=qT.t[:], in0=qv.t[:], in1=e1.t[:], op=ALU.mult), [qv, e1], [qT])
                op("dve", lambda E, kT=kT, kv=kv, e2=e2: E.tensor_tensor(out=kT.t[:], in0=kv.t[:], in1=e2.t[:], op=ALU.mult), [kv, e2], [kT])
                it.unit.append((qT, kT, ktok, c1, c2, c3))
                yield

            vt = it.vt
            for hh in range(2):
                ps = proj_fm(it, VOFF[kind] + hh * 128)
                vst = r16[f"sq{hh}"]
                act(vst.t[:], ps.t[:], AF.Identity, bias=C(f"v{hh}"), reads=[ps, cols], writes=[vst])
                yield
            ps = gen_bank(kind)
            vp = ps.t[:].bitcast(BF16)
            for hh in range(2):
                vst = r16[f"sq{hh}"]
                for j in range(4):
                    op("pe", lambda E, hh=hh, j=j, vst=vst: E.transpose(vp[:, (hh * 4 + j) * 128:(hh * 4 + j + 1) * 128],
                                                                        vst.t[:, j * 128:(j + 1) * 128], identb.t[:]),
                       [vst, identb], [ps])
            act(vt.t[:], vp.rearrange("p (h j d) -> p j h d", h=2, j=4), AF.Copy, reads=[ps], writes=[vt])
            yield
            for u in range(len(it.unit)):
                qT, kT, ktok = it.unit[u][0], it.unit[u][1], it.unit[u][2]
                ps = gen_bank(kind)
                kp = ps.t[:, 0:256].bitcast(BF16)
                for j in range(4):
                    op("pe", lambda E, j=j, kT=kT, kp=kp: E.transpose(kp[:, j * 128:(j + 1) * 128], kT.t[:, j * 128:(j + 1) * 128], identb.t[:]),
                       [kT, identb], [ps])
                act(ktok.t[:], kp, AF.Copy, reads=[ps], writes=[ktok])
                yield

        def genA2(it):
            kind = it.kind
            C = lambda nm, tab=None: Ccol(it, nm, tab)
            zoff = {"hg": 512, "gla": 256, "ml": 768}[kind]
            for hh in range(2):
                ps = proj_fm(it, zoff + hh * 128)
                act(it.gz[hh].t[:], ps.t[:], AF.Silu, bias=C(f"z{hh}"), reads=[ps, cols], writes=[it.gz[hh]])
                yield

        Ob = [pb[4], pb[5]]
        Db = [pb[6], pb[7]]

        def att_bank(it, k):
            return pb[2], pb[2].t[:, (k % 2) * 128:(k % 2 + 1) * 128]

        def genCore(it):
            kind, ML, dvp, vt = it.kind, it.ML, it.dvp, it.vt
            unit, heads = it.unit, it.heads
            if it.s == 0:
                for u in range(it.nunit):
                    op("pool", lambda E, u=u: E.memset(Sst[u].t[:], 0.0), [], [Sst[u]])
            for j in range(4):
                for u in range(len(unit)):
                    c3 = unit[u][5]
                    op("dve", lambda E, u=u, c3=c3, j=j: E.tensor_scalar(Sbf[u].t[:, 0:dvp], Sst[u].t[:, 0:dvp], c3.t[:, j:j + 1], None, op0=ALU.mult),
                       [Sst[u], c3], [Sbf[u]])
                atts = []
                abs_ = [att_bank(it, j * 2 + hi) for hi in range(2)]
                for hi, (u, p0, dk) in enumerate(heads):
                    qT, kT = unit[u][0], unit[u][1]
                    ab, a_ap = abs_[hi]
                    op("pe", lambda E, a_ap=a_ap, qT=qT, kT=kT, p0=p0, dk=dk, j=j: E.matmul(
                        a_ap[:, 64:128], lhsT=kT.t[p0:p0 + dk, j * TT:(j + 1) * TT], rhs=qT.t[p0:p0 + dk, j * TT + 64:(j + 1) * TT],
                        start=True, stop=True), [qT, kT], [ab])
                    op("pe", lambda E, a_ap=a_ap, qT=qT, kT=kT, p0=p0, dk=dk, j=j: E.matmul(
                        a_ap[0:64, 0:64], lhsT=kT.t[p0:p0 + dk, j * TT:j * TT + 64], rhs=qT.t[p0:p0 + dk, j * TT:j * TT + 64],
                        start=True, stop=True), [qT, kT], [ab])
                for hi, (u, p0, dk) in enumerate(heads):
                    ab, a_ap = abs_[hi]
                    asb = attp.get()
                    op("dve", lambda E, asb=asb, a_ap=a_ap: E.tensor_tensor(out=asb.t[:], in0=a_ap, in1=mask.t[:], op=ALU.mult),
                       [ab, mask], [asb])
                    atts.append(asb)
                yield
                for hi, (u, p0, dk) in enumerate(heads):
                    qT, kT, ktok = unit[u][0], unit[u][1], unit[u][2]
                    asb = atts[hi]
                    outs = [(Ob[hi], 0)] + ([(Db[hi], 128)] if ML else [])
                    for (ob, vo) in outs:
                        op("pe", lambda E, ob=ob, vo=vo, asb=asb, hi=hi, j=j: E.matmul(
                            ob.t[:, j * TT:(j + 1) * TT], lhsT=(vt.t[:, j, hi, :] if vo == 0 else onesb.t[:]), rhs=asb.t[:], start=True, stop=False),
                           [vt, asb, onesb], [ob])
                        op("pe", lambda E, ob=ob, vo=vo, u=u, p0=p0, dk=dk, qT=qT, j=j: E.matmul(
                            ob.t[:, j * TT:(j + 1) * TT], lhsT=Sbf[u].t[p0:p0 + dk, vo:vo + 128], rhs=qT.t[p0:p0 + dk, j * TT:(j + 1) * TT],
                            start=False, stop=True), [Sbf[u], qT], [ob])
                    ucol = 0 if ML else u * 128
                    op("pe", lambda E, ktok=ktok, p0=p0, dk=dk, hi=hi, j=j, ucol=ucol: E.matmul(
                        pb[2].t[p0:p0 + dk, 256 + ucol:256 + ucol + 128], lhsT=ktok.t[:, j * 128 + p0:j * 128 + p0 + dk], rhs=vt.t[:, j, hi, :],
                        start=True, stop=True), [ktok, vt], [pb[2]])
                    if ML:
                        op("pe", lambda E, ktok=ktok, p0=p0, dk=dk, hi=hi, j=j: E.matmul(
                            pb[2].t[p0:p0 + dk, 384:512], lhsT=ktok.t[:, j * 128 + p0:j * 128 + p0 + dk], rhs=onesb.t[:],
                            start=True, stop=True), [ktok, onesb], [pb[2]])
                for u in range(len(unit)):
                    c1, c2 = unit[u][3], unit[u][4]
                    ucol = 0 if ML else u * 128
                    op("dve", lambda E, u=u, c2=c2, j=j, ucol=ucol: E.tensor_scalar(
                        utmp[u].t[:, 0:dvp], pb[2].t[:, 256 + ucol:256 + ucol + dvp], c2.t[:, j:j + 1], None, op0=ALU.mult),
                       [pb[2], c2], [utmp[u]])
                    op("dve", lambda E, u=u, c1=c1, j=j: E.scalar_tensor_tensor(
                        out=Sst[u].t[:, 0:dvp], in0=Sst[u].t[:, 0:dvp], scalar=c1.t[:, j:j + 1], in1=utmp[u].t[:, 0:dvp],
                        op0=ALU.mult, op1=ALU.add), [Sst[u], c1, utmp[u]], [Sst[u]])
                yield

        def genPost(it):
            kind, ML = it.kind, it.ML
            it.ys = []
            dsqs = [r32["qs1"], r32["kk1"]]
            if ML:
                for hi in range(2):
                    dsq = dsqs[hi]
                    act(dsq.t[:], Db[hi].t[:], AF.Square, reads=[Db[hi]], writes=[dsq])
                for hi in range(2):
                    dsq = dsqs[hi]
                    op("dve", lambda E, dsq=dsq: E.tensor_scalar(dsq.t[:], dsq.t[:], 1.0, 512.0 * EPS, op0=ALU.max, op1=ALU.mult), [dsq], [dsq])
                for hi in range(2):
                    pso = proj_fm(it, 512 + hi * 128)
                    act(it.tho[hi].t[:], pso.t[:], AF.Tanh, bias=Ccol(it, f"o{hi}", hcols), scale=0.5, reads=[pso, hcols], writes=[it.tho[hi]])
                    yield
            for hi in range(2):
                nwc = Ccol(it, f"nw{hi}")
                if ML:
                    u2 = r32["qs0"]
                    op("dve", lambda E, u2=u2, hi=hi: E.scalar_tensor_tensor(
                        out=u2.t[:], in0=it.tho[hi].t[:], scalar=1.0, in1=Ob[hi].t[:], op0=ALU.add, op1=ALU.mult),
                       [it.tho[hi], Ob[hi]], [u2])
                    src, srcb = u2.t[:], u2
                else:
                    src, srcb = Ob[hi].t[:], Ob[hi]
                osq = r16[f"sq{hi}"]
                act(osq.t[:], src, AF.Square, reads=[srcb], writes=[osq])
                ps = gen_bank(kind)
                op("pe", lambda E, ps=ps, osq=osq: E.matmul(ps.t[:], lhsT=onesb.t[:], rhs=osq.t[:], start=True, stop=True), [onesb, osq], [ps])
                lnv = r32["lnv0"]
                if ML:
                    dsq = dsqs[hi]
                    op("dve", lambda E, ps=ps, dsq=dsq: E.tensor_tensor(out=dsq.t[:], in0=ps.t[:], in1=dsq.t[:], op=ALU.add), [ps, dsq], [dsq])
                    act(lnv.t[:], dsq.t[:], AF.Ln, reads=[dsq], writes=[lnv])
                else:
                    act(lnv.t[:], ps.t[:], AF.Ln, bias=cc(C_E128), reads=[ps, cst], writes=[lnv])
                act(lnv.t[:], lnv.t[:], AF.Exp, scale=-0.5, bias=cc(C_LNS128), reads=[lnv, cst], writes=[lnv])
                t2 = it.gz[hi]
                op("dve", lambda E, t2=t2, nwc=nwc, lnv=lnv: E.scalar_tensor_tensor(
                    out=t2.t[:], in0=t2.t[:], scalar=nwc, in1=lnv.t[:], op0=ALU.mult, op1=ALU.mult), [t2, cols, lnv], [t2])
                y_ = r16[f"y{hi}"]
                op("dve", lambda E, y_=y_, src=src, t2=t2: E.tensor_tensor(out=y_.t[:], in0=src, in1=t2.t[:], op=ALU.mult), [srcb, t2], [y_])
                it.ys.append(y_)
                yield

        def genOut(it):
            for c in range(KC):
                ps = gen_bank(it.kind)
                for hi in range(2):
                    op("pe", lambda E, ps=ps, hi=hi, c=c: E.matmul(ps.t[:], lhsT=it.wo.t[:, hi, c * 128:(c + 1) * 128], rhs=it.ys[hi].t[:],
                                                                  start=(hi == 0), stop=(hi == 1)), [it.wo, it.ys[hi]], [ps])
                op("dve", lambda E, ps=ps, c=c: E.tensor_tensor(out=xT.t[:, c, it.tok], in0=xT.t[:, c, it.tok], in1=ps.t[:], op=ALU.add),
                   [ps, xTg[c][it.s]], [xTg[c][it.s]])
                yield

        def run(*gens):
            gens = [g for g in gens if g is not None]
            while gens:
                for g in list(gens):
                    try:
                        next(g)
                    except StopIteration:
                        gens.remove(g)

        DEFER_OUT = True
        pending = None
        for i, it in enumerate(items):
            nxt = items[i + 1] if i + 1 < len(items) else None
            step_kind[0] = "ml" if it.first_of_layer else it.kind
            if it.first_of_layer:
                norm_phase(it.l)
                run(genA1(it))
                run(genA2(it))
            pipe_next = nxt is not None and not nxt.first_of_layer
            step_kind[0] = it.kind
            if DEFER_OUT:
                run(genCore(it), genA1(nxt) if pipe_next else None, genOut(pending) if pending is not None else None)
                if pending is not None and pending.last_of_group:
                    load_weights(pending.n + 2)
                pending = None
                run(genPost(it), genA2(nxt) if pipe_next else None)
                if it.last_of_layer:
                    run(genOut(it))
                    if it.last_of_group:
                        load_weights(it.n + 2)
                else:
                    pending = it
            else:
                run(genCore(it), genA1(nxt) if pipe_next else None)
                run(genPost(it))
                run(genOut(it), genA2(nxt) if pipe_next else None)
                if it.last_of_group:
                    load_weights(it.n + 2)

        S.dma("sp", fnw_ap, fnw_d, writes=fnw_b)
        ssq = S.sbuf("ssq", [128, 1], F32)
        junk, junk2 = r32["kk0"], r32["kk1"]
        for tt in range(NT):
            s = tt // 4
            xo_ap, xo_b = xin_views[tt % 2]
            for half in range(2):
                ps = pb[half]
                for cq in range(4):
                    c = half * 4 + cq
                    op("pe", lambda E, c=c, cq=cq, ps=ps, tt=tt: E.transpose(ps.t[:, cq * 128:(cq + 1) * 128],
                                                                             xT.t[:, c, tt * TT:(tt + 1) * TT], identf.t[:]),
                       [xTg[c][s], identf], [ps])
                if half == 0:
                    op("dve", lambda E, ps=ps, xo_ap=xo_ap: E.tensor_copy(xo_ap[:, 0:512], ps.t[:]), [ps], xo_b)
                else:
                    act(xo_ap[:, 512:1024], ps.t[:], AF.Copy, reads=[ps], writes=xo_b)
            act(junk.t[:], xo_ap[:, 0:512], AF.Square, reads=xo_b, writes=[junk])
            act(junk2.t[:], xo_ap[:, 512:1024], AF.Square, reads=xo_b, writes=[junk2])
            op("dve", lambda E: E.tensor_tensor(out=junk.t[:], in0=junk.t[:], in1=junk2.t[:], op=ALU.add), [junk, junk2], [junk])
            op("dve", lambda E: E.tensor_reduce(out=ssq.t[:], in_=junk.t[:], axis=mybir.AxisListType.X, op=ALU.add), [junk], [ssq])
            act(ssq.t[:], ssq.t[:], AF.Ln, bias=cc(C_E1024), reads=[ssq, cst], writes=[ssq])
            act(ssq.t[:], ssq.t[:], AF.Exp, scale=-0.5, reads=[ssq], writes=[ssq])
            op("dve", lambda E, xo_ap=xo_ap: E.scalar_tensor_tensor(out=xo_ap, in0=xo_ap, scalar=ssq.t[:, 0:1], in1=fnw_ap, op0=ALU.mult, op1=ALU.mult),
               xo_b + [ssq] + fnw_b, xo_b)
            op("dve", lambda E, xo_ap=xo_ap: E.tensor_scalar(xo_ap, xo_ap, 32.0, None, op0=ALU.mult), xo_b, xo_b)
            S.dma("sp", y_d[tt * TT:(tt + 1) * TT, :], xo_ap, reads=xo_b, final=True)
        S.finish()
        used = S.used
    return nc, used


_CACHE = {}


def kernel(**inputs):
    x = np.asarray(inputs["x"], np.float32)
    B, T, _ = x.shape
    L = int(np.asarray(inputs["w_in"]).shape[0])
    shared = host_prep(inputs, L)
    key = (T, L)
    if key not in _CACHE:
        _CACHE[key] = build(T, L)
    nc = _CACHE[key]
    in_maps = []
    for b in range(B):
        m = dict(shared)
        m["x"] = np.ascontiguousarray(x[b])
        in_maps.append(m)
    res = run_bass_kernel_spmd(nc, in_maps, core_ids=list(range(B)))
    return np.stack([np.asarray(r["y"], np.float32) for r in res.results], axis=0)
```

```python
import math
from contextlib import ExitStack

import numpy as np
import concourse.bass as bass
import concourse.mybir as mybir
from concourse.bass_utils import run_bass_kernel_spmd

F32 = mybir.dt.float32
BF16 = mybir.dt.bfloat16
AF = mybir.ActivationFunctionType
ALU = mybir.AluOpType

D_MODEL = 1024
KC = 8
SEQ = 2048
DEPTH = 4
BATCH = 8
EPS = 1e-6
ST = 512
TT = 128

OFF = dict(hg_q=0, hg_f=512, hg_i=1024, hg_z=1536, gla_q=2048, gla_k=2304, gla_v=2560, gla_a=3072,
           gla_z=3088, ml_q=3600, ml_k=3856, ml_v=4112, ml_i=4624, ml_f=4628, ml_o=4632, ml_z=5144)

EPOCH = 30000
SPARSE = True
NDMASEM = 16


class Buf:
    def __init__(self, t, name):
        self.t = t
        self.name = name
        self.writer = None
        self.readers = {}


class Sched:
    def __init__(self, nc, stack, needed=None):
        self.nc = nc
        self.stack = stack
        self.dry = needed is None
        self.needed = needed if needed is not None else set()
        self.used = set()
        self.E = {"pe": nc.tensor, "act": nc.scalar, "dve": nc.vector, "pool": nc.gpsimd, "sp": nc.sync}
        self.cnt = {e: 0 for e in self.E}
        self.sig = {e: 0 for e in self.E}
        self.rank = {e: {} for e in self.E}
        self.sems = {e: [] for e in self.E}
        self.known = {e: {} for e in self.E}
        self.dpool = {}
        self.final = []

    def sbuf(self, name, shape, dt):
        return Buf(self.stack.enter_context(self.nc.sbuf_tensor("sb_" + name, shape, dt)), name)

    def psum(self, name, shape, dt):
        return Buf(self.stack.enter_context(self.nc.psum_tensor("ps_" + name, shape, dt)), name)

    def view(self, parent, name):
        return Buf(parent.t, name)

    def _newsem(self, name):
        return self.stack.enter_context(self.nc.semaphore(name))

    def _esem(self, e, r):
        ep = (r - 1) // EPOCH
        while len(self.sems[e]) <= ep:
            self.sems[e].append(self._newsem(f"s_{e}_{len(self.sems[e])}"))
        return self.sems[e][ep], r - ep * EPOCH

    def _deps(self, e, reads, writes):
        deps = {}

        def add(tok):
            key = tok[1]
            val = tok[2] if tok[0] == "e" else tok[3]
            if key not in deps or deps[key][0] < val:
                deps[key] = (val, tok)

        for r in reads:
            if r.writer is not None:
                add(r.writer)
        for w in writes:
            if w.writer is not None and (w.writer[1] != e or e != "pe"):
                add(w.writer)
            for tok in w.readers.values():
                if tok[1] != e or e != "pe":
                    add(tok)
        for key, (val, tok) in deps.items():
            if self.known[e].get(key, 0) >= val:
                continue
            if tok[0] == "e":
                f, k = tok[1], tok[2]
                self.used.add((f, k))
                sem, sval = self._esem(f, self.rank[f][k])
                self.E[e].wait_ge(sem, sval)
            else:
                self.E[e].wait_ge(tok[2], tok[3])
            self.known[e][key] = val

    def _commit(self, tok, reads, writes):
        for r in reads:
            r.readers[tok[1]] = tok
        for w in writes:
            w.writer = tok
            w.readers = {}

    def op(self, e, fn, reads=(), writes=()):
        self._deps(e, reads, writes)
        ins = fn(self.E[e])
        self.cnt[e] += 1
        k = self.cnt[e]
        if self.dry or (e, k) in self.needed:
            self.sig[e] += 1
            sem, _ = self._esem(e, self.sig[e])
            ins.then_inc(sem, 1)
        self.rank[e][k] = self.sig[e]
        self._commit(("e", e, k), reads, writes)
        return ins

    def dma(self, e, out, in_, reads=(), writes=(), final=False):
        P = self.dpool.setdefault(e, {"sem": [], "use": [], "next": 0})
        i = P["next"]
        P["next"] = (P["next"] + 1) % NDMASEM
        if i >= len(P["sem"]):
            P["sem"].append(self._newsem(f"s_dma_{e}_{i}"))
            P["use"].append(0)
        key = ("dma" + e, i)
        if P["use"][i] > 0 and self.known[e].get(key, 0) < 16 * P["use"][i]:
            self.E[e].wait_ge(P["sem"][i], 16 * P["use"][i])
            self.known[e][key] = 16 * P["use"][i]
        self._deps(e, reads, writes)
        ins = self.E[e].dma_start(out=out, in_=in_)
        P["use"][i] += 1
        tok = ("d", key, P["sem"][i], 16 * P["use"][i])
        ins.then_inc(P["sem"][i], 16)
        self._commit(tok, reads, writes)
        if final:
            self.final.append(tok)
        return ins

    def finish(self):
        for tok in self.final:
            self.E["sp"].wait_ge(tok[2], tok[3])


def col_layout(L):
    idx = {}

    def add(*name):
        idx[name] = len(idx)

    for l in range(L):
        for c in range(KC):
            add("nw", l, c)
        for pr in range(2):
            for nm in ("q0", "q1", "f0", "f1", "z0", "z1", "nw0", "nw1", "v0", "v1"):
                add("hg", l, pr, nm)
            for nm in ("q", "k", "z0", "z1", "a16", "ba2", "nw0", "nw1", "v0", "v1"):
                add("gla", l, pr, nm)
            for nm in ("q", "k", "i", "f", "o0", "o1", "z0", "z1", "nw0", "nw1", "v0", "v1", "cbq", "cbk",
                       "cwq0", "cwq1", "cwq2", "cwq3", "cwk0", "cwk1", "cwk2", "cwk3"):
                add("ml", l, pr, nm)
    return idx


GROUPS = [("hg", 0), ("hg", 1), ("gla", 0), ("gla", 1), ("ml", 0), ("ml", 1)]
NCOLS = {"hg": 1024, "gla": 784, "ml": 1280}
VOFF = {"hg": 768, "gla": 528, "ml": 1024}


def host_prep(inp, L):
    w_in = np.asarray(inp["w_in"], np.float32)
    b_in = np.asarray(inp["b_in"], np.float32)
    w_out = np.asarray(inp["w_out"], np.float32)
    ar = np.arange
    out = {}

    def colsel(kind, pr):
        h0, h1 = 2 * pr, 2 * pr + 1
        if kind == "hg":
            parts = [OFF["hg_q"] + h0 * 128 + ar(128), OFF["hg_q"] + h1 * 128 + ar(128),
                     OFF["hg_f"] + h0 * 128 + ar(128), OFF["hg_f"] + h1 * 128 + ar(128),
                     OFF["hg_z"] + h0 * 128 + ar(128), OFF["hg_z"] + h1 * 128 + ar(128),
                     OFF["hg_i"] + h0 * 128 + ar(256)]
        elif kind == "gla":
            parts = [OFF["gla_q"] + h0 * 64 + ar(128), OFF["gla_k"] + h0 * 64 + ar(128),
                     OFF["gla_z"] + h0 * 128 + ar(128), OFF["gla_z"] + h1 * 128 + ar(128),
                     OFF["gla_a"] + ar(16), OFF["gla_v"] + h0 * 128 + ar(256)]
        else:
            rep = lambda c0: np.concatenate([np.full(64, c0 + h0), np.full(64, c0 + h1)])
            parts = [OFF["ml_q"] + h0 * 64 + ar(128), OFF["ml_k"] + h0 * 64 + ar(128),
                     rep(OFF["ml_i"]), rep(OFF["ml_f"]),
                     OFF["ml_o"] + h0 * 128 + ar(128), OFF["ml_o"] + h1 * 128 + ar(128),
                     OFF["ml_z"] + h0 * 128 + ar(128), OFF["ml_z"] + h1 * 128 + ar(128),
                     OFF["ml_v"] + h0 * 128 + ar(256)]
        return np.concatenate(parts)

    for kind in ("hg", "gla", "ml"):
        arr = np.empty((L, 2, 128, KC, NCOLS[kind]), np.float32)
        for l in range(L):
            for pr in range(2):
                sel = colsel(kind, pr)
                arr[l, pr] = w_in[l][:, sel].reshape(KC, 128, -1).transpose(1, 0, 2)
        out["w_" + kind] = arr
    wo = np.empty((L, 6, 128, 2, D_MODEL), np.float32)
    for l in range(L):
        for g in range(6):
            wo[l, g] = w_out[l][g * 256:(g + 1) * 256].reshape(2, 128, D_MODEL).transpose(1, 0, 2)
    out["w_o"] = wo

    idx = col_layout(L)
    cols = np.zeros((128, len(idx)), np.float32)
    for l in range(L):
        for c in range(KC):
            cols[:, idx["nw", l, c]] = inp["norm_w"][l][c * 128:(c + 1) * 128]
        for pr in range(2):
            h0, h1 = 2 * pr, 2 * pr + 1
            b = b_in[l]
            sl = lambda seg, h, w=128: b[OFF[seg] + h * w: OFF[seg] + (h + 1) * w]
            cols[:, idx["hg", l, pr, "q0"]] = sl("hg_q", h0)
            cols[:, idx["hg", l, pr, "q1"]] = sl("hg_q", h1)
            cols[:, idx["hg", l, pr, "f0"]] = sl("hg_f", h0)
            cols[:, idx["hg", l, pr, "f1"]] = sl("hg_f", h1)
            cols[:, idx["hg", l, pr, "z0"]] = sl("hg_z", h0)
            cols[:, idx["hg", l, pr, "z1"]] = sl("hg_z", h1)
            cols[:, idx["hg", l, pr, "nw0"]] = inp["hg_norm_w"][l][h0 * 128:(h0 + 1) * 128]
            cols[:, idx["hg", l, pr, "nw1"]] = inp["hg_norm_w"][l][h1 * 128:(h1 + 1) * 128]
            cols[:, idx["gla", l, pr, "q"]] = b[OFF["gla_q"] + h0 * 64: OFF["gla_q"] + h0 * 64 + 128]
            cols[:, idx["gla", l, pr, "k"]] = b[OFF["gla_k"] + h0 * 64: OFF["gla_k"] + h0 * 64 + 128]
            cols[:, idx["gla", l, pr, "z0"]] = sl("gla_z", h0)
            cols[:, idx["gla", l, pr, "z1"]] = sl("gla_z", h1)
            cols[0:16, idx["gla", l, pr, "a16"]] = b[OFF["gla_a"]:OFF["gla_a"] + 16]
            cols[:, idx["gla", l, pr, "ba2"]] = inp["gla_b_a2"][l][h0 * 64: h0 * 64 + 128]
            cols[:, idx["gla", l, pr, "nw0"]] = inp["gla_norm_w"][l][h0 * 128:(h0 + 1) * 128]
            cols[:, idx["gla", l, pr, "nw1"]] = inp["gla_norm_w"][l][h1 * 128:(h1 + 1) * 128]
            cols[:, idx["ml", l, pr, "q"]] = b[OFF["ml_q"] + h0 * 64: OFF["ml_q"] + h0 * 64 + 128]
            cols[:, idx["ml", l, pr, "k"]] = b[OFF["ml_k"] + h0 * 64: OFF["ml_k"] + h0 * 64 + 128]
            cols[0:64, idx["ml", l, pr, "i"]] = b[OFF["ml_i"] + h0]
            cols[64:128, idx["ml", l, pr, "i"]] = b[OFF["ml_i"] + h1]
            cols[0:64, idx["ml", l, pr, "f"]] = b[OFF["ml_f"] + h0]
            cols[64:128, idx["ml", l, pr, "f"]] = b[OFF["ml_f"] + h1]
            cols[:, idx["ml", l, pr, "o0"]] = sl("ml_o", h0)
            cols[:, idx["ml", l, pr, "o1"]] = sl("ml_o", h1)
            cols[:, idx["ml", l, pr, "z0"]] = sl("ml_z", h0)
            cols[:, idx["ml", l, pr, "z1"]] = sl("ml_z", h1)
            cols[:, idx["ml", l, pr, "nw0"]] = inp["ml_norm_w"][l][h0 * 128:(h0 + 1) * 128]
            cols[:, idx["ml", l, pr, "nw1"]] = inp["ml_norm_w"][l][h1 * 128:(h1 + 1) * 128]
            cw = inp["ml_conv_w"][l]
            cb = inp["ml_conv_b"][l]
            cols[:, idx["ml", l, pr, "cbq"]] = cb[h0 * 64: h0 * 64 + 128]
            cols[:, idx["ml", l, pr, "cbk"]] = cb[256 + h0 * 64: 256 + h0 * 64 + 128]
            for j in range(4):
                cols[:, idx["ml", l, pr, f"cwq{j}"]] = cw[j, h0 * 64: h0 * 64 + 128]
                cols[:, idx["ml", l, pr, f"cwk{j}"]] = cw[j, 256 + h0 * 64: 256 + h0 * 64 + 128]
            for kind, seg in (("hg", "hg_i"), ("gla", "gla_v"), ("ml", "ml_v")):
                cols[:, idx[kind, l, pr, "v0"]] = sl(seg, h0)
                cols[:, idx[kind, l, pr, "v1"]] = sl(seg, h1)
    out["cols"] = cols
    out["wa2"] = np.ascontiguousarray(np.asarray(inp["gla_w_a2"], np.float32)[:L].transpose(1, 0, 2))
    out["lbl"] = np.ascontiguousarray(
        np.asarray(inp["hg_lb_logits"], np.float32)[:L].reshape(L, 4, 128).transpose(2, 1, 0))
    out["fnw"] = np.ascontiguousarray(np.broadcast_to(np.asarray(inp["final_norm_w"], np.float32)[None, :], (128, D_MODEL)))
    return out


def build(T=SEQ, L=DEPTH, enable=("hg", "gla", "ml")):
    _, used = _build(T, L, enable, None)
    nc, _ = _build(T, L, enable, used if SPARSE else None)
    return nc


def _build(T, L, enable, needed):
    nc = bass.Bass("TRN2", target_bir_lowering=False)
    NT = T // TT
    NS = T // ST
    idx = col_layout(L)
    NCL = len(idx)

    def dram(name, shape, kind="ExternalInput"):
        return nc.dram_tensor(name, list(shape), F32, kind=kind).ap()

    x_d = dram("x", [T, D_MODEL])
    y_d = dram("y", [T, D_MODEL], kind="ExternalOutput")
    w_d = {k: dram("w_" + k, [L, 2, 128, KC, NCOLS[k]]) for k in ("hg", "gla", "ml")}
    wo_d = dram("w_o", [L, 6, 128, 2, D_MODEL])
    cols_d = dram("cols", [128, NCL])
    wa2_d = dram("wa2", [16, L, 256])
    lbl_d = dram("lbl", [128, 4, L])
    fnw_d = dram("fnw", [128, D_MODEL])

    with ExitStack() as stack:
        S = Sched(nc, stack, needed)
        op = S.op

        xT = S.sbuf("xT", [128, KC, T], F32)
        xTg = [[S.view(xT, f"xT{c}_{s}") for s in range(NS)] for c in range(KC)]
        hT = S.sbuf("hT", [128, KC, T], BF16)
        hTg = [S.view(hT, f"hT{s}") for s in range(NS)]
        full = set(enable) == {"hg", "gla", "ml"}
        wbuf = [S.sbuf(f"wbuf{i}", [128, KC, 1280 if (i == 0 or not full) else 1024], BF16) for i in range(2)]
        wobuf = [S.sbuf(f"wobuf{i}", [128, 2, D_MODEL], BF16) for i in range(2)]
        cols = S.sbuf("cols", [128, NCL], F32)
        ncols = S.sbuf("ncols", [128, NCL], F32)
        hcols = S.sbuf("hcols", [128, NCL], F32)
        wa2 = S.sbuf("wa2", [16, L, 256], BF16)
        lbl = S.sbuf("lbl", [128, 4, L], F32)
        lbA = S.sbuf("lbA", [128, 4, L], F32)
        lbB = S.sbuf("lbB", [128, 4, L], F32)
        lbNB = S.sbuf("lbNB", [128, 4, L], F32)
        identf = S.sbuf("identf", [128, 128], F32)
        identb = S.sbuf("identb", [128, 128], BF16)
        onesb = S.sbuf("onesb", [128, 128], BF16)
        mask = S.sbuf("mask", [128, 128], BF16)
        rst = S.sbuf("rst", [128, ST], BF16)
        cst = S.sbuf("cst", [128, 8], F32)
        C_LN8, C_E1024, C_E128, C_ONE, C_ZERO, C_LN32, C_LNS128 = 0, 1, 2, 3, 4, 5, 6

        pb = [S.psum(f"pb{i}", [128, 512], F32) for i in range(8)]
        att_slots = [pb[2]] * 4
        u_slots = [pb[3]] * 2
        ktp = pb[3]

        def att_ap(i):
            return pb[2].t[:, i * 128:(i + 1) * 128]

        def ktp_ap():
            return pb[3].t[:, 256:512].bitcast(BF16)

        S.dma("sp", cols.t[:], cols_d, writes=[cols])
        S.dma("sp", lbl.t[:], lbl_d, writes=[lbl])
        S.dma("pool", wa2.t[:], wa2_d, writes=[wa2])
        op("dve", lambda E: E.tensor_scalar(ncols.t[:], cols.t[:], -1.0, None, op0=ALU.mult), [cols], [ncols])
        op("dve", lambda E: E.tensor_scalar(hcols.t[:], cols.t[:], 0.5, None, op0=ALU.mult), [cols], [hcols])
        op("pool", lambda E: E.memset(identf.t[:], 0.0), [], [identf])
        op("pool", lambda E: E.affine_select(out=identf.t[:], in_=identf.t[:], pattern=[[-1, 128]],
                                             compare_op=ALU.not_equal, fill=1.0, base=0, channel_multiplier=1),
           [identf], [identf])
        op("pool", lambda E: E.tensor_copy(identb.t[:], identf.t[:]), [identf], [identb])
        op("pool", lambda E: E.memset(onesb.t[:], 1.0), [], [onesb])
        op("pool", lambda E: E.memset(mask.t[:], 1.0), [], [mask])
        op("pool", lambda E: E.affine_select(out=mask.t[:], in_=mask.t[:], pattern=[[1, 128]],
                                             compare_op=ALU.is_ge, fill=0.0, base=0, channel_multiplier=-1),
           [mask], [mask])
        op("pool", lambda E: E.memset(rst.t[:], 1.0), [], [rst])
        for j in range(ST // TT):
            op("pool", lambda E, j=j: E.memset(rst.t[:, j * TT:j * TT + 1], 0.0), [], [rst])
        for ci, val in ((C_LN8, math.log(0.125)), (C_E1024, 1024.0 * EPS), (C_E128, 128.0 * EPS),
                        (C_ONE, 1.0), (C_ZERO, 0.0), (C_LN32, math.log(32.0)), (C_LNS128, 0.5 * math.log(128.0))):
            op("pool", lambda E, ci=ci, val=val: E.memset(cst.t[:, ci:ci + 1], val), [], [cst])

        def cc(i):
            return cst.t[:, i:i + 1]

        for b in (2, 3):
            op("dve", lambda E, b=b: E.memset(pb[b].t[:], 0.0), [], [pb[b]])

        tmpl = S.sbuf("tmpl", [128, 4, L], F32)
        suml = S.sbuf("suml", [128, 4], F32)
        lb = S.sbuf("lb", [128, 4, L], F32)
        op("act", lambda E: E.activation(out=tmpl.t[:], in_=lbl.t[:], func=AF.Exp), [lbl], [tmpl])
        op("dve", lambda E: E.tensor_reduce(out=suml.t[:], in_=tmpl.t[:], axis=mybir.AxisListType.X, op=ALU.add), [tmpl], [suml])
        op("dve", lambda E: E.reciprocal(suml.t[:], suml.t[:]), [suml], [suml])
        op("dve", lambda E: E.tensor_tensor(out=tmpl.t[:], in0=tmpl.t[:], in1=suml.t[:].unsqueeze(2).to_broadcast([128, 4, L]),
                                            op=ALU.mult), [tmpl, suml], [tmpl])
        op("dve", lambda E: E.memset(lb.t[:], 0.0), [], [lb])
        for l in range(1, L):
            op("dve", lambda E, l=l: E.tensor_tensor(out=lb.t[:, :, l], in0=lb.t[:, :, l - 1], in1=tmpl.t[:, :, l], op=ALU.add),
               [lb, tmpl], [lb])
        op("dve", lambda E: E.tensor_scalar(lbA.t[:], lb.t[:], 0.5, 0.5, op0=ALU.mult, op1=ALU.add), [lb], [lbA])
        op("dve", lambda E: E.tensor_scalar(lbB.t[:], lb.t[:], -0.5, 0.5, op0=ALU.mult, op1=ALU.add), [lb], [lbB])
        op("dve", lambda E: E.tensor_scalar(lbNB.t[:], lb.t[:], 0.5, -0.5, op0=ALU.mult, op1=ALU.add), [lb], [lbNB])

        R32N = ["qs0", "qs1", "th0", "th1", "gz0_0", "gz1_0", "gz0_1", "gz1_1", "kk0", "kk1", "bcs0", "br0", "e1_0", "lnv0"]
        W32 = S.sbuf("W32", [128, len(R32N), ST], F32)
        r32 = {}
        for i, nm in enumerate(R32N):
            b_ = Buf(W32.t[:, i, :], nm)
            r32[nm] = b_
        R16N = ["sq0", "sq1", "y0", "y1"] + [f"{a}{u}_{p}" for p in range(2) for u in range(2) for a in ("qT", "kT", "ktok")]
        W16 = S.sbuf("W16", [128, len(R16N), ST], BF16)
        r16 = {}
        for i, nm in enumerate(R16N):
            r16[nm] = Buf(W16.t[:, i, :], nm)

        def wide32(i0):
            return W32.t[:, i0:i0 + 2, :].rearrange("p a b -> p (a b)")

        xin_views = [(wide32(0), [r32["qs0"], r32["qs1"]]), (wide32(2), [r32["th0"], r32["th1"]])]
        fnw_ap, fnw_b = wide32(4), [r32["gz0_0"], r32["gz1_0"]]

        class Rot:
            def __init__(self, name, n, shape, dt):
                self.b = [S.sbuf(f"{name}{i}", shape, dt) for i in range(n)]
                self.i = 0

            def get(self):
                r = self.b[self.i]
                self.i = (self.i + 1) % len(self.b)
                return r

        attp = Rot("attsb_", 4, [128, 128], BF16)
        scb = {f"{a}_{u}_{p}": S.sbuf(f"sc_{a}_{u}_{p}", [128, 4], F32) for a in ("c1", "c2", "c3") for u in range(2) for p in range(2)}
        vml = [S.sbuf(f"vsb{i}", [128, 4, 2, 128], BF16) for i in range(2)]
        Sst = [S.sbuf(f"Sst{i}", [128, 256 >> i], F32) for i in range(2)]
        Sbf = [S.sbuf(f"Sbf{i}", [128, 256 >> i], BF16) for i in range(2)]
        utmp = [S.sbuf(f"utmp{i}", [128, 256 >> i], F32) for i in range(2)]
        convin = [S.sbuf(f"convin{i}", [128, ST + 3], F32) for i in range(2)]
        vcnt = [0]
        gen_i = [0]

        step_kind = ["ml"]

        def gen_bank(kind):
            banks = [0, 1, 3] if step_kind[0] == "ml" else [0, 1, 3, 6, 7]
            b = banks[gen_i[0] % len(banks)]
            gen_i[0] += 1
            return pb[b]

        def act(out, in_, func, bias=None, scale=1.0, reads=(), writes=()):
            kw = {}
            if bias is not None:
                kw["bias"] = bias
            return op("act", lambda E: E.activation(out=out, in_=in_, func=func, scale=scale, **kw), reads, writes)

        for tt in range(NT):
            xin_ap, xin_b = xin_views[tt % 2]
            S.dma("sp", xin_ap, x_d[tt * TT:(tt + 1) * TT, :], writes=xin_b)
            st_ = tt // 4
            for half in range(2):
                ps = pb[half]
                for cq in range(4):
                    c = half * 4 + cq
                    op("pe", lambda E, c=c, cq=cq, ps=ps, xin_ap=xin_ap: E.transpose(ps.t[:, cq * 128:(cq + 1) * 128],
                                                                                   xin_ap[:, c * 128:(c + 1) * 128], identf.t[:]),
                       xin_b + [identf], [ps])
                tg = [xTg[half * 4 + cq][st_] for cq in range(4)]
                if half == 0:
                    op("dve", lambda E, ps=ps, half=half, tt=tt: E.tensor_copy(
                        xT.t[:, half * 4:(half + 1) * 4, tt * TT:(tt + 1) * TT],
                        ps.t[:].rearrange("p (a b) -> p a b", a=4)), [ps], tg)
                else:
                    act(xT.t[:, half * 4:(half + 1) * 4, tt * TT:(tt + 1) * TT],
                        ps.t[:].rearrange("p (a b) -> p a b", a=4), AF.Copy, reads=[ps], writes=tg)

        ORDER = [4, 0, 5, 1, 2, 3]
        glist = [(l, gi) for l in range(L) for gi in ORDER if GROUPS[gi][0] in enable]

        def load_weights(n):
            if n >= len(glist):
                return
            l, gi = glist[n]
            kind, pr = GROUPS[gi]
            wb = wbuf[n % 2]
            wo = wobuf[n % 2]
            ncl = NCOLS[kind]
            for c in range(KC):
                S.dma("pool", wb.t[:, c, 0:ncl], w_d[kind][l, pr, :, c, :], writes=[wb])
            S.dma("pool", wo.t[:], wo_d[l, gi], writes=[wo])

        load_weights(0)
        load_weights(1)

        def norm_phase(l):
            for s in range(NS):
                tok = slice(s * ST, (s + 1) * ST)
                ps = gen_bank("ml")
                for c in range(KC):
                    sq = r16[f"sq{c % 2}"]
                    act(sq.t[:], xT.t[:, c, tok], AF.Square, reads=[xTg[c][s]], writes=[sq])
                    op("pe", lambda E, ps=ps, sq=sq, c=c: E.matmul(ps.t[:], lhsT=onesb.t[:], rhs=sq.t[:], start=(c == 0), stop=(c == KC - 1)),
                       [onesb, sq], [ps])
                rstd = r32["lnv0"]
                act(rstd.t[:], ps.t[:], AF.Ln, bias=cc(C_E1024), reads=[ps, cst], writes=[rstd])
                act(rstd.t[:], rstd.t[:], AF.Exp, scale=-0.5, bias=cc(C_LN32), reads=[rstd, cst], writes=[rstd])
                for c in range(KC):
                    ci = idx["nw", l, c]
                    op("dve",
                       lambda E, c=c, ci=ci, rstd=rstd, tok=tok: E.scalar_tensor_tensor(
                           out=hT.t[:, c, tok], in0=xT.t[:, c, tok], scalar=cols.t[:, ci:ci + 1], in1=rstd.t[:],
                           op0=ALU.mult, op1=ALU.mult),
                       [xTg[c][s], cols, rstd], [hTg[s]])

        class Item:
            pass

        items = []
        for n, (l, gi) in enumerate(glist):
            for s in range(NS):
                it = Item()
                it.l, it.gi, it.s, it.n = l, gi, s, n
                it.kind, it.pr = GROUPS[gi]
                it.p = len(items) % 2
                it.ML = it.kind == "ml"
                it.dvp = 256 if it.ML else 128
                it.wb, it.wo = wbuf[n % 2], wobuf[n % 2]
                it.tok = slice(s * ST, (s + 1) * ST)
                it.vt = vml[len(items) % 2]
                it.gz = [r32[f"gz0_{it.p}"], r32[f"gz1_{it.p}"]]
                it.tho = [r32["th0"], r32["th1"]]
                it.heads = [(0, 0, 128), (1, 0, 128)] if it.kind == "hg" else [(0, 0, 64), (0, 64, 64)]
                it.nunit = 2 if it.kind == "hg" else 1
                it.last_of_group = (s == NS - 1)
                it.last_of_layer = it.last_of_group and (n + 1 == len(glist) or glist[n + 1][0] != l)
                it.first_of_layer = (s == 0) and (n == 0 or glist[n - 1][0] != l)
                items.append(it)

        def Ccol(it, nm, tab=None):
            tab = cols if tab is None else tab
            i = idx[it.kind, it.l, it.pr, nm]
            return tab.t[:, i:i + 1]

        def proj_fm(it, j, M=128):
            ps = gen_bank(it.kind)
            for c in range(KC):
                op("pe", lambda E, c=c, ps=ps: E.matmul(ps.t[0:M, :], lhsT=it.wb.t[:, c, j:j + M], rhs=hT.t[:, c, it.tok],
                                                        start=(c == 0), stop=(c == KC - 1)),
                   [it.wb, hTg[it.s]], [ps])
            return ps

        def genA1(it):
            kind, l, pr, s = it.kind, it.l, it.pr, it.s
            C = lambda nm, tab=None: Ccol(it, nm, tab)
            if kind == "hg":
                qs = [r32["qs0"], r32["qs1"]]
                th = [r32["th0"], r32["th1"]]
                kkb = [r32["kk0"], r32["kk1"]]
                for hh in range(2):
                    ps = proj_fm(it, hh * 128)
                    act(qs[hh].t[:], ps.t[:], AF.Silu, bias=C(f"q{hh}"), reads=[ps, cols], writes=[qs[hh]])
                    yield
                for hh in range(2):
                    ps = proj_fm(it, 256 + hh * 128)
                    act(th[hh].t[:], ps.t[:], AF.Tanh, bias=C(f"f{hh}", hcols), scale=0.5, reads=[ps, hcols], writes=[th[hh]])
                    yield
                prep = []
                for hh in range(2):
                    h = 2 * pr + hh
                    op("pool", lambda E, hh=hh, h=h: E.tensor_scalar(kkb[hh].t[:], th[hh].t[:], lbNB.t[:, h, l:l + 1], lbB.t[:, h, l:l + 1],
                                                                    op0=ALU.mult, op1=ALU.add), [th[hh], lbNB, lbB], [kkb[hh]])
                    op("pool", lambda E, hh=hh, h=h: E.tensor_scalar(th[hh].t[:], th[hh].t[:], lbB.t[:, h, l:l + 1], lbA.t[:, h, l:l + 1],
                                                                    op0=ALU.mult, op1=ALU.add), [th[hh], lbB, lbA], [th[hh]])
                for hh in range(2):
                    act(th[hh].t[:], th[hh].t[:], AF.Ln, reads=[th[hh]], writes=[th[hh]])
                    prep.append((qs[hh], kkb[hh], th[hh], 1.0, None))
            elif kind == "gla":
                ql, kl, sp_ = r32["qs0"], r32["qs1"], r32["th0"]
                ps = proj_fm(it, 0)
                act(ql.t[:], ps.t[:], AF.Identity, bias=C("q"), reads=[ps, cols], writes=[ql])
                yield
                ps = proj_fm(it, 128)
                act(kl.t[:], ps.t[:], AF.Identity, bias=C("k"), reads=[ps, cols], writes=[kl])
                yield
                ps = proj_fm(it, 512, M=16)
                i16 = idx[kind, l, pr, "a16"]
                a16b = r16["sq1"]
                act(a16b.t[0:16, :], ps.t[0:16, :], AF.Identity, bias=cols.t[0:16, i16:i16 + 1], reads=[ps, cols], writes=[a16b])
                ps = gen_bank(kind)
                op("pe", lambda E, ps=ps: E.matmul(ps.t[:], lhsT=wa2.t[0:16, l, pr * 128:(pr + 1) * 128], rhs=a16b.t[0:16, :],
                                                   start=True, stop=True), [wa2, a16b], [ps])
                act(sp_.t[:], ps.t[:], AF.Exp, bias=C("ba2", ncols), scale=-1.0, reads=[ps, ncols], writes=[sp_])
                act(sp_.t[:], sp_.t[:], AF.Ln, bias=cc(C_ONE), reads=[sp_, cst], writes=[sp_])
                yield
                prep = [(ql, kl, sp_, -1.0 / 16.0, None)]
            else:
                cv = [r32["qs0"], r32["qs1"]]
                if s == 0:
                    for i in range(2):
                        op("pool", lambda E, i=i: E.memset(convin[i].t[:, 0:3], 0.0), [], [convin[i]])
                for qi, nm in enumerate(("q", "k")):
                    ps = proj_fm(it, qi * 128)
                    cin = convin[qi]
                    acc = cv[qi]
                    act(cin.t[:, 3:ST + 3], ps.t[:], AF.Identity, bias=C(nm), reads=[ps, cols], writes=[cin])
                    op("pool", lambda E, acc=acc, cin=cin, nm=nm: E.tensor_scalar(
                        acc.t[:], cin.t[:, 0:ST], C(f"cw{nm}0"), C(f"cb{nm}"), op0=ALU.mult, op1=ALU.add), [cin, cols], [acc])
                    for j in range(1, 4):
                        op("dve", lambda E, acc=acc, cin=cin, nm=nm, j=j: E.scalar_tensor_tensor(
                            out=acc.t[:], in0=cin.t[:, j:j + ST], scalar=C(f"cw{nm}{j}"), in1=acc.t[:],
                            op0=ALU.mult, op1=ALU.add), [cin, cols, acc], [acc])
                    op("pool", lambda E, cin=cin: E.tensor_copy(cin.t[:, 0:3], cin.t[:, ST:ST + 3]), [cin], [cin])
                    yield
                for qi in range(2):
                    act(cv[qi].t[:], cv[qi].t[:], AF.Silu, reads=[cv[qi]], writes=[cv[qi]])
                irow, sp_ = r32["kk0"], r32["kk1"]
                ps = proj_fm(it, 256)
                act(irow.t[:], ps.t[:], AF.Identity, bias=C("i"), reads=[ps, cols], writes=[irow])
                yield
                ps = proj_fm(it, 384)
                act(sp_.t[:], ps.t[:], AF.Exp, bias=C("f", ncols), scale=-1.0, reads=[ps, ncols], writes=[sp_])
                act(sp_.t[:], sp_.t[:], AF.Ln, bias=cc(C_ONE), reads=[sp_, cst], writes=[sp_])
                yield
                prep = [(cv[0], cv[1], sp_, -1.0, irow)]

            it.unit = []
            for u, (qv, kv, gsrc, gscale, irow) in enumerate(prep):
                bcs, br, e1 = r32["bcs0"], r32["br0"], r32["e1_0"]
                op("dve", lambda E, bcs=bcs, gsrc=gsrc: E.tensor_tensor_scan(bcs.t[:], rst.t[:], gsrc.t[:], 0.0, op0=ALU.mult, op1=ALU.add),
                   [rst, gsrc], [bcs])
                b3 = bcs.t[:].rearrange("p (a b) -> p a b", a=4)
                br3 = br.t[:].rearrange("p (a b) -> p a b", a=4)
                op("pool", lambda E, br3=br3, b3=b3: E.tensor_tensor(out=br3, in0=b3, in1=b3[:, :, 63:64].to_broadcast([128, 4, 128]),
                                                                   op=ALU.subtract), [bcs], [br])
                c1, c2, c3 = (scb[f"{a}_{u}_{it.p}"] for a in ("c1", "c2", "c3"))
                act(c1.t[:], b3[:, :, 127], AF.Exp, scale=gscale, reads=[bcs], writes=[c1])
                act(c2.t[:], br3[:, :, 127], AF.Exp, scale=gscale, reads=[br], writes=[c2])
                act(c3.t[:], b3[:, :, 63], AF.Exp, scale=gscale, reads=[bcs], writes=[c3])
                if kind == "hg":
                    act(e1.t[:], br.t[:], AF.Exp, scale=gscale, reads=[br], writes=[e1])
                else:
                    act(e1.t[:], br.t[:], AF.Exp, scale=gscale, bias=cc(C_LN8), reads=[br, cst], writes=[e1])
                if irow is None:
                    act(br.t[:], br.t[:], AF.Exp, scale=-gscale, reads=[br], writes=[br])
                else:
                    op("pool", lambda E, br=br, irow=irow: E.tensor_tensor(out=br.t[:], in0=br.t[:], in1=irow.t[:], op=ALU.add),
                       [br, irow], [br])
                    act(br.t[:], br.t[:], AF.Exp, reads=[br], writes=[br])
                e2 = br
                qT, kT, ktok = (r16[f"{a}{u}_{it.p}"] for a in ("qT", "kT", "ktok"))
                op("dve", lambda E, qT=qT, qv=qv, e1=e1: E.tensor_tensor(out=qT.t[:], in0=qv.t[:], in1=e1.t[:], op=ALU.mult), [qv, e1], [qT])
                op("dve", lambda E, kT=kT, kv=kv, e2=e2: E.tensor_tensor(out=kT.t[:], in0=kv.t[:], in1=e2.t[:], op=ALU.mult), [kv, e2], [kT])
                it.unit.append((qT, kT, ktok, c1, c2, c3))
                yield

            vt = it.vt
            for hh in range(2):
                ps = proj_fm(it, VOFF[kind] + hh * 128)
                vst = r16[f"sq{hh}"]
                act(vst.t[:], ps.t[:], AF.Identity, bias=C(f"v{hh}"), reads=[ps, cols], writes=[vst])
                yield
            ps = gen_bank(kind)
            vp = ps.t[:].bitcast(BF16)
            for hh in range(2):
                vst = r16[f"sq{hh}"]
                for j in range(4):
                    op("pe", lambda E, hh=hh, j=j, vst=vst: E.transpose(vp[:, (hh * 4 + j) * 128:(hh * 4 + j + 1) * 128],
                                                                        vst.t[:, j * 128:(j + 1) * 128], identb.t[:]),
                       [vst, identb], [ps])
            act(vt.t[:], vp.rearrange("p (h j d) -> p j h d", h=2, j=4), AF.Copy, reads=[ps], writes=[vt])
            yield
            for u in range(len(it.unit)):
                qT, kT, ktok = it.unit[u][0], it.unit[u][1], it.unit[u][2]
                ps = gen_bank(kind)
                kp = ps.t[:, 0:256].bitcast(BF16)
                for j in range(4):
                    op("pe", lambda E, j=j, kT=kT, kp=kp: E.transpose(kp[:, j * 128:(j + 1) * 128], kT.t[:, j * 128:(j + 1) * 128], identb.t[:]),
                       [kT, identb], [ps])
                act(ktok.t[:], kp, AF.Copy, reads=[ps], writes=[ktok])
                yield

        def genA2(it):
            kind = it.kind
            C = lambda nm, tab=None: Ccol(it, nm, tab)
            zoff = {"hg": 512, "gla": 256, "ml": 768}[kind]
            for hh in range(2):
                ps = proj_fm(it, zoff + hh * 128)
                act(it.gz[hh].t[:], ps.t[:], AF.Silu, bias=C(f"z{hh}"), reads=[ps, cols], writes=[it.gz[hh]])
                yield

        Ob = [pb[4], pb[5]]
        Db = [pb[6], pb[7]]

        def att_bank(it, k):
            return pb[2], pb[2].t[:, (k % 2) * 128:(k % 2 + 1) * 128]

        def genCore(it):
            kind, ML, dvp, vt = it.kind, it.ML, it.dvp, it.vt
            unit, heads = it.unit, it.heads
            if it.s == 0:
                for u in range(it.nunit):
                    op("pool", lambda E, u=u: E.memset(Sst[u].t[:], 0.0), [], [Sst[u]])
            for j in range(4):
                for u in range(len(unit)):
                    c3 = unit[u][5]
                    op("dve", lambda E, u=u, c3=c3, j=j: E.tensor_scalar(Sbf[u].t[:, 0:dvp], Sst[u].t[:, 0:dvp], c3.t[:, j:j + 1], None, op0=ALU.mult),
                       [Sst[u], c3], [Sbf[u]])
                atts = []
                abs_ = [att_bank(it, j * 2 + hi) for hi in range(2)]
                for hi, (u, p0, dk) in enumerate(heads):
                    qT, kT = unit[u][0], unit[u][1]
                    ab, a_ap = abs_[hi]
                    op("pe", lambda E, a_ap=a_ap, qT=qT, kT=kT, p0=p0, dk=dk, j=j: E.matmul(
                        a_ap[:, 64:128], lhsT=kT.t[p0:p0 + dk, j * TT:(j + 1) * TT], rhs=qT.t[p0:p0 + dk, j * TT + 64:(j + 1) * TT],
                        start=True, stop=True), [qT, kT], [ab])
                    op("pe", lambda E, a_ap=a_ap, qT=qT, kT=kT, p0=p0, dk=dk, j=j: E.matmul(
                        a_ap[0:64, 0:64], lhsT=kT.t[p0:p0 + dk, j * TT:j * TT + 64], rhs=qT.t[p0:p0 + dk, j * TT:j * TT + 64],
                        start=True, stop=True), [qT, kT], [ab])
                for hi, (u, p0, dk) in enumerate(heads):
                    ab, a_ap = abs_[hi]
                    asb = attp.get()
                    op("dve", lambda E, asb=asb, a_ap=a_ap: E.tensor_tensor(out=asb.t[:], in0=a_ap, in1=mask.t[:], op=ALU.mult),
                       [ab, mask], [asb])
                    atts.append(asb)
                yield
                for hi, (u, p0, dk) in enumerate(heads):
                    qT, kT, ktok = unit[u][0], unit[u][1], unit[u][2]
                    asb = atts[hi]
                    outs = [(Ob[hi], 0)] + ([(Db[hi], 128)] if ML else [])
                    for (ob, vo) in outs:
                        op("pe", lambda E, ob=ob, vo=vo, asb=asb, hi=hi, j=j: E.matmul(
                            ob.t[:, j * TT:(j + 1) * TT], lhsT=(vt.t[:, j, hi, :] if vo == 0 else onesb.t[:]), rhs=asb.t[:], start=True, stop=False),
                           [vt, asb, onesb], [ob])
                        op("pe", lambda E, ob=ob, vo=vo, u=u, p0=p0, dk=dk, qT=qT, j=j: E.matmul(
                            ob.t[:, j * TT:(j + 1) * TT], lhsT=Sbf[u].t[p0:p0 + dk, vo:vo + 128], rhs=qT.t[p0:p0 + dk, j * TT:(j + 1) * TT],
                            start=False, stop=True), [Sbf[u], qT], [ob])
                    ucol = 0 if ML else u * 128
                    op("pe", lambda E, ktok=ktok, p0=p0, dk=dk, hi=hi, j=j, ucol=ucol: E.matmul(
                        pb[2].t[p0:p0 + dk, 256 + ucol:256 + ucol + 128], lhsT=ktok.t[:, j * 128 + p0:j * 128 + p0 + dk], rhs=vt.t[:, j, hi, :],
                        start=True, stop=True), [ktok, vt], [pb[2]])
                    if ML:
                        op("pe", lambda E, ktok=ktok, p0=p0, dk=dk, hi=hi, j=j: E.matmul(
                            pb[2].t[p0:p0 + dk, 384:512], lhsT=ktok.t[:, j * 128 + p0:j * 128 + p0 + dk], rhs=onesb.t[:],
                            start=True, stop=True), [ktok, onesb], [pb[2]])
                for u in range(len(unit)):
                    c1, c2 = unit[u][3], unit[u][4]
                    ucol = 0 if ML else u * 128
                    op("dve", lambda E, u=u, c2=c2, j=j, ucol=ucol: E.tensor_scalar(
                        utmp[u].t[:, 0:dvp], pb[2].t[:, 256 + ucol:256 + ucol + dvp], c2.t[:, j:j + 1], None, op0=ALU.mult),
                       [pb[2], c2], [utmp[u]])
                    op("dve", lambda E, u=u, c1=c1, j=j: E.scalar_tensor_tensor(
                        out=Sst[u].t[:, 0:dvp], in0=Sst[u].t[:, 0:dvp], scalar=c1.t[:, j:j + 1], in1=utmp[u].t[:, 0:dvp],
                        op0=ALU.mult, op1=ALU.add), [Sst[u], c1, utmp[u]], [Sst[u]])
                yield

        def postHead(it, hi, dsqs):
            kind, ML = it.kind, it.ML
            nwc = Ccol(it, f"nw{hi}")
            lnv = r32["lnv0"] if hi == 0 else r32["e1_0"]
            if ML:
                u2 = r32["qs0"] if hi == 0 else r32["kk0"]
                op("dve", lambda E: E.scalar_tensor_tensor(
                    out=u2.t[:], in0=it.tho[hi].t[:], scalar=1.0, in1=Ob[hi].t[:], op0=ALU.add, op1=ALU.mult),
                   [it.tho[hi], Ob[hi]], [u2])
                yield
                src, srcb = u2.t[:], u2
            else:
                src, srcb = Ob[hi].t[:], Ob[hi]
            osq = r16[f"sq{hi}"]
            act(osq.t[:], src, AF.Square, reads=[srcb], writes=[osq])
            yield
            ps = gen_bank(kind)
            op("pe", lambda E: E.matmul(ps.t[:], lhsT=onesb.t[:], rhs=osq.t[:], start=True, stop=True), [onesb, osq], [ps])
            if ML:
                dsq = dsqs[hi]
                op("dve", lambda E: E.tensor_tensor(out=dsq.t[:], in0=ps.t[:], in1=dsq.t[:], op=ALU.add), [ps, dsq], [dsq])
                yield
                act(lnv.t[:], dsq.t[:], AF.Ln, reads=[dsq], writes=[lnv])
            else:
                act(lnv.t[:], ps.t[:], AF.Ln, bias=cc(C_E128), reads=[ps, cst], writes=[lnv])
            yield
            act(lnv.t[:], lnv.t[:], AF.Exp, scale=-0.5, bias=cc(C_LNS128), reads=[lnv, cst], writes=[lnv])
            yield
            t2 = it.gz[hi]
            op("dve", lambda E: E.scalar_tensor_tensor(
                out=t2.t[:], in0=t2.t[:], scalar=nwc, in1=lnv.t[:], op0=ALU.mult, op1=ALU.mult), [t2, cols, lnv], [t2])
            yield
            y_ = r16[f"y{hi}"]
            op("dve", lambda E: E.tensor_tensor(out=y_.t[:], in0=src, in1=t2.t[:], op=ALU.mult), [srcb, t2], [y_])
            it.ys[hi] = y_
            yield

        def genPost(it):
            ML = it.ML
            it.ys = [None, None]
            dsqs = [r32["qs1"], r32["kk1"]]
            if ML:
                for hi in range(2):
                    dsq = dsqs[hi]
                    act(dsq.t[:], Db[hi].t[:], AF.Square, reads=[Db[hi]], writes=[dsq])
                for hi in range(2):
                    dsq = dsqs[hi]
                    op("dve", lambda E, dsq=dsq: E.tensor_scalar(dsq.t[:], dsq.t[:], 1.0, 512.0 * EPS, op0=ALU.max, op1=ALU.mult), [dsq], [dsq])
                for hi in range(2):
                    pso = proj_fm(it, 512 + hi * 128)
                    act(it.tho[hi].t[:], pso.t[:], AF.Tanh, bias=Ccol(it, f"o{hi}", hcols), scale=0.5, reads=[pso, hcols], writes=[it.tho[hi]])
                    yield
            alive = [postHead(it, 0, dsqs), postHead(it, 1, dsqs)]
            k = 0
            while alive:
                for g in list(alive):
                    try:
                        next(g)
                    except StopIteration:
                        alive.remove(g)
                k += 1
                if k % 3 == 0:
                    yield

        def genOut(it):
            for c in range(KC):
                ps = gen_bank(it.kind)
                for hi in range(2):
                    op("pe", lambda E, ps=ps, hi=hi, c=c: E.matmul(ps.t[:], lhsT=it.wo.t[:, hi, c * 128:(c + 1) * 128], rhs=it.ys[hi].t[:],
                                                                  start=(hi == 0), stop=(hi == 1)), [it.wo, it.ys[hi]], [ps])
                op("dve", lambda E, ps=ps, c=c: E.tensor_tensor(out=xT.t[:, c, it.tok], in0=xT.t[:, c, it.tok], in1=ps.t[:], op=ALU.add),
                   [ps, xTg[c][it.s]], [xTg[c][it.s]])
                yield

        def run(*gens):
            gens = [g for g in gens if g is not None]
            while gens:
                for g in list(gens):
                    try:
                        next(g)
                    except StopIteration:
                        gens.remove(g)

        DEFER_OUT = True
        pending = None
        for i, it in enumerate(items):
            nxt = items[i + 1] if i + 1 < len(items) else None
            step_kind[0] = "ml" if it.first_of_layer else it.kind
            if it.first_of_layer:
                norm_phase(it.l)
                run(genA1(it))
                run(genA2(it))
            pipe_next = nxt is not None and not nxt.first_of_layer
            step_kind[0] = it.kind
            if DEFER_OUT:
                run(genCore(it), genA1(nxt) if pipe_next else None, genOut(pending) if pending is not None else None)
                if pending is not None and pending.last_of_group:
                    load_weights(pending.n + 2)
                pending = None
                run(genPost(it), genA2(nxt) if pipe_next else None)
                if it.last_of_layer:
                    run(genOut(it))
                    if it.last_of_group:
                        load_weights(it.n + 2)
                else:
                    pending = it
            else:
                run(genCore(it), genA1(nxt) if pipe_next else None)
                run(genPost(it))
                run(genOut(it), genA2(nxt) if pipe_next else None)
                if it.last_of_group:
                    load_weights(it.n + 2)

        S.dma("sp", fnw_ap, fnw_d, writes=fnw_b)
        ssq = S.sbuf("ssq", [128, 1], F32)
        junk, junk2 = r32["kk0"], r32["kk1"]
        for tt in range(NT):
            s = tt // 4
            xo_ap, xo_b = xin_views[tt % 2]
            for half in range(2):
                ps = pb[half]
                for cq in range(4):
                    c = half * 4 + cq
                    op("pe", lambda E, c=c, cq=cq, ps=ps, tt=tt: E.transpose(ps.t[:, cq * 128:(cq + 1) * 128],
                                                                             xT.t[:, c, tt * TT:(tt + 1) * TT], identf.t[:]),
                       [xTg[c][s], identf], [ps])
                if half == 0:
                    op("dve", lambda E, ps=ps, xo_ap=xo_ap: E.tensor_copy(xo_ap[:, 0:512], ps.t[:]), [ps], xo_b)
                else:
                    act(xo_ap[:, 512:1024], ps.t[:], AF.Copy, reads=[ps], writes=xo_b)
            act(junk.t[:], xo_ap[:, 0:512], AF.Square, reads=xo_b, writes=[junk])
            act(junk2.t[:], xo_ap[:, 512:1024], AF.Square, reads=xo_b, writes=[junk2])
            op("dve", lambda E: E.tensor_tensor(out=junk.t[:], in0=junk.t[:], in1=junk2.t[:], op=ALU.add), [junk, junk2], [junk])
            op("dve", lambda E: E.tensor_reduce(out=ssq.t[:], in_=junk.t[:], axis=mybir.AxisListType.X, op=ALU.add), [junk], [ssq])
            act(ssq.t[:], ssq.t[:], AF.Ln, bias=cc(C_E1024), reads=[ssq, cst], writes=[ssq])
            act(ssq.t[:], ssq.t[:], AF.Exp, scale=-0.5, reads=[ssq], writes=[ssq])
            op("dve", lambda E, xo_ap=xo_ap: E.scalar_tensor_tensor(out=xo_ap, in0=xo_ap, scalar=ssq.t[:, 0:1], in1=fnw_ap, op0=ALU.mult, op1=ALU.mult),
               xo_b + [ssq] + fnw_b, xo_b)
            op("dve", lambda E, xo_ap=xo_ap: E.tensor_scalar(xo_ap, xo_ap, 32.0, None, op0=ALU.mult), xo_b, xo_b)
            S.dma("sp", y_d[tt * TT:(tt + 1) * TT, :], xo_ap, reads=xo_b, final=True)
        S.finish()
        used = S.used
    return nc, used


_CACHE = {}


def kernel(**inputs):
    x = np.asarray(inputs["x"], np.float32)
    B, T, _ = x.shape
    L = int(np.asarray(inputs["w_in"]).shape[0])
    shared = host_prep(inputs, L)
    key = (T, L)
    if key not in _CACHE:
        _CACHE[key] = build(T, L)
    nc = _CACHE[key]
    in_maps = []
    for b in range(B):
        m = dict(shared)
        m["x"] = np.ascontiguousarray(x[b])
        in_maps.append(m)
    res = run_bass_kernel_spmd(nc, in_maps, core_ids=list(range(B)))
    return np.stack([np.asarray(r["y"], np.float32) for r in res.results], axis=0)
```

```python
import math
from contextlib import ExitStack

import numpy as np
import concourse.bass as bass
import concourse.mybir as mybir
from concourse.bass_utils import run_bass_kernel_spmd

F32 = mybir.dt.float32
BF16 = mybir.dt.bfloat16
AF = mybir.ActivationFunctionType
ALU = mybir.AluOpType

D_MODEL = 1024
KC = 8
SEQ = 2048
DEPTH = 4
BATCH = 8
EPS = 1e-6
ST = 512
TT = 128

OFF = dict(hg_q=0, hg_f=512, hg_i=1024, hg_z=1536, gla_q=2048, gla_k=2304, gla_v=2560, gla_a=3072,
           gla_z=3088, ml_q=3600, ml_k=3856, ml_v=4112, ml_i=4624, ml_f=4628, ml_o=4632, ml_z=5144)

EPOCH = 30000
SPARSE = True
NDMASEM = 16


class Buf:
    def __init__(self, t, name):
        self.t = t
        self.name = name
        self.writer = None
        self.readers = {}


class Sched:
    def __init__(self, nc, stack, needed=None):
        self.nc = nc
        self.stack = stack
        self.dry = needed is None
        self.needed = needed if needed is not None else set()
        self.used = set()
        self.E = {"pe": nc.tensor, "act": nc.scalar, "dve": nc.vector, "pool": nc.gpsimd, "sp": nc.sync}
        self.cnt = {e: 0 for e in self.E}
        self.sig = {e: 0 for e in self.E}
        self.rank = {e: {} for e in self.E}
        self.sems = {e: [] for e in self.E}
        self.known = {e: {} for e in self.E}
        self.dpool = {}
        self.final = []

    def sbuf(self, name, shape, dt):
        return Buf(self.stack.enter_context(self.nc.sbuf_tensor("sb_" + name, shape, dt)), name)

    def psum(self, name, shape, dt):
        return Buf(self.stack.enter_context(self.nc.psum_tensor("ps_" + name, shape, dt)), name)

    def view(self, parent, name):
        return Buf(parent.t, name)

    def _newsem(self, name):
        return self.stack.enter_context(self.nc.semaphore(name))

    def _esem(self, e, r):
        ep = (r - 1) // EPOCH
        while len(self.sems[e]) <= ep:
            self.sems[e].append(self._newsem(f"s_{e}_{len(self.sems[e])}"))
        return self.sems[e][ep], r - ep * EPOCH

    def _deps(self, e, reads, writes):
        deps = {}

        def add(tok):
            key = tok[1]
            val = tok[2] if tok[0] == "e" else tok[3]
            if key not in deps or deps[key][0] < val:
                deps[key] = (val, tok)

        for r in reads:
            if r.writer is not None:
                add(r.writer)
        for w in writes:
            if w.writer is not None and (w.writer[1] != e or e != "pe"):
                add(w.writer)
            for tok in w.readers.values():
                if tok[1] != e or e != "pe":
                    add(tok)
        for key, (val, tok) in deps.items():
            if self.known[e].get(key, 0) >= val:
                continue
            if tok[0] == "e":
                f, k = tok[1], tok[2]
                self.used.add((f, k))
                sem, sval = self._esem(f, self.rank[f][k])
                self.E[e].wait_ge(sem, sval)
            else:
                self.E[e].wait_ge(tok[2], tok[3])
            self.known[e][key] = val

    def _commit(self, tok, reads, writes):
        for r in reads:
            r.readers[tok[1]] = tok
        for w in writes:
            w.writer = tok
            w.readers = {}

    def op(self, e, fn, reads=(), writes=()):
        self._deps(e, reads, writes)
        ins = fn(self.E[e])
        self.cnt[e] += 1
        k = self.cnt[e]
        if self.dry or (e, k) in self.needed:
            self.sig[e] += 1
            sem, _ = self._esem(e, self.sig[e])
            ins.then_inc(sem, 1)
        self.rank[e][k] = self.sig[e]
        self._commit(("e", e, k), reads, writes)
        return ins

    def dma(self, e, out, in_, reads=(), writes=(), final=False):
        P = self.dpool.setdefault(e, {"sem": [], "use": [], "next": 0})
        i = P["next"]
        P["next"] = (P["next"] + 1) % NDMASEM
        if i >= len(P["sem"]):
            P["sem"].append(self._newsem(f"s_dma_{e}_{i}"))
            P["use"].append(0)
        key = ("dma" + e, i)
        if P["use"][i] > 0 and self.known[e].get(key, 0) < 16 * P["use"][i]:
            self.E[e].wait_ge(P["sem"][i], 16 * P["use"][i])
            self.known[e][key] = 16 * P["use"][i]
        self._deps(e, reads, writes)
        ins = self.E[e].dma_start(out=out, in_=in_)
        P["use"][i] += 1
        tok = ("d", key, P["sem"][i], 16 * P["use"][i])
        ins.then_inc(P["sem"][i], 16)
        self._commit(tok, reads, writes)
        if final:
            self.final.append(tok)
        return ins

    def finish(self):
        for tok in self.final:
            self.E["sp"].wait_ge(tok[2], tok[3])


def col_layout(L):
    idx = {}

    def add(*name):
        idx[name] = len(idx)

    for l in range(L):
        for c in range(KC):
            add("nw", l, c)
        for pr in range(2):
            for nm in ("q0", "q1", "f0", "f1", "z0", "z1", "nw0", "nw1", "v0", "v1"):
                add("hg", l, pr, nm)
            for nm in ("q", "k", "z0", "z1", "a16", "ba2", "nw0", "nw1", "v0", "v1"):
                add("gla", l, pr, nm)
            for nm in ("q", "k", "i", "f", "o0", "o1", "z0", "z1", "nw0", "nw1", "v0", "v1", "cbq", "cbk",
                       "cwq0", "cwq1", "cwq2", "cwq3", "cwk0", "cwk1", "cwk2", "cwk3"):
                add("ml", l, pr, nm)
    return idx


GROUPS = [("hg", 0), ("hg", 1), ("gla", 0), ("gla", 1), ("ml", 0), ("ml", 1)]
NCOLS = {"hg": 1024, "gla": 784, "ml": 1280}
VOFF = {"hg": 768, "gla": 528, "ml": 1024}


def host_prep(inp, L):
    w_in = np.asarray(inp["w_in"], np.float32)
    b_in = np.asarray(inp["b_in"], np.float32)
    w_out = np.asarray(inp["w_out"], np.float32)
    ar = np.arange
    out = {}

    def colsel(kind, pr):
        h0, h1 = 2 * pr, 2 * pr + 1
        if kind == "hg":
            parts = [OFF["hg_q"] + h0 * 128 + ar(128), OFF["hg_q"] + h1 * 128 + ar(128),
                     OFF["hg_f"] + h0 * 128 + ar(128), OFF["hg_f"] + h1 * 128 + ar(128),
                     OFF["hg_z"] + h0 * 128 + ar(128), OFF["hg_z"] + h1 * 128 + ar(128),
                     OFF["hg_i"] + h0 * 128 + ar(256)]
        elif kind == "gla":
            parts = [OFF["gla_q"] + h0 * 64 + ar(128), OFF["gla_k"] + h0 * 64 + ar(128),
                     OFF["gla_z"] + h0 * 128 + ar(128), OFF["gla_z"] + h1 * 128 + ar(128),
                     OFF["gla_a"] + ar(16), OFF["gla_v"] + h0 * 128 + ar(256)]
        else:
            rep = lambda c0: np.concatenate([np.full(64, c0 + h0), np.full(64, c0 + h1)])
            parts = [OFF["ml_q"] + h0 * 64 + ar(128), OFF["ml_k"] + h0 * 64 + ar(128),
                     rep(OFF["ml_i"]), rep(OFF["ml_f"]),
                     OFF["ml_o"] + h0 * 128 + ar(128), OFF["ml_o"] + h1 * 128 + ar(128),
                     OFF["ml_z"] + h0 * 128 + ar(128), OFF["ml_z"] + h1 * 128 + ar(128),
                     OFF["ml_v"] + h0 * 128 + ar(256)]
        return np.concatenate(parts)

    for kind in ("hg", "gla", "ml"):
        arr = np.empty((L, 2, 128, KC, NCOLS[kind]), np.float32)
        for l in range(L):
            for pr in range(2):
                sel = colsel(kind, pr)
                arr[l, pr] = w_in[l][:, sel].reshape(KC, 128, -1).transpose(1, 0, 2)
        out["w_" + kind] = arr
    wo = np.empty((L, 6, 128, 2, D_MODEL), np.float32)
    for l in range(L):
        for g in range(6):
            wo[l, g] = w_out[l][g * 256:(g + 1) * 256].reshape(2, 128, D_MODEL).transpose(1, 0, 2)
    out["w_o"] = wo

    idx = col_layout(L)
    cols = np.zeros((128, len(idx)), np.float32)
    for l in range(L):
        for c in range(KC):
            cols[:, idx["nw", l, c]] = inp["norm_w"][l][c * 128:(c + 1) * 128]
        for pr in range(2):
            h0, h1 = 2 * pr, 2 * pr + 1
            b = b_in[l]
            sl = lambda seg, h, w=128: b[OFF[seg] + h * w: OFF[seg] + (h + 1) * w]
            cols[:, idx["hg", l, pr, "q0"]] = sl("hg_q", h0)
            cols[:, idx["hg", l, pr, "q1"]] = sl("hg_q", h1)
            cols[:, idx["hg", l, pr, "f0"]] = sl("hg_f", h0)
            cols[:, idx["hg", l, pr, "f1"]] = sl("hg_f", h1)
            cols[:, idx["hg", l, pr, "z0"]] = sl("hg_z", h0)
            cols[:, idx["hg", l, pr, "z1"]] = sl("hg_z", h1)
            cols[:, idx["hg", l, pr, "nw0"]] = inp["hg_norm_w"][l][h0 * 128:(h0 + 1) * 128]
            cols[:, idx["hg", l, pr, "nw1"]] = inp["hg_norm_w"][l][h1 * 128:(h1 + 1) * 128]
            cols[:, idx["gla", l, pr, "q"]] = b[OFF["gla_q"] + h0 * 64: OFF["gla_q"] + h0 * 64 + 128]
            cols[:, idx["gla", l, pr, "k"]] = b[OFF["gla_k"] + h0 * 64: OFF["gla_k"] + h0 * 64 + 128]
            cols[:, idx["gla", l, pr, "z0"]] = sl("gla_z", h0)
            cols[:, idx["gla", l, pr, "z1"]] = sl("gla_z", h1)
            cols[0:16, idx["gla", l, pr, "a16"]] = b[OFF["gla_a"]:OFF["gla_a"] + 16]
            cols[:, idx["gla", l, pr, "ba2"]] = inp["gla_b_a2"][l][h0 * 64: h0 * 64 + 128]
            cols[:, idx["gla", l, pr, "nw0"]] = inp["gla_norm_w"][l][h0 * 128:(h0 + 1) * 128]
            cols[:, idx["gla", l, pr, "nw1"]] = inp["gla_norm_w"][l][h1 * 128:(h1 + 1) * 128]
            cols[:, idx["ml", l, pr, "q"]] = b[OFF["ml_q"] + h0 * 64: OFF["ml_q"] + h0 * 64 + 128]
            cols[:, idx["ml", l, pr, "k"]] = b[OFF["ml_k"] + h0 * 64: OFF["ml_k"] + h0 * 64 + 128]
            cols[0:64, idx["ml", l, pr, "i"]] = b[OFF["ml_i"] + h0]
            cols[64:128, idx["ml", l, pr, "i"]] = b[OFF["ml_i"] + h1]
            cols[0:64, idx["ml", l, pr, "f"]] = b[OFF["ml_f"] + h0]
            cols[64:128, idx["ml", l, pr, "f"]] = b[OFF["ml_f"] + h1]
            cols[:, idx["ml", l, pr, "o0"]] = sl("ml_o", h0)
            cols[:, idx["ml", l, pr, "o1"]] = sl("ml_o", h1)
            cols[:, idx["ml", l, pr, "z0"]] = sl("ml_z", h0)
            cols[:, idx["ml", l, pr, "z1"]] = sl("ml_z", h1)
            cols[:, idx["ml", l, pr, "nw0"]] = inp["ml_norm_w"][l][h0 * 128:(h0 + 1) * 128]
            cols[:, idx["ml", l, pr, "nw1"]] = inp["ml_norm_w"][l][h1 * 128:(h1 + 1) * 128]
            cw = inp["ml_conv_w"][l]
            cb = inp["ml_conv_b"][l]
            cols[:, idx["ml", l, pr, "cbq"]] = cb[h0 * 64: h0 * 64 + 128]
            cols[:, idx["ml", l, pr, "cbk"]] = cb[256 + h0 * 64: 256 + h0 * 64 + 128]
            for j in range(4):
                cols[:, idx["ml", l, pr, f"cwq{j}"]] = cw[j, h0 * 64: h0 * 64 + 128]
                cols[:, idx["ml", l, pr, f"cwk{j}"]] = cw[j, 256 + h0 * 64: 256 + h0 * 64 + 128]
            for kind, seg in (("hg", "hg_i"), ("gla", "gla_v"), ("ml", "ml_v")):
                cols[:, idx[kind, l, pr, "v0"]] = sl(seg, h0)
                cols[:, idx[kind, l, pr, "v1"]] = sl(seg, h1)
    out["cols"] = cols
    out["wa2"] = np.ascontiguousarray(np.asarray(inp["gla_w_a2"], np.float32)[:L].transpose(1, 0, 2))
    out["lbl"] = np.ascontiguousarray(
        np.asarray(inp["hg_lb_logits"], np.float32)[:L].reshape(L, 4, 128).transpose(2, 1, 0))
    out["fnw"] = np.ascontiguousarray(np.broadcast_to(np.asarray(inp["final_norm_w"], np.float32)[None, :], (128, D_MODEL)))
    return out


def build(T=SEQ, L=DEPTH, enable=("hg", "gla", "ml")):
    _, used = _build(T, L, enable, None)
    nc, _ = _build(T, L, enable, used if SPARSE else None)
    return nc


def _build(T, L, enable, needed):
    nc = bass.Bass("TRN2", target_bir_lowering=False)
    NT = T // TT
    NS = T // ST
    idx = col_layout(L)
    NCL = len(idx)

    def dram(name, shape, kind="ExternalInput"):
        return nc.dram_tensor(name, list(shape), F32, kind=kind).ap()

    x_d = dram("x", [T, D_MODEL])
    y_d = dram("y", [T, D_MODEL], kind="ExternalOutput")
    w_d = {k: dram("w_" + k, [L, 2, 128, KC, NCOLS[k]]) for k in ("hg", "gla", "ml")}
    wo_d = dram("w_o", [L, 6, 128, 2, D_MODEL])
    cols_d = dram("cols", [128, NCL])
    wa2_d = dram("wa2", [16, L, 256])
    lbl_d = dram("lbl", [128, 4, L])
    fnw_d = dram("fnw", [128, D_MODEL])

    with ExitStack() as stack:
        S = Sched(nc, stack, needed)
        op = S.op

        xT = S.sbuf("xT", [128, KC, T], F32)
        xTg = [[S.view(xT, f"xT{c}_{s}") for s in range(NS)] for c in range(KC)]
        hT = S.sbuf("hT", [128, KC, T], BF16)
        hTg = [S.view(hT, f"hT{s}") for s in range(NS)]
        full = set(enable) == {"hg", "gla", "ml"}
        wbuf = [S.sbuf(f"wbuf{i}", [128, KC, 1280 if (i == 0 or not full) else 1024], BF16) for i in range(2)]
        wobuf = [S.sbuf(f"wobuf{i}", [128, 2, D_MODEL], BF16) for i in range(2)]
        cols = S.sbuf("cols", [128, NCL], F32)
        ncols = S.sbuf("ncols", [128, NCL], F32)
        hcols = S.sbuf("hcols", [128, NCL], F32)
        wa2 = S.sbuf("wa2", [16, L, 256], BF16)
        lbl = S.sbuf("lbl", [128, 4, L], F32)
        lbA = S.sbuf("lbA", [128, 4, L], F32)
        lbB = S.sbuf("lbB", [128, 4, L], F32)
        lbNB = S.sbuf("lbNB", [128, 4, L], F32)
        identf = S.sbuf("identf", [128, 128], F32)
        identb = S.sbuf("identb", [128, 128], BF16)
        onesb = S.sbuf("onesb", [128, 128], BF16)
        mask = S.sbuf("mask", [128, 128], BF16)
        rst = S.sbuf("rst", [128, ST], BF16)
        cst = S.sbuf("cst", [128, 8], F32)
        C_LN8, C_E1024, C_E128, C_ONE, C_ZERO, C_LN32, C_LNS128 = 0, 1, 2, 3, 4, 5, 6

        pb = [S.psum(f"pb{i}", [128, 512], F32) for i in range(8)]
        att_slots = [pb[2]] * 4
        u_slots = [pb[3]] * 2
        ktp = pb[3]

        def att_ap(i):
            return pb[2].t[:, i * 128:(i + 1) * 128]

        def ktp_ap():
            return pb[3].t[:, 256:512].bitcast(BF16)

        S.dma("sp", cols.t[:], cols_d, writes=[cols])
        S.dma("sp", lbl.t[:], lbl_d, writes=[lbl])
        S.dma("pool", wa2.t[:], wa2_d, writes=[wa2])
        op("dve", lambda E: E.tensor_scalar(ncols.t[:], cols.t[:], -1.0, None, op0=ALU.mult), [cols], [ncols])
        op("dve", lambda E: E.tensor_scalar(hcols.t[:], cols.t[:], 0.5, None, op0=ALU.mult), [cols], [hcols])
        op("pool", lambda E: E.memset(identf.t[:], 0.0), [], [identf])
        op("pool", lambda E: E.affine_select(out=identf.t[:], in_=identf.t[:], pattern=[[-1, 128]],
                                             compare_op=ALU.not_equal, fill=1.0, base=0, channel_multiplier=1),
           [identf], [identf])
        op("pool", lambda E: E.tensor_copy(identb.t[:], identf.t[:]), [identf], [identb])
        op("pool", lambda E: E.memset(onesb.t[:], 1.0), [], [onesb])
        op("pool", lambda E: E.memset(mask.t[:], 1.0), [], [mask])
        op("pool", lambda E: E.affine_select(out=mask.t[:], in_=mask.t[:], pattern=[[1, 128]],
                                             compare_op=ALU.is_ge, fill=0.0, base=0, channel_multiplier=-1),
           [mask], [mask])
        op("pool", lambda E: E.memset(rst.t[:], 1.0), [], [rst])
        for j in range(ST // TT):
            op("pool", lambda E, j=j: E.memset(rst.t[:, j * TT:j * TT + 1], 0.0), [], [rst])
        for ci, val in ((C_LN8, math.log(0.125)), (C_E1024, 1024.0 * EPS), (C_E128, 128.0 * EPS),
                        (C_ONE, 1.0), (C_ZERO, 0.0), (C_LN32, math.log(32.0)), (C_LNS128, 0.5 * math.log(128.0))):
            op("pool", lambda E, ci=ci, val=val: E.memset(cst.t[:, ci:ci + 1], val), [], [cst])

        def cc(i):
            return cst.t[:, i:i + 1]

        for b in (2, 3):
            op("dve", lambda E, b=b: E.memset(pb[b].t[:], 0.0), [], [pb[b]])

        tmpl = S.sbuf("tmpl", [128, 4, L], F32)
        suml = S.sbuf("suml", [128, 4], F32)
        lb = S.sbuf("lb", [128, 4, L], F32)
        op("act", lambda E: E.activation(out=tmpl.t[:], in_=lbl.t[:], func=AF.Exp), [lbl], [tmpl])
        op("dve", lambda E: E.tensor_reduce(out=suml.t[:], in_=tmpl.t[:], axis=mybir.AxisListType.X, op=ALU.add), [tmpl], [suml])
        op("dve", lambda E: E.reciprocal(suml.t[:], suml.t[:]), [suml], [suml])
        op("dve", lambda E: E.tensor_tensor(out=tmpl.t[:], in0=tmpl.t[:], in1=suml.t[:].unsqueeze(2).to_broadcast([128, 4, L]),
                                            op=ALU.mult), [tmpl, suml], [tmpl])
        op("dve", lambda E: E.memset(lb.t[:], 0.0), [], [lb])
        for l in range(1, L):
            op("dve", lambda E, l=l: E.tensor_tensor(out=lb.t[:, :, l], in0=lb.t[:, :, l - 1], in1=tmpl.t[:, :, l], op=ALU.add),
               [lb, tmpl], [lb])
        op("dve", lambda E: E.tensor_scalar(lbA.t[:], lb.t[:], 0.5, 0.5, op0=ALU.mult, op1=ALU.add), [lb], [lbA])
        op("dve", lambda E: E.tensor_scalar(lbB.t[:], lb.t[:], -0.5, 0.5, op0=ALU.mult, op1=ALU.add), [lb], [lbB])
        op("dve", lambda E: E.tensor_scalar(lbNB.t[:], lb.t[:], 0.5, -0.5, op0=ALU.mult, op1=ALU.add), [lb], [lbNB])

        R32N = ["qs0", "qs1", "th0", "th1", "gz0_0", "gz1_0", "gz0_1", "gz1_1", "kk0", "kk1", "bcs0", "br0", "e1_0", "lnv0"]
        W32 = S.sbuf("W32", [128, len(R32N), ST], F32)
        r32 = {}
        for i, nm in enumerate(R32N):
            b_ = Buf(W32.t[:, i, :], nm)
            r32[nm] = b_
        R16N = ["sq0", "sq1", "y0", "y1"] + [f"{a}{u}_{p}" for p in range(2) for u in range(2) for a in ("qT", "kT", "ktok")]
        W16 = S.sbuf("W16", [128, len(R16N), ST], BF16)
        r16 = {}
        for i, nm in enumerate(R16N):
            r16[nm] = Buf(W16.t[:, i, :], nm)

        def wide32(i0):
            return W32.t[:, i0:i0 + 2, :].rearrange("p a b -> p (a b)")

        xin_views = [(wide32(0), [r32["qs0"], r32["qs1"]]), (wide32(2), [r32["th0"], r32["th1"]])]
        fnw_ap, fnw_b = wide32(4), [r32["gz0_0"], r32["gz1_0"]]

        class Rot:
            def __init__(self, name, n, shape, dt):
                self.b = [S.sbuf(f"{name}{i}", shape, dt) for i in range(n)]
                self.i = 0

            def get(self):
                r = self.b[self.i]
                self.i = (self.i + 1) % len(self.b)
                return r

        attp = Rot("attsb_", 4, [128, 128], BF16)
        scb = {f"{a}_{u}_{p}": S.sbuf(f"sc_{a}_{u}_{p}", [128, 4], F32) for a in ("c1", "c2", "c3") for u in range(2) for p in range(2)}
        vml = [S.sbuf(f"vsb{i}", [128, 4, 2, 128], BF16) for i in range(2)]
        Sst = [S.sbuf(f"Sst{i}", [128, 256 >> i], F32) for i in range(2)]
        Sbf = [S.sbuf(f"Sbf{i}", [128, 256 >> i], BF16) for i in range(2)]
        utmp = [S.sbuf(f"utmp{i}", [128, 256 >> i], F32) for i in range(2)]
        convin = [S.sbuf(f"convin{i}", [128, ST + 3], F32) for i in range(2)]
        vcnt = [0]
        gen_i = [0]

        step_kind = ["ml"]

        def gen_bank(kind):
            banks = [0, 1, 3] if step_kind[0] == "ml" else [0, 1, 3, 6, 7]
            b = banks[gen_i[0] % len(banks)]
            gen_i[0] += 1
            return pb[b]

        def act(out, in_, func, bias=None, scale=1.0, reads=(), writes=()):
            kw = {}
            if bias is not None:
                kw["bias"] = bias
            return op("act", lambda E: E.activation(out=out, in_=in_, func=func, scale=scale, **kw), reads, writes)

        for tt in range(NT):
            xin_ap, xin_b = xin_views[tt % 2]
            S.dma("sp", xin_ap, x_d[tt * TT:(tt + 1) * TT, :], writes=xin_b)
            st_ = tt // 4
            for half in range(2):
                ps = pb[half]
                for cq in range(4):
                    c = half * 4 + cq
                    op("pe", lambda E, c=c, cq=cq, ps=ps, xin_ap=xin_ap: E.transpose(ps.t[:, cq * 128:(cq + 1) * 128],
                                                                                   xin_ap[:, c * 128:(c + 1) * 128], identf.t[:]),
                       xin_b + [identf], [ps])
                tg = [xTg[half * 4 + cq][st_] for cq in range(4)]
                if half == 0:
                    op("dve", lambda E, ps=ps, half=half, tt=tt: E.tensor_copy(
                        xT.t[:, half * 4:(half + 1) * 4, tt * TT:(tt + 1) * TT],
                        ps.t[:].rearrange("p (a b) -> p a b", a=4)), [ps], tg)
                else:
                    act(xT.t[:, half * 4:(half + 1) * 4, tt * TT:(tt + 1) * TT],
                        ps.t[:].rearrange("p (a b) -> p a b", a=4), AF.Copy, reads=[ps], writes=tg)

        ORDER = [4, 0, 5, 1, 2, 3]
        glist = [(l, gi) for l in range(L) for gi in ORDER if GROUPS[gi][0] in enable]

        def load_weights(n):
            if n >= len(glist):
                return
            l, gi = glist[n]
            kind, pr = GROUPS[gi]
            wb = wbuf[n % 2]
            wo = wobuf[n % 2]
            ncl = NCOLS[kind]
            for c in range(KC):
                S.dma("pool", wb.t[:, c, 0:ncl], w_d[kind][l, pr, :, c, :], writes=[wb])
            S.dma("pool", wo.t[:], wo_d[l, gi], writes=[wo])

        load_weights(0)
        load_weights(1)

        def norm_phase(l):
            for s in range(NS):
                tok = slice(s * ST, (s + 1) * ST)
                ps = gen_bank("ml")
                for c in range(KC):
                    sq = r16[f"sq{c % 2}"]
                    act(sq.t[:], xT.t[:, c, tok], AF.Square, reads=[xTg[c][s]], writes=[sq])
                    op("pe", lambda E, ps=ps, sq=sq, c=c: E.matmul(ps.t[:], lhsT=onesb.t[:], rhs=sq.t[:], start=(c == 0), stop=(c == KC - 1)),
                       [onesb, sq], [ps])
                rstd = r32["lnv0"]
                act(rstd.t[:], ps.t[:], AF.Ln, bias=cc(C_E1024), reads=[ps, cst], writes=[rstd])
                act(rstd.t[:], rstd.t[:], AF.Exp, scale=-0.5, bias=cc(C_LN32), reads=[rstd, cst], writes=[rstd])
                for c in range(KC):
                    ci = idx["nw", l, c]
                    op("dve",
                       lambda E, c=c, ci=ci, rstd=rstd, tok=tok: E.scalar_tensor_tensor(
                           out=hT.t[:, c, tok], in0=xT.t[:, c, tok], scalar=cols.t[:, ci:ci + 1], in1=rstd.t[:],
                           op0=ALU.mult, op1=ALU.mult),
                       [xTg[c][s], cols, rstd], [hTg[s]])

        class Item:
            pass

        items = []
        for n, (l, gi) in enumerate(glist):
            for s in range(NS):
                it = Item()
                it.l, it.gi, it.s, it.n = l, gi, s, n
                it.kind, it.pr = GROUPS[gi]
                it.p = len(items) % 2
                it.ML = it.kind == "ml"
                it.dvp = 256 if it.ML else 128
                it.wb, it.wo = wbuf[n % 2], wobuf[n % 2]
                it.tok = slice(s * ST, (s + 1) * ST)
                it.vt = vml[len(items) % 2]
                it.gz = [r32[f"gz0_{it.p}"], r32[f"gz1_{it.p}"]]
                it.tho = [r32["th0"], r32["th1"]]
                it.heads = [(0, 0, 128), (1, 0, 128)] if it.kind == "hg" else [(0, 0, 64), (0, 64, 64)]
                it.nunit = 2 if it.kind == "hg" else 1
                it.last_of_group = (s == NS - 1)
                it.last_of_layer = it.last_of_group and (n + 1 == len(glist) or glist[n + 1][0] != l)
                it.first_of_layer = (s == 0) and (n == 0 or glist[n - 1][0] != l)
                items.append(it)

        def Ccol(it, nm, tab=None):
            tab = cols if tab is None else tab
            i = idx[it.kind, it.l, it.pr, nm]
            return tab.t[:, i:i + 1]

        def proj_fm(it, j, M=128):
            ps = gen_bank(it.kind)
            for c in range(KC):
                op("pe", lambda E, c=c, ps=ps: E.matmul(ps.t[0:M, :], lhsT=it.wb.t[:, c, j:j + M], rhs=hT.t[:, c, it.tok],
                                                        start=(c == 0), stop=(c == KC - 1)),
                   [it.wb, hTg[it.s]], [ps])
            return ps

        def genA1(it):
            kind, l, pr, s = it.kind, it.l, it.pr, it.s
            C = lambda nm, tab=None: Ccol(it, nm, tab)
            if kind == "hg":
                qs = [r32["qs0"], r32["qs1"]]
                th = [r32["th0"], r32["th1"]]
                kkb = [r32["kk0"], r32["kk1"]]
                for hh in range(2):
                    ps = proj_fm(it, hh * 128)
                    act(qs[hh].t[:], ps.t[:], AF.Silu, bias=C(f"q{hh}"), reads=[ps, cols], writes=[qs[hh]])
                    yield
                for hh in range(2):
                    ps = proj_fm(it, 256 + hh * 128)
                    act(th[hh].t[:], ps.t[:], AF.Tanh, bias=C(f"f{hh}", hcols), scale=0.5, reads=[ps, hcols], writes=[th[hh]])
                    yield
                prep = []
                for hh in range(2):
                    h = 2 * pr + hh
                    op("dve", lambda E, hh=hh, h=h: E.tensor_scalar(kkb[hh].t[:], th[hh].t[:], lbNB.t[:, h, l:l + 1], lbB.t[:, h, l:l + 1],
                                                                    op0=ALU.mult, op1=ALU.add), [th[hh], lbNB, lbB], [kkb[hh]])
                    op("dve", lambda E, hh=hh, h=h: E.tensor_scalar(th[hh].t[:], th[hh].t[:], lbB.t[:, h, l:l + 1], lbA.t[:, h, l:l + 1],
                                                                    op0=ALU.mult, op1=ALU.add), [th[hh], lbB, lbA], [th[hh]])
                for hh in range(2):
                    act(th[hh].t[:], th[hh].t[:], AF.Ln, reads=[th[hh]], writes=[th[hh]])
                    prep.append((qs[hh], kkb[hh], th[hh], 1.0, None))
            elif kind == "gla":
                ql, kl, sp_ = r32["qs0"], r32["qs1"], r32["th0"]
                ps = proj_fm(it, 0)
                act(ql.t[:], ps.t[:], AF.Identity, bias=C("q"), reads=[ps, cols], writes=[ql])
                yield
                ps = proj_fm(it, 128)
                act(kl.t[:], ps.t[:], AF.Identity, bias=C("k"), reads=[ps, cols], writes=[kl])
                yield
                ps = proj_fm(it, 512, M=16)
                i16 = idx[kind, l, pr, "a16"]
                a16b = r16["sq1"]
                act(a16b.t[0:16, :], ps.t[0:16, :], AF.Identity, bias=cols.t[0:16, i16:i16 + 1], reads=[ps, cols], writes=[a16b])
                ps = gen_bank(kind)
                op("pe", lambda E, ps=ps: E.matmul(ps.t[:], lhsT=wa2.t[0:16, l, pr * 128:(pr + 1) * 128], rhs=a16b.t[0:16, :],
                                                   start=True, stop=True), [wa2, a16b], [ps])
                act(sp_.t[:], ps.t[:], AF.Exp, bias=C("ba2", ncols), scale=-1.0, reads=[ps, ncols], writes=[sp_])
                act(sp_.t[:], sp_.t[:], AF.Ln, bias=cc(C_ONE), reads=[sp_, cst], writes=[sp_])
                yield
                prep = [(ql, kl, sp_, -1.0 / 16.0, None)]
            else:
                cv = [r32["qs0"], r32["qs1"]]
                if s == 0:
                    for i in range(2):
                        op("pool", lambda E, i=i: E.memset(convin[i].t[:, 0:3], 0.0), [], [convin[i]])
                for qi, nm in enumerate(("q", "k")):
                    ps = proj_fm(it, qi * 128)
                    cin = convin[qi]
                    acc = cv[qi]
                    act(cin.t[:, 3:ST + 3], ps.t[:], AF.Identity, bias=C(nm), reads=[ps, cols], writes=[cin])
                    op("pool", lambda E, acc=acc, cin=cin, nm=nm: E.tensor_scalar(
                        acc.t[:], cin.t[:, 0:ST], C(f"cw{nm}0"), C(f"cb{nm}"), op0=ALU.mult, op1=ALU.add), [cin, cols], [acc])
                    for j in range(1, 4):
                        op("dve", lambda E, acc=acc, cin=cin, nm=nm, j=j: E.scalar_tensor_tensor(
                            out=acc.t[:], in0=cin.t[:, j:j + ST], scalar=C(f"cw{nm}{j}"), in1=acc.t[:],
                            op0=ALU.mult, op1=ALU.add), [cin, cols, acc], [acc])
                    op("pool", lambda E, cin=cin: E.tensor_copy(cin.t[:, 0:3], cin.t[:, ST:ST + 3]), [cin], [cin])
                    yield
                for qi in range(2):
                    act(cv[qi].t[:], cv[qi].t[:], AF.Silu, reads=[cv[qi]], writes=[cv[qi]])
                irow, sp_ = r32["kk0"], r32["kk1"]
                ps = proj_fm(it, 256)
                act(irow.t[:], ps.t[:], AF.Identity, bias=C("i"), reads=[ps, cols], writes=[irow])
                yield
                ps = proj_fm(it, 384)
                act(sp_.t[:], ps.t[:], AF.Exp, bias=C("f", ncols), scale=-1.0, reads=[ps, ncols], writes=[sp_])
                act(sp_.t[:], sp_.t[:], AF.Ln, bias=cc(C_ONE), reads=[sp_, cst], writes=[sp_])
                yield
                prep = [(cv[0], cv[1], sp_, -1.0, irow)]

            it.unit = []
            for u, (qv, kv, gsrc, gscale, irow) in enumerate(prep):
                bcs, br, e1 = r32["bcs0"], r32["br0"], r32["e1_0"]
                op("dve", lambda E, bcs=bcs, gsrc=gsrc: E.tensor_tensor_scan(bcs.t[:], rst.t[:], gsrc.t[:], 0.0, op0=ALU.mult, op1=ALU.add),
                   [rst, gsrc], [bcs])
                b3 = bcs.t[:].rearrange("p (a b) -> p a b", a=4)
                br3 = br.t[:].rearrange("p (a b) -> p a b", a=4)
                op("dve", lambda E, br3=br3, b3=b3: E.tensor_tensor(out=br3, in0=b3, in1=b3[:, :, 63:64].to_broadcast([128, 4, 128]),
                                                                   op=ALU.subtract), [bcs], [br])
                c1, c2, c3 = (scb[f"{a}_{u}_{it.p}"] for a in ("c1", "c2", "c3"))
                act(c1.t[:], b3[:, :, 127], AF.Exp, scale=gscale, reads=[bcs], writes=[c1])
                act(c2.t[:], br3[:, :, 127], AF.Exp, scale=gscale, reads=[br], writes=[c2])
                act(c3.t[:], b3[:, :, 63], AF.Exp, scale=gscale, reads=[bcs], writes=[c3])
                if kind == "hg":
                    act(e1.t[:], br.t[:], AF.Exp, scale=gscale, reads=[br], writes=[e1])
                else:
                    act(e1.t[:], br.t[:], AF.Exp, scale=gscale, bias=cc(C_LN8), reads=[br, cst], writes=[e1])
                if irow is None:
                    act(br.t[:], br.t[:], AF.Exp, scale=-gscale, reads=[br], writes=[br])
                else:
                    op("dve", lambda E, br=br, irow=irow: E.tensor_tensor(out=br.t[:], in0=br.t[:], in1=irow.t[:], op=ALU.add),
                       [br, irow], [br])
                    act(br.t[:], br.t[:], AF.Exp, reads=[br], writes=[br])
                e2 = br
                qT, kT, ktok = (r16[f"{a}{u}_{it.p}"] for a in ("qT", "kT", "ktok"))
                op("dve", lambda E, qT=qT, qv=qv, e1=e1: E.tensor_tensor(out=qT.t[:], in0=qv.t[:], in1=e1.t[:], op=ALU.mult), [qv, e1], [qT])
                op("dve", lambda E, kT=kT, kv=kv, e2=e2: E.tensor_tensor(out=kT.t[:], in0=kv.t[:], in1=e2.t[:], op=ALU.mult), [kv, e2], [kT])
                it.unit.append((qT, kT, ktok, c1, c2, c3))
                yield

            vt = it.vt
            for hh in range(2):
                ps = proj_fm(it, VOFF[kind] + hh * 128)
                vst = r16[f"sq{hh}"]
                act(vst.t[:], ps.t[:], AF.Identity, bias=C(f"v{hh}"), reads=[ps, cols], writes=[vst])
                yield
            ps = gen_bank(kind)
            vp = ps.t[:].bitcast(BF16)
            for hh in range(2):
                vst = r16[f"sq{hh}"]
                for j in range(4):
                    op("pe", lambda E, hh=hh, j=j, vst=vst: E.transpose(vp[:, (hh * 4 + j) * 128:(hh * 4 + j + 1) * 128],
                                                                        vst.t[:, j * 128:(j + 1) * 128], identb.t[:]),
                       [vst, identb], [ps])
            act(vt.t[:], vp.rearrange("p (h j d) -> p j h d", h=2, j=4), AF.Copy, reads=[ps], writes=[vt])
            yield
            for u in range(len(it.unit)):
                qT, kT, ktok = it.unit[u][0], it.unit[u][1], it.unit[u][2]
                ps = gen_bank(kind)
                kp = ps.t[:, 0:256].bitcast(BF16)
                for j in range(4):
                    op("pe", lambda E, j=j, kT=kT, kp=kp: E.transpose(kp[:, j * 128:(j + 1) * 128], kT.t[:, j * 128:(j + 1) * 128], identb.t[:]),
                       [kT, identb], [ps])
                act(ktok.t[:], kp, AF.Copy, reads=[ps], writes=[ktok])
                yield

        def genA2(it):
            kind = it.kind
            C = lambda nm, tab=None: Ccol(it, nm, tab)
            zoff = {"hg": 512, "gla": 256, "ml": 768}[kind]
            for hh in range(2):
                ps = proj_fm(it, zoff + hh * 128)
                act(it.gz[hh].t[:], ps.t[:], AF.Silu, bias=C(f"z{hh}"), reads=[ps, cols], writes=[it.gz[hh]])
                yield

        Ob = [pb[4], pb[5]]
        Db = [pb[6], pb[7]]

        def att_bank(it, k):
            return pb[2], pb[2].t[:, (k % 2) * 128:(k % 2 + 1) * 128]

        def genCore(it):
            kind, ML, dvp, vt = it.kind, it.ML, it.dvp, it.vt
            unit, heads = it.unit, it.heads
            if it.s == 0:
                for u in range(it.nunit):
                    op("pool", lambda E, u=u: E.memset(Sst[u].t[:], 0.0), [], [Sst[u]])
            for j in range(4):
                for u in range(len(unit)):
                    c3 = unit[u][5]
                    op("dve", lambda E, u=u, c3=c3, j=j: E.tensor_scalar(Sbf[u].t[:, 0:dvp], Sst[u].t[:, 0:dvp], c3.t[:, j:j + 1], None, op0=ALU.mult),
                       [Sst[u], c3], [Sbf[u]])
                atts = []
                abs_ = [att_bank(it, j * 2 + hi) for hi in range(2)]
                for hi, (u, p0, dk) in enumerate(heads):
                    qT, kT = unit[u][0], unit[u][1]
                    ab, a_ap = abs_[hi]
                    op("pe", lambda E, a_ap=a_ap, qT=qT, kT=kT, p0=p0, dk=dk, j=j: E.matmul(
                        a_ap[:, 64:128], lhsT=kT.t[p0:p0 + dk, j * TT:(j + 1) * TT], rhs=qT.t[p0:p0 + dk, j * TT + 64:(j + 1) * TT],
                        start=True, stop=True), [qT, kT], [ab])
                    op("pe", lambda E, a_ap=a_ap, qT=qT, kT=kT, p0=p0, dk=dk, j=j: E.matmul(
                        a_ap[0:64, 0:64], lhsT=kT.t[p0:p0 + dk, j * TT:j * TT + 64], rhs=qT.t[p0:p0 + dk, j * TT:j * TT + 64],
                        start=True, stop=True), [qT, kT], [ab])
                for hi, (u, p0, dk) in enumerate(heads):
                    ab, a_ap = abs_[hi]
                    asb = attp.get()
                    op("dve", lambda E, asb=asb, a_ap=a_ap: E.tensor_tensor(out=asb.t[:], in0=a_ap, in1=mask.t[:], op=ALU.mult),
                       [ab, mask], [asb])
                    atts.append(asb)
                yield
                for hi, (u, p0, dk) in enumerate(heads):
                    qT, kT, ktok = unit[u][0], unit[u][1], unit[u][2]
                    asb = atts[hi]
                    outs = [(Ob[hi], 0)] + ([(Db[hi], 128)] if ML else [])
                    for (ob, vo) in outs:
                        op("pe", lambda E, ob=ob, vo=vo, asb=asb, hi=hi, j=j: E.matmul(
                            ob.t[:, j * TT:(j + 1) * TT], lhsT=(vt.t[:, j, hi, :] if vo == 0 else onesb.t[:]), rhs=asb.t[:], start=True, stop=False),
                           [vt, asb, onesb], [ob])
                        op("pe", lambda E, ob=ob, vo=vo, u=u, p0=p0, dk=dk, qT=qT, j=j: E.matmul(
                            ob.t[:, j * TT:(j + 1) * TT], lhsT=Sbf[u].t[p0:p0 + dk, vo:vo + 128], rhs=qT.t[p0:p0 + dk, j * TT:(j + 1) * TT],
                            start=False, stop=True), [Sbf[u], qT], [ob])
                    ucol = 0 if ML else u * 128
                    op("pe", lambda E, ktok=ktok, p0=p0, dk=dk, hi=hi, j=j, ucol=ucol: E.matmul(
                        pb[2].t[p0:p0 + dk, 256 + ucol:256 + ucol + 128], lhsT=ktok.t[:, j * 128 + p0:j * 128 + p0 + dk], rhs=vt.t[:, j, hi, :],
                        start=True, stop=True), [ktok, vt], [pb[2]])
                    if ML:
                        op("pe", lambda E, ktok=ktok, p0=p0, dk=dk, hi=hi, j=j: E.matmul(
                            pb[2].t[p0:p0 + dk, 384:512], lhsT=ktok.t[:, j * 128 + p0:j * 128 + p0 + dk], rhs=onesb.t[:],
                            start=True, stop=True), [ktok, onesb], [pb[2]])
                for u in range(len(unit)):
                    c1, c2 = unit[u][3], unit[u][4]
                    ucol = 0 if ML else u * 128
                    op("dve", lambda E, u=u, c2=c2, j=j, ucol=ucol: E.tensor_scalar(
                        utmp[u].t[:, 0:dvp], pb[2].t[:, 256 + ucol:256 + ucol + dvp], c2.t[:, j:j + 1], None, op0=ALU.mult),
                       [pb[2], c2], [utmp[u]])
                    op("dve", lambda E, u=u, c1=c1, j=j: E.scalar_tensor_tensor(
                        out=Sst[u].t[:, 0:dvp], in0=Sst[u].t[:, 0:dvp], scalar=c1.t[:, j:j + 1], in1=utmp[u].t[:, 0:dvp],
                        op0=ALU.mult, op1=ALU.add), [Sst[u], c1, utmp[u]], [Sst[u]])
                yield

        def postHead(it, hi, dsqs):
            kind, ML = it.kind, it.ML
            nwc = Ccol(it, f"nw{hi}")
            lnv = r32["lnv0"] if hi == 0 else r32["e1_0"]
            if ML:
                u2 = r32["qs0"] if hi == 0 else r32["kk0"]
                op("dve", lambda E: E.scalar_tensor_tensor(
                    out=u2.t[:], in0=it.tho[hi].t[:], scalar=1.0, in1=Ob[hi].t[:], op0=ALU.add, op1=ALU.mult),
                   [it.tho[hi], Ob[hi]], [u2])
                yield
                src, srcb = u2.t[:], u2
            else:
                src, srcb = Ob[hi].t[:], Ob[hi]
            osq = r16[f"sq{hi}"]
            act(osq.t[:], src, AF.Square, reads=[srcb], writes=[osq])
            yield
            ps = gen_bank(kind)
            op("pe", lambda E: E.matmul(ps.t[:], lhsT=onesb.t[:], rhs=osq.t[:], start=True, stop=True), [onesb, osq], [ps])
            if ML:
                dsq = dsqs[hi]
                op("dve", lambda E: E.tensor_tensor(out=dsq.t[:], in0=ps.t[:], in1=dsq.t[:], op=ALU.add), [ps, dsq], [dsq])
                yield
                act(lnv.t[:], dsq.t[:], AF.Ln, reads=[dsq], writes=[lnv])
            else:
                act(lnv.t[:], ps.t[:], AF.Ln, bias=cc(C_E128), reads=[ps, cst], writes=[lnv])
            yield
            act(lnv.t[:], lnv.t[:], AF.Exp, scale=-0.5, bias=cc(C_LNS128), reads=[lnv, cst], writes=[lnv])
            yield
            t2 = it.gz[hi]
            op("dve", lambda E: E.scalar_tensor_tensor(
                out=t2.t[:], in0=t2.t[:], scalar=nwc, in1=lnv.t[:], op0=ALU.mult, op1=ALU.mult), [t2, cols, lnv], [t2])
            yield
            y_ = r16[f"y{hi}"]
            op("dve", lambda E: E.tensor_tensor(out=y_.t[:], in0=src, in1=t2.t[:], op=ALU.mult), [srcb, t2], [y_])
            it.ys[hi] = y_
            yield

        def genPost(it):
            ML = it.ML
            it.ys = [None, None]
            dsqs = [r32["qs1"], r32["kk1"]]
            if ML:
                for hi in range(2):
                    dsq = dsqs[hi]
                    act(dsq.t[:], Db[hi].t[:], AF.Square, reads=[Db[hi]], writes=[dsq])
                for hi in range(2):
                    dsq = dsqs[hi]
                    op("dve", lambda E, dsq=dsq: E.tensor_scalar(dsq.t[:], dsq.t[:], 1.0, 512.0 * EPS, op0=ALU.max, op1=ALU.mult), [dsq], [dsq])
                for hi in range(2):
                    pso = proj_fm(it, 512 + hi * 128)
                    act(it.tho[hi].t[:], pso.t[:], AF.Tanh, bias=Ccol(it, f"o{hi}", hcols), scale=0.5, reads=[pso, hcols], writes=[it.tho[hi]])
                    yield
            alive = [postHead(it, 0, dsqs), postHead(it, 1, dsqs)]
            k = 0
            while alive:
                for g in list(alive):
                    try:
                        next(g)
                    except StopIteration:
                        alive.remove(g)
                k += 1
                if k % 3 == 0:
                    yield

        def genOut(it):
            for c in range(KC):
                ps = gen_bank(it.kind)
                for hi in range(2):
                    op("pe", lambda E, ps=ps, hi=hi, c=c: E.matmul(ps.t[:], lhsT=it.wo.t[:, hi, c * 128:(c + 1) * 128], rhs=it.ys[hi].t[:],
                                                                  start=(hi == 0), stop=(hi == 1)), [it.wo, it.ys[hi]], [ps])
                op("dve", lambda E, ps=ps, c=c: E.tensor_tensor(out=xT.t[:, c, it.tok], in0=xT.t[:, c, it.tok], in1=ps.t[:], op=ALU.add),
                   [ps, xTg[c][it.s]], [xTg[c][it.s]])
                yield

        def run(*gens):
            gens = [g for g in gens if g is not None]
            while gens:
                for g in list(gens):
                    try:
                        next(g)
                    except StopIteration:
                        gens.remove(g)

        DEFER_OUT = True
        pending = None
        for i, it in enumerate(items):
            nxt = items[i + 1] if i + 1 < len(items) else None
            step_kind[0] = "ml" if it.first_of_layer else it.kind
            if it.first_of_layer:
                norm_phase(it.l)
                run(genA1(it))
                run(genA2(it))
            pipe_next = nxt is not None and not nxt.first_of_layer
            step_kind[0] = it.kind
            if DEFER_OUT:
                run(genCore(it), genA1(nxt) if pipe_next else None, genOut(pending) if pending is not None else None)
                if pending is not None and pending.last_of_group:
                    load_weights(pending.n + 2)
                pending = None
                run(genPost(it), genA2(nxt) if pipe_next else None)
                if it.last_of_layer:
                    run(genOut(it))
                    if it.last_of_group:
                        load_weights(it.n + 2)
                else:
                    pending = it
            else:
                run(genCore(it), genA1(nxt) if pipe_next else None)
                run(genPost(it))
                run(genOut(it), genA2(nxt) if pipe_next else None)
                if it.last_of_group:
                    load_weights(it.n + 2)

        S.dma("sp", fnw_ap, fnw_d, writes=fnw_b)
        ssq = S.sbuf("ssq", [128, 1], F32)
        junk, junk2 = r32["kk0"], r32["kk1"]
        for tt in range(NT):
            s = tt // 4
            xo_ap, xo_b = xin_views[tt % 2]
            for half in range(2):
                ps = pb[half]
                for cq in range(4):
                    c = half * 4 + cq
                    op("pe", lambda E, c=c, cq=cq, ps=ps, tt=tt: E.transpose(ps.t[:, cq * 128:(cq + 1) * 128],
                                                                             xT.t[:, c, tt * TT:(tt + 1) * TT], identf.t[:]),
                       [xTg[c][s], identf], [ps])
                if half == 0:
                    op("dve", lambda E, ps=ps, xo_ap=xo_ap: E.tensor_copy(xo_ap[:, 0:512], ps.t[:]), [ps], xo_b)
                else:
                    act(xo_ap[:, 512:1024], ps.t[:], AF.Copy, reads=[ps], writes=xo_b)
            act(junk.t[:], xo_ap[:, 0:512], AF.Square, reads=xo_b, writes=[junk])
            act(junk2.t[:], xo_ap[:, 512:1024], AF.Square, reads=xo_b, writes=[junk2])
            op("dve", lambda E: E.tensor_tensor(out=junk.t[:], in0=junk.t[:], in1=junk2.t[:], op=ALU.add), [junk, junk2], [junk])
            op("dve", lambda E: E.tensor_reduce(out=ssq.t[:], in_=junk.t[:], axis=mybir.AxisListType.X, op=ALU.add), [junk], [ssq])
            act(ssq.t[:], ssq.t[:], AF.Ln, bias=cc(C_E1024), reads=[ssq, cst], writes=[ssq])
            act(ssq.t[:], ssq.t[:], AF.Exp, scale=-0.5, reads=[ssq], writes=[ssq])
            op("dve", lambda E, xo_ap=xo_ap: E.scalar_tensor_tensor(out=xo_ap, in0=xo_ap, scalar=ssq.t[:, 0:1], in1=fnw_ap, op0=ALU.mult, op1=ALU.mult),
               xo_b + [ssq] + fnw_b, xo_b)
            op("dve", lambda E, xo_ap=xo_ap: E.tensor_scalar(xo_ap, xo_ap, 32.0, None, op0=ALU.mult), xo_b, xo_b)
            S.dma("sp", y_d[tt * TT:(tt + 1) * TT, :], xo_ap, reads=xo_b, final=True)
        S.finish()
        used = S.used
    return nc, used


_CACHE = {}


def kernel(**inputs):
    x = np.asarray(inputs["x"], np.float32)
    B, T, _ = x.shape
    L = int(np.asarray(inputs["w_in"]).shape[0])
    shared = host_prep(inputs, L)
    key = (T, L)
    if key not in _CACHE:
        _CACHE[key] = build(T, L)
    nc = _CACHE[key]
    in_maps = []
    for b in range(B):
        m = dict(shared)
        m["x"] = np.ascontiguousarray(x[b])
        in_maps.append(m)
    res = run_bass_kernel_spmd(nc, in_maps, core_ids=list(range(B)))
    return np.stack([np.asarray(r["y"], np.float32) for r in res.results], axis=0)
```
